# Optimizing a Trainium2 kernel written in Bass

```python
import math
import jax, jax.numpy as jnp
from jax import lax
import numpy as np

D_MODEL = 1024
BATCH = 4
SEQ = 4096
DEPTH = 2
DEC_BATCH = 2
DEC_SEQ = 16384
PAST_LEN = 128

HEAD_DIM = 64
MIX_WIDTH = D_MODEL
NA_HEADS = (MIX_WIDTH // 2) // HEAD_DIM
NA_WIDTH = NA_HEADS * HEAD_DIM
GRID_W = 64
NA_WIN_ROWS = 8
NA_WIN_COLS = 16
POOL_WIDTH = MIX_WIDTH - NA_WIDTH
POOL_WINDOWS = (2, 4, 8, 16)
POOL_GROUPS = len(POOL_WINDOWS)
POOL_GROUP_DIM = POOL_WIDTH // POOL_GROUPS
EVEN_IN = 3 * NA_WIDTH + POOL_WIDTH
SWA_HEADS = 8
SWA_KV_HEADS = 2
SWA_WIDTH = SWA_HEADS * HEAD_DIM
SWA_KV_WIDTH = SWA_KV_HEADS * HEAD_DIM
SWA_WINDOW = 128
SWA_BLOCK = 128
CONV_WIDTH = MIX_WIDTH - SWA_WIDTH
CONV_K = 3
ODD_IN = SWA_WIDTH + 2 * SWA_KV_WIDTH + 3 * CONV_WIDTH
D_FF = -(-(8 * D_MODEL) // (3 * 256)) * 256
ROPE_THETA = 10000.0
EPS = 1e-6
N_EVEN = (DEPTH + 1) // 2
N_ODD = DEPTH // 2

kernel_name = "hybrid_natten_pool_swa_conv_encoder"


def _rmsnorm(x, g):
    xf = x.astype(jnp.float32)
    y = xf * lax.rsqrt(jnp.mean(xf * xf, axis=-1, keepdims=True) + EPS)
    return (y * g.astype(jnp.float32)).astype(x.dtype)


def _modulate(h, shift, scale):
    return h * (1 + scale[:, None, :]) + shift[:, None, :]


def _rope(x):
    L = x.shape[1]
    inv = 1.0 / (ROPE_THETA ** (jnp.arange(0, HEAD_DIM, 2, dtype=jnp.float32) / HEAD_DIM))
    ang = jnp.arange(L, dtype=jnp.float32)[:, None] * inv[None, :]
    cos = jnp.cos(ang)[None, :, None, :]
    sin = jnp.sin(ang)[None, :, None, :]
    x1, x2 = jnp.split(x.astype(jnp.float32), 2, axis=-1)
    out = jnp.concatenate([x1 * cos - x2 * sin, x1 * sin + x2 * cos], axis=-1)
    return out.astype(x.dtype)


def _neighbourhood_attention(q, k, v, rpb):
    B, L, H, Dh = q.shape
    rows = L // GRID_W
    wr = min(NA_WIN_ROWS, rows)
    r = jnp.arange(rows)
    row_start = jnp.clip(r - wr // 2, 0, rows - wr)
    key_rows = row_start[:, None] + jnp.arange(wr)[None, :]
    cq = jnp.arange(GRID_W)
    col_start = jnp.clip(cq - NA_WIN_COLS // 2, 0, GRID_W - NA_WIN_COLS)
    kc = jnp.arange(GRID_W)
    col_ok = (kc[None, :] >= col_start[:, None]) & (kc[None, :] < col_start[:, None] + NA_WIN_COLS)
    qg = q.reshape(B, rows, GRID_W, H, Dh)
    kg = k.reshape(B, rows, GRID_W, H, Dh)[:, key_rows]
    vg = v.reshape(B, rows, GRID_W, H, Dh)[:, key_rows]
    s = jnp.einsum('brqhd,brikhd->bhrqik', qg, kg).astype(jnp.float32) * (Dh ** -0.5)
    dr = key_rows - r[:, None] + (NA_WIN_ROWS - 1)
    dc = jnp.clip(kc[None, :] - cq[:, None], -(NA_WIN_COLS - 1), NA_WIN_COLS - 1) + (NA_WIN_COLS - 1)
    bias = rpb.astype(jnp.float32)[:, dr][..., dc]
    bias = bias.transpose(0, 1, 3, 2, 4)
    s = jnp.where(col_ok[None, None, None, :, None, :], s + bias[None], -jnp.inf)
    p = jax.nn.softmax(s.reshape(B, H, rows, GRID_W, wr * GRID_W), axis=-1)
    p = p.reshape(B, H, rows, GRID_W, wr, GRID_W).astype(v.dtype)
    out = jnp.einsum('bhrqik,brikhd->brqhd', p, vg)
    return out.reshape(B, L, H * Dh)


def _multiscale_pool(u, w_grp, scale):
    B, L, _ = u.shape
    ug = u.reshape(B, L, POOL_GROUPS, POOL_GROUP_DIM)
    cs = jnp.cumsum(ug.astype(jnp.float32), axis=1)
    cs = jnp.pad(cs, ((0, 0), (1, 0), (0, 0), (0, 0)))
    t = jnp.arange(L)
    pooled = []
    for g, w in enumerate(POOL_WINDOWS):
        lo = jnp.clip(t - w // 2, 0, L)
        hi = jnp.clip(t - w // 2 + w, 0, L)
        csg = cs[:, :, g]
        cnt = (hi - lo).astype(jnp.float32)[None, :, None]
        pooled.append((csg[:, hi] - csg[:, lo]) / cnt)
    mixed = jnp.stack(pooled, axis=2) - ug.astype(jnp.float32)
    y = jnp.einsum('blgc,gce->blge', mixed.astype(u.dtype), w_grp)
    return y.reshape(B, L, POOL_WIDTH) * scale


def _window_gqa_sink(q, k, v, sink):
    B, L, Hq, Dh = q.shape
    Hkv = k.shape[2]
    G = Hq // Hkv
    nb = L // SWA_BLOCK
    qb = q.reshape(B, nb, SWA_BLOCK, Hkv, G, Dh)
    pad = ((0, 0), (SWA_BLOCK, SWA_BLOCK), (0, 0), (0, 0))
    kp = jnp.pad(k, pad)
    vp = jnp.pad(v, pad)
    idx = jnp.arange(nb)[:, None] * SWA_BLOCK + jnp.arange(3 * SWA_BLOCK)[None, :]
    kb = kp[:, idx]
    vb = vp[:, idx]
    s = jnp.einsum('bnqhgd,bnkhd->bhgnqk', qb, kb).astype(jnp.float32) * (Dh ** -0.5)
    qpos = jnp.arange(nb)[:, None] * SWA_BLOCK + jnp.arange(SWA_BLOCK)[None, :]
    kpos = idx - SWA_BLOCK
    ok = (jnp.abs(qpos[:, :, None] - kpos[:, None, :]) <= SWA_WINDOW) & (kpos >= 0)[:, None, :] & (kpos < L)[:, None, :]
    s = jnp.where(ok[None, None, None], s, -jnp.inf)
    sink_l = sink.astype(jnp.float32).reshape(Hkv, G)[None, :, :, None, None, None]
    m = jnp.maximum(jnp.max(s, axis=-1, keepdims=True), sink_l)
    e = jnp.exp(s - m)
    p = e / (jnp.sum(e, axis=-1, keepdims=True) + jnp.exp(sink_l - m))
    out = jnp.einsum('bhgnqk,bnkhd->bnqhgd', p.astype(v.dtype), vb)
    return out.reshape(B, L, Hq * Dh)


def _gated_short_conv(bg, cg, xin, w, b):
    u = cg * xin
    L = u.shape[1]
    up = jnp.pad(u, ((0, 0), (CONV_K // 2, CONV_K // 2), (0, 0)))
    conv = b
    for j in range(CONV_K):
        conv = conv + up[:, j:j + L] * w[j]
    return bg * conv


def _even_mixer(h, w_in, rpb, pool_w, pool_scale, w_out):
    B, L, _ = h.shape
    z = h @ w_in
    q, k, v, u = jnp.split(z, [NA_WIDTH, 2 * NA_WIDTH, 3 * NA_WIDTH], axis=-1)
    shp = (B, L, NA_HEADS, HEAD_DIM)
    a = _neighbourhood_attention(q.reshape(shp), k.reshape(shp), v.reshape(shp), rpb)
    p = _multiscale_pool(u, pool_w, pool_scale)
    return jnp.concatenate([a, p], axis=-1) @ w_out


def _odd_mixer(h, w_in, sink, conv_w, conv_b, w_out):
    B, L, _ = h.shape
    z = h @ w_in
    s1 = SWA_WIDTH
    s2 = s1 + SWA_KV_WIDTH
    s3 = s2 + SWA_KV_WIDTH
    s4 = s3 + CONV_WIDTH
    s5 = s4 + CONV_WIDTH
    q, k, v, bg, cg, xin = jnp.split(z, [s1, s2, s3, s4, s5], axis=-1)
    q = _rope(q.reshape(B, L, SWA_HEADS, HEAD_DIM))
    k = _rope(k.reshape(B, L, SWA_KV_HEADS, HEAD_DIM))
    v = v.reshape(B, L, SWA_KV_HEADS, HEAD_DIM)
    a = _window_gqa_sink(q, k, v, sink)
    d = _gated_short_conv(bg, cg, xin, conv_w, conv_b)
    return jnp.concatenate([a, d], axis=-1) @ w_out


def _swiglu(h, w1, w3, w2):
    return (jax.nn.silu(h @ w1) * (h @ w3)) @ w2


def _trunk(x, c, ada_w, ada_b, norm_g, final_g, ffn_w1, ffn_w3, ffn_w2,
           even_w_in, na_rpb, pool_w, pool_scale, even_w_out,
           odd_w_in, swa_sink, conv_w, conv_b, odd_w_out):
    c_act = jax.nn.silu(c)
    for i in range(DEPTH):
        mod = c_act @ ada_w[i] + ada_b[i]
        sh1, sc1, g1, sh2, sc2, g2 = jnp.split(mod, 6, axis=-1)
        h = _modulate(_rmsnorm(x, norm_g[i, 0]), sh1, sc1)
        j = i // 2
        if i % 2 == 0:
            mix = _even_mixer(h, even_w_in[j], na_rpb[j], pool_w[j], pool_scale[j], even_w_out[j])
        else:
            mix = _odd_mixer(h, odd_w_in[j], swa_sink[j], conv_w[j], conv_b[j], odd_w_out[j])
        x = x + g1[:, None, :] * mix
        h = _modulate(_rmsnorm(x, norm_g[i, 1]), sh2, sc2)
        x = x + g2[:, None, :] * _swiglu(h, ffn_w1[i], ffn_w3[i], ffn_w2[i])
    return _rmsnorm(x, final_g)


def setup_inputs(seed: int = 0) -> dict:
    key = jax.random.key(seed)
    ks = jax.random.split(key, 24)
    f32 = jnp.float32
    n = lambda k, s: jax.random.normal(k, s, dtype=f32)
    D = D_MODEL
    return {
        "x_prompt": n(ks[0], (BATCH, SEQ, D)),
        "x_sample": n(ks[1], (DEC_BATCH, DEC_SEQ, D)),
        "c_prompt": n(ks[2], (BATCH, D)),
        "c_sample": n(ks[3], (DEC_BATCH, D)),
        "ada_w": n(ks[4], (DEPTH, D, 6 * D)) * (0.5 * D ** -0.5),
        "ada_b": n(ks[5], (DEPTH, 6 * D)) * 0.01,
        "norm_g": 1.0 + 0.01 * n(ks[6], (DEPTH, 2, D)),
        "final_g": 1.0 + 0.01 * n(ks[7], (D,)),
        "ffn_w1": n(ks[8], (DEPTH, D, D_FF)) * D ** -0.5,
        "ffn_w3": n(ks[9], (DEPTH, D, D_FF)) * D ** -0.5,
        "ffn_w2": n(ks[10], (DEPTH, D_FF, D)) * D_FF ** -0.5,
        "even_w_in": n(ks[11], (N_EVEN, D, EVEN_IN)) * D ** -0.5,
        "na_rpb": n(ks[12], (N_EVEN, NA_HEADS, 2 * NA_WIN_ROWS - 1, 2 * NA_WIN_COLS - 1)) * 0.1,
        "pool_w": n(ks[13], (N_EVEN, POOL_GROUPS, POOL_GROUP_DIM, POOL_GROUP_DIM)) * POOL_GROUP_DIM ** -0.5,
        "pool_scale": 1.0 + 0.02 * n(ks[14], (N_EVEN, POOL_WIDTH)),
        "even_w_out": n(ks[15], (N_EVEN, MIX_WIDTH, D)) * MIX_WIDTH ** -0.5,
        "odd_w_in": n(ks[16], (N_ODD, D, ODD_IN)) * D ** -0.5,
        "swa_sink": n(ks[17], (N_ODD, SWA_HEADS)) * 0.5,
        "conv_w": n(ks[18], (N_ODD, CONV_K, CONV_WIDTH)) * CONV_K ** -0.5,
        "conv_b": n(ks[19], (N_ODD, CONV_WIDTH)) * 0.01,
        "odd_w_out": n(ks[20], (N_ODD, MIX_WIDTH, D)) * MIX_WIDTH ** -0.5,
    }


def reference(x_prompt, x_sample, c_prompt, c_sample, ada_w, ada_b, norm_g, final_g,
              ffn_w1, ffn_w3, ffn_w2, even_w_in, na_rpb, pool_w, pool_scale, even_w_out,
              odd_w_in, swa_sink, conv_w, conv_b, odd_w_out):
    y_prompt = _trunk(x_prompt, c_prompt, ada_w, ada_b, norm_g, final_g, ffn_w1, ffn_w3, ffn_w2,
                      even_w_in, na_rpb, pool_w, pool_scale, even_w_out,
                      odd_w_in, swa_sink, conv_w, conv_b, odd_w_out)
    y_sample = _trunk(x_sample, c_sample, ada_w, ada_b, norm_g, final_g, ffn_w1, ffn_w3, ffn_w2,
                      even_w_in, na_rpb, pool_w, pool_scale, even_w_out,
                      odd_w_in, swa_sink, conv_w, conv_b, odd_w_out)
    return (y_prompt, y_sample)
```

```python
import contextlib
import os
import numpy as np
DBG = int(os.environ.get('KDBG', '9'))
import concourse.bass as bass
import concourse.mybir as mybir
from concourse.bass_utils import run_bass_kernel_spmd

F32 = mybir.dt.float32
BF16 = mybir.dt.bfloat16
AF = mybir.ActivationFunctionType
ALU = mybir.AluOpType

ENG_NAMES = ("pe", "act", "dve", "pool", "sp")
D = 1024
KC = 8
DFF = 2816
FC = 22
GW = 64
TR = 16
NE, NM, NC = 28 * 64, 20 * 64, 16 * 64
MID0, CORE0 = 4 * 64, 6 * 64
NEG = -30000.0
EPS = 1e-6


class _Op:
    __slots__ = ("eng", "fn", "reads", "writes", "dma", "grp", "deps", "sig", "cnt", "idx")


class Prog:
    def __init__(self, nc):
        self.nc = nc
        self.ops = []
        self.last_w = {}
        self.readers = {}
        self.bar = None
        self.since = []
        self.ranges = {}
        self.overl = None
        self.acc = {}

    def register(self, name, lo, hi):
        self.ranges.setdefault(name, []).append((lo, hi))
        self.overl = None

    def _overlaps(self):
        names = list(self.ranges)
        ov = {n: [] for n in names}
        for i, a in enumerate(names):
            for b in names[i + 1:]:
                hit = any(l1 < h2 and l2 < h1 for (l1, h1) in self.ranges[a] for (l2, h2) in self.ranges[b])
                if hit:
                    ov[a].append(b)
                    ov[b].append(a)
        self.overl = ov

    def _alias_deps(self, o, reads, writes):
        if self.overl is None:
            self._overlaps()
        rn = set((k if isinstance(k, str) else k[0]) for k in reads)
        wn = set((k if isinstance(k, str) else k[0]) for k in writes)
        deps = set()
        for n in wn:
            for c in self.overl.get(n, ()):
                a = self.acc.get(c)
                if a:
                    deps.update(a["r"].values()); deps.update(a["w"].values()); deps.update(a["rd"]); deps.update(a["wd"])
                    self.acc[c] = None
        for n in rn:
            if n in wn:
                continue
            for c in self.overl.get(n, ()):
                a = self.acc.get(c)
                if a:
                    deps.update(a["w"].values()); deps.update(a["wd"])
        for n, kind in [(n, "w") for n in wn] + [(n, "r") for n in rn if n not in wn]:
            if n not in self.ranges:
                continue
            a = self.acc.get(n)
            if not a:
                a = {"r": {}, "w": {}, "rd": set(), "wd": set()}
                self.acc[n] = a
            if o.dma:
                a[kind + "d"].add(o.idx)
            else:
                a[kind][o.eng] = o.idx
        return deps

    def barrier(self, fn):
        deps = set(self.since)
        if self.bar is not None:
            deps.add(self.bar)
        o = self.op("pool", fn)
        o.deps |= deps
        o.deps.discard(o.idx)
        self.bar = o.idx
        self.since = []
        return o

    def op(self, eng, fn, reads=(), writes=(), dma=False, grp=None):
        o = _Op()
        o.eng, o.fn, o.dma, o.grp = eng, fn, dma, grp
        o.idx = len(self.ops)
        deps = set()
        for r in reads:
            w = self.last_w.get(r)
            if w is not None:
                deps.add(w)
        for r in writes:
            w = self.last_w.get(r)
            if w is not None:
                deps.add(w)
            rl = self.readers.get(r)
            if rl:
                deps.update(rl)
        if self.bar is not None:
            deps.add(self.bar)
        deps |= self._alias_deps(o, reads, writes)
        deps.discard(o.idx)
        o.deps = deps
        self.since.append(o.idx)
        for r in reads:
            self.readers.setdefault(r, []).append(o.idx)
        for r in writes:
            self.last_w[r] = o.idx
            self.readers[r] = []
        self.ops.append(o)
        return o

    def emit(self):
        nc = self.nc
        ops = self.ops
        for o in ops:
            o.sig = bool(o.dma)
        for o in ops:
            for d in o.deps:
                p = ops[d]
                if p.dma or p.eng != o.eng or o.eng != "pe" or o.dma:
                    p.sig = True
        cnt = {}
        for o in ops:
            if not o.sig:
                continue
            key = ("dma", o.grp) if o.dma else ("eng", o.eng)
            cnt[key] = cnt.get(key, 0) + (16 if o.dma else 1)
            o.cnt = (key, cnt[key])
        self.final_counts = dict(cnt)
        stack = contextlib.ExitStack()
        sems = {}
        for i, k in enumerate(cnt.keys()):
            sems[k] = stack.enter_context(nc.semaphore("s%d" % i))
        know = {e: {} for e in ENG_NAMES}
        snap = {}
        streams = {e: [] for e in ENG_NAMES}
        for o in ops:
            kn = know[o.eng]
            waits = {}
            for d in sorted(o.deps):
                p = ops[d]
                if (not p.dma) and p.eng == o.eng and o.eng == "pe" and not o.dma:
                    continue
                key, val = p.cnt
                if kn.get(key, 0) >= val:
                    continue
                waits[key] = max(waits.get(key, 0), val)
                for k2, v2 in snap[d].items():
                    if kn.get(k2, 0) < v2:
                        kn[k2] = v2
                kn[key] = max(kn.get(key, 0), val)
            streams[o.eng].append((list(waits.items()), o))
            if o.sig:
                s = dict(kn)
                s[o.cnt[0]] = max(s.get(o.cnt[0], 0), o.cnt[1])
                snap[o.idx] = s
        with stack:
            with nc.Block() as block:
                def mk(ename):
                    def body(eng):
                        for wl, o in streams[ename]:
                            for k, v in wl:
                                eng.wait_ge(sems[k], v)
                            ins = o.fn(eng)
                            if o.sig:
                                ins.then_inc(sems[o.cnt[0]], 16 if o.dma else 1)
                        if ename == "sp":
                            for k, v in self.final_counts.items():
                                eng.wait_ge(sems[k], v)
                    return body
                block.tensor(mk("pe"))
                block.scalar(mk("act"))
                block.vector(mk("dve"))
                block.gpsimd(mk("pool"))
                block.sync(mk("sp"))


def tk(name, c, t0, t1):
    return [(name, c, b) for b in range(t0 // 128, (t1 - 1) // 128 + 1)]


def tblocks(t0, t1, step=512):
    out = []
    t = t0
    while t < t1:
        out.append((t, min(t + step, t1)))
        t += step
    return out


def na_blocks(m):
    if m == 1:
        return list(range(1, 7))
    if m == 8:
        return list(range(7, 13))
    return list(range(m, m + 5))


def build_program(NT, stop_after=None, stages=("mix0", "ffn0", "mix1", "ffn1"), _plans=None, _record=False):
    if _plans is None and not _record:
        _plans = build_program(NT, stop_after, stages, _record=True)
    nc = bass.Bass("TRN2", target_bir_lowering=False)

    def din(name, shape):
        return nc.dram_tensor(name, list(shape), F32, kind="ExternalInput").ap()

    xT = din("xT", [NT, 128, KC, NE])
    NTP = 8
    cT = din("cT", [128, KC, NTP])
    valid_d = din("valid", [NT, 1, NE])
    invc_d = din("invc", [NT, 1, 4 * NM])
    nam_d = din("nam", [NT, 28, NM])
    swm_d = din("swm", [NT, 1, NM])
    rope_d = din("rope", [NT, 128, 2 * NM])
    ada_d = din("ada", [2, 48, 128, KC * 128])
    adab_d = din("adab", [128, 96])
    ng_d = din("ng", [128, 40])
    wa0_d = din("wa0", [12, 128, KC * 128])
    wv0_d = din("wv0", [128, KC * 512])
    wo0_d = din("wo0", [8, 128, KC * 128])
    wa1_d = din("wa1", [24, 128, KC * 128])
    wv1_d = din("wv1", [128, KC * 128])
    wo1_d = din("wo1", [8, 128, KC * 128])
    w1_d = din("w1", [2, FC, 128, KC * 128])
    w3_d = din("w3", [2, FC, 128, KC * 128])
    w2_d = din("w2", [2, 8, 128, FC * 128])
    poolw_d = din("poolw", [128, 512])
    small_d = din("small", [128, 32])
    ttab_d = din("ttab", [128, 8 * 16 * 64])
    cmat_d = din("cmat", [128, 6 * 128])
    ebig_d = din("ebig", [128, 14 * 128])
    yT = nc.dram_tensor("yT", [NT, 128, KC, NC], F32, kind="ExternalOutput").ap()

    P = Prog(nc)
    st = contextlib.ExitStack()
    off = [16576]
    KEYNAME = {"sq1": "sq", "sq2": "sq", "sq": "sq", "rstd": "rstd", "PT1": "PT1", "V1": "V1", "q1T": "q1T", "k1T": "k1T",
               "wv1": "wv1", "rec1": "rec1", "cat1": "cat1", "u1c": "u1c", "On1_all": "On1_all", "ring2": "ring2", "ystage": "ntmp", "PTL": "PT1", "ring": "ring2", "qst": "qst",
               "Vx": "Vx", "xs": "xs", "invg": "invc"}

    def sb(name, shape, dt):
        nbytes = int(np.prod(shape[1:])) * (4 if dt == F32 else 2)
        nbytes = (nbytes + 31) // 32 * 32
        t = nc.alloc_sbuf_tensor_at(name, list(shape), dt, offset=off[0])
        kn = KEYNAME.get(name.rstrip("0123456789"), name.rstrip("0123456789"))
        kn = KEYNAME.get(name, kn)
        P.register(kn, off[0], off[0] + nbytes)
        off[0] += nbytes
        return t

    ident = sb("ident", [128, 128], BF16)
    emat = sb("emat", [128, 128], BF16)
    onesb = sb("onesb", [128, 128], BF16)
    tri = sb("tri", [128, 384], BF16)
    ebig = sb("ebig", [128, 14 * 128], BF16)
    onesw = sb("onesw", [128, 512], BF16)
    ttab = sb("ttab", [128, 8 * 16 * 64], BF16)
    poolw = sb("poolw", [128, 512], BF16)
    small = sb("small", [128, 32], F32)
    esink = sb("esink", [128, 8], F32)
    adab = sb("adab", [128, 96], F32)
    ng = sb("ng", [128, 40], F32)
    epsb = sb("epsb", [128, 1], F32)
    cact = sb("cact", [128, KC * NTP], BF16)
    cin = sb("cin", [128, KC * NTP], F32)
    modT = sb("modT", [128, 96 * NTP], F32)
    gsb = sb("gsb", [128, 32], F32)
    bscr = sb("bscr", [128, 8], F32)
    _csig = sb("csig", [128, KC * 8], F32)
    sb_late = lambda name, shape, dt: _csig
    x = sb("x", [128, KC * NM], F32)
    hT = sb("hT", [128, KC * NE], BF16)
    rstd_off = off[0]
    rstdp = [sb("rstd%d" % i, [128, 512], F32) for i in range(2)]
    validb = sb("validb", [128, NE], BF16)
    ntmp_off = off[0]
    ntmp = [sb("ntmp%d" % i, [128, 512], F32) for i in range(2)]
    ringA = [sb("ringA%d" % i, [128, KC * 128], BF16) for i in range(4)]
    ring2 = [sb("ring2%d" % i, [128, FC * 128], BF16) for i in range(2)]
    _keep = off[0]
    off[0] = rstd_off
    qst0 = sb("qst0", [128, NM], BF16)
    assert off[0] <= rstd_off + 4096
    off[0] = ntmp_off
    qst1 = sb("qst1", [128, NM], BF16)
    assert off[0] <= ntmp_off + 4096
    off[0] = _keep
    qst = [qst0, qst1]
    ov0 = off[0]
    xs = [sb("xs%d" % i, [128, KC * 512], F32) for i in range(2)]
    sq = sb("sq", [128, KC * 512], BF16)
    end_l0a = off[0]
    off[0] = ov0
    qT = sb("qT", [128, 4 * NM], BF16)
    kT = sb("kT", [128, 4 * NE], BF16)
    Vx = sb("Vx", [128, 14 * 8 * 65], BF16)
    PT = [sb("PT%d" % i, [128, 768], BF16) for i in range(2)]
    rec = sb("rec", [128, 16], F32)
    namq = sb("namq", [128, NM], BF16)
    UO = 16
    NU = NM + 2 * UO
    u = sb("u", [128, 4 * NU], F32)
    ov2 = off[0]
    wv = sb("wv", [128, KC * 512], BF16)
    end_l0b = off[0]
    off[0] = ov2
    ptmp = [sb("ptmp%d" % i, [128, NU], F32) for i in range(2)]
    invg = sb("invg", [128, NM], F32)
    end_l0c = off[0]
    off[0] = ov2
    On_all = sb("On_all", [128, 10 * 512], BF16)
    PT = PT + [sb("PT%d" % i, [128, 768], BF16) for i in range(2, 7)]
    end_l0d = off[0]
    off[0] = ov0
    aT = sb("aT", [128, FC * NM], BF16)
    sq2 = sb("sq2", [128, KC * 512], BF16)
    sg = [sb("sg%d" % i, [128, 512], F32) for i in range(2)]
    end_ffn = off[0]
    off[0] = ov0
    yst = sb("yst", [128, 16 * 512], F32)
    assert off[0] <= 229344, off[0]
    end_ffn = max(end_ffn, off[0])
    off[0] = ov0
    q1T = sb("q1T", [128, 4 * NC], BF16)
    k1T = sb("k1T", [128, 2 * NM], BF16)
    V1 = sb("V1", [128, 10 * 2 * 65], BF16)
    wv1 = sb("wv1", [128, KC * 128], BF16)
    PT1 = [sb("PTL%d" % i, [128, 384], BF16) for i in range(4)]
    On1_all = sb("On1_all", [128, 8 * 512], BF16)
    rec1 = sb("rec1", [128, 16], F32)
    kmask = sb("kmask", [128, NM], BF16)
    rt = [sb("rt%d" % i, [128, 512], F32) for i in range(2)]
    cat1 = sb("cat1", [128, KC * NC], BF16)
    u1c = sb("u1c", [128, NM], F32)
    bgc = sb("bgc", [128, NM], F32)
    cgs = [sb("cgs%d" % i, [128, 512], F32) for i in range(2)]
    cv = sb("cv", [128, NC], F32)
    ov3 = off[0]
    sq1 = sb("sq1", [128, KC * 512], BF16)
    end_l1a = off[0]
    off[0] = ov3
    ropet = sb("ropet", [128, 2 * NM], F32)
    end_l1 = max(off[0], end_l1a)
    ystage = ntmp
    end_l0a = max(end_l0a, end_l0c, end_l0d)
    top = max(end_l0a, end_l0b, end_ffn, end_l1)
    assert top <= 229344, (top, end_l0a, end_l0b, end_ffn, end_l1)
    OVL = "ovl"

    ps = [st.enter_context(nc.psum_tensor("ps%d" % i, [128, 512], F32)) for i in range(7)]
    pst = st.enter_context(nc.psum_tensor("pst", [128, 512], F32))
    psi = [0]

    def bank():
        i = psi[0] % 5
        psi[0] += 1
        return i

    evi = [0]

    def evac_eng():
        evi[0] += 1
        return "act" if evi[0] % 2 else "dve"

    def v3(t, a, b):
        return t[:].rearrange("p (a b) -> p a b", a=a, b=b)

    P.op("pool", lambda e: e.dma_start(out=ident[:], in_=cmat_d[:, 0:128]), writes=["ident"], dma=True, grp="c0")
    P.op("pool", lambda e: e.dma_start(out=emat[:], in_=cmat_d[:, 128:256]), writes=["emat"], dma=True, grp="c1")
    P.op("pool", lambda e: e.dma_start(out=onesb[:], in_=cmat_d[:, 256:384]), writes=["onesb"], dma=True, grp="c2")
    P.op("pool", lambda e: e.dma_start(out=tri[:], in_=cmat_d[:, 384:768]), writes=["tri"], dma=True, grp="c3")
    P.op("pool", lambda e: e.dma_start(out=ebig[:], in_=ebig_d), writes=["ebig"], dma=True, grp="c10")
    P.op("dve", lambda e: e.memset(onesw[:], 1.0), writes=["onesw"])
    P.op("pool", lambda e: e.dma_start(out=ttab[:], in_=ttab_d), writes=["ttab"], dma=True, grp="c4")
    P.op("pool", lambda e: e.dma_start(out=poolw[:], in_=poolw_d), writes=["poolw"], dma=True, grp="c5")
    P.op("sp", lambda e: e.dma_start(out=small[:], in_=small_d), writes=["small"], dma=True, grp="c6")
    P.op("sp", lambda e: e.dma_start(out=adab[:], in_=adab_d), writes=["adab"], dma=True, grp="c7")
    P.op("sp", lambda e: e.dma_start(out=ng[:], in_=ng_d), writes=["ng"], dma=True, grp="c8")
    P.op("sp", lambda e: e.dma_start(out=cin[:], in_=cT.rearrange("p a b -> p (a b)")), writes=["cin"], dma=True, grp="c9")
    P.op("dve", lambda e: e.memset(epsb[:], EPS), writes=["epsb"])
    P.op("act", lambda e: e.activation(out=esink[:], in_=small[:, 20:28], func=AF.Exp), reads=["small"], writes=["esink"])
    csig = sb_late("csig", [128, KC * 8], F32)
    P.op("act", lambda e: e.activation(out=csig[:], in_=cin[:], func=AF.Sigmoid), reads=["cin"], writes=["csig"])
    P.op("dve", lambda e: e.tensor_tensor(out=cact[:], in0=cin[:], in1=csig[:], op=ALU.mult), reads=["cin", "csig"], writes=["cact"])

    rings = {"ringA": ringA, "ring2": ring2}
    req = {"ringA": [], "ring2": []}
    issued = {"ringA": 0, "ring2": 0}

    def _issue(ringname, i, src):
        ring = rings[ringname]
        rb = ring[i % len(ring)]
        rk = (ringname, i % len(ring))
        P.op("pool", lambda e: e.dma_start(out=rb[:], in_=src), writes=[rk], dma=True, grp=rk)

    def get_tile(ringname, src, pending=0):
        ring = rings[ringname]
        nb = len(ring)
        k = len(req[ringname])
        req[ringname].append(src)
        plan = _plans[ringname] if _plans is not None else req[ringname]
        limit = min(len(plan) - 1, k - pending + nb - 1)
        while issued[ringname] <= limit:
            i = issued[ringname]
            _issue(ringname, i, plan[i])
            issued[ringname] += 1
        return k % nb

    cact3 = v3(cact, KC, NTP)
    for l in range(2):
        for j in range(48):
            sl_ = get_tile("ringA", ada_d[l, j])
            rb = ringA[sl_]
            rk = ("ringA", sl_)
            b = bank()
            rb3 = v3(rb, KC, 128)
            for kc in range(KC):
                P.op("pe", (lambda b, rb3, kc: lambda e: e.matmul(ps[b][:, 0:NTP], rb3[:, kc, :], cact3[:, kc, :],
                                                                     start=(kc == 0), stop=(kc == KC - 1)))(b, rb3, kc),
                     reads=[rk, "cact"], writes=[("ps", b)])
            P.op("act", (lambda b, l, j: lambda e: e.activation(
                out=modT[:, (l * 48 + j) * NTP:(l * 48 + j + 1) * NTP], in_=ps[b][:, 0:NTP], func=AF.Identity,
                bias=adab[:, l * 48 + j:l * 48 + j + 1], scale=1.0))(b, l, j),
                reads=[("ps", b), "adab"], writes=[("modT", l, j)])
    modT3 = v3(modT, 96, NTP)

    def mod_ap(l, j, t):
        return modT3[:, l * 48 + j, t:t + 1]


    def dense_A(tiles, rhs_fn, toks, evac_fn, kcn=KC, ring=None, ringname="ringA", two=False):
        ring = ring or ringA
        nb = len(ring)
        n = len(tiles)
        for j in range(n):
            sl_ = get_tile(ringname, tiles[j])
            rb3 = v3(ring[sl_], kcn, 128)
            rk = (ringname, sl_)
            for (t0, t1) in toks:
                b = bank()
                for kc in range(kcn):
                    rap, rkeys = rhs_fn(kc, t0, t1)
                    P.op("pe", (lambda b, rb3, kc, rap, t0, t1: lambda e: e.matmul(
                        ps[b][:, 0:t1 - t0], rb3[:, kc, :], rap, start=(kc == 0), stop=(kc == kcn - 1)))(b, rb3, kc, rap, t0, t1),
                        reads=[rk] + rkeys, writes=[("ps", b)])
                evac_fn(j, t0, t1, b)

    ntc = [0]

    def norm_mod(src3, s0, n, gs_c, sh_c, sqbuf, dst0, tile_reads, src_name, stream=None):
        sq3 = v3(sqbuf, KC, 512)
        h3 = v3(hT, KC, NE)
        for bi_, (t0, t1) in enumerate(tblocks(0, n)):
            w = t1 - t0
            if stream is not None:
                xb = xs[bi_ % 2]
                xb3 = v3(xb, KC, 512)
                P.op("sp", (lambda xb3, t0, t1: lambda e: e.dma_start(out=xb3[:, :, 0:t1 - t0], in_=stream[:, :, t0:t1]))(xb3, t0, t1),
                     writes=[("xs", bi_ % 2)], dma=True, grp=("xs", bi_ % 2))
                srcap = (lambda xb3: lambda c, t0, t1: xb3[:, c, 0:t1 - t0])(xb3)
                srck = (lambda k: lambda c, t0, t1: [k])(("xs", bi_ % 2))
            else:
                srcap = lambda c, t0, t1: src3[:, c, s0 + t0:s0 + t1]
                srck = lambda c, t0, t1: tk(src_name, c, s0 + t0, s0 + t1)
            b = bank()
            rstd = rstdp[bi_ % 2]
            rk_ = [("rstd", bi_ % 2)]
            for c in range(KC):
                P.op("act", (lambda c, t0, t1, srcap: lambda e: e.activation(out=sq3[:, c, 0:t1 - t0], in_=srcap(c, t0, t1),
                                                                      func=AF.Square))(c, t0, t1, srcap),
                     reads=srck(c, t0, t1) + tile_reads, writes=[("sq", c)])
                P.op("pe", (lambda c, b, w: lambda e: e.matmul(ps[b][:, 0:w], onesb[:], sq3[:, c, 0:w], start=(c == 0), stop=(c == KC - 1)))(c, b, w),
                     reads=[("sq", c), "onesb"], writes=[("ps", b)])
            P.op("act", (lambda b, w, rstd: lambda e: e.activation(out=rstd[:, 0:w], in_=ps[b][:, 0:w], func=AF.Sqrt,
                                                                  bias=epsb[:, 0:1], scale=1.0 / D))(b, w, rstd),
                 reads=[("ps", b), "epsb"], writes=rk_)
            P.op("dve", (lambda w, rstd: lambda e: e.reciprocal(out=rstd[:, 0:w], in_=rstd[:, 0:w]))(w, rstd),
                 reads=rk_, writes=rk_)
            temps = [(ntmp[0], ("ntmp", 0)), (ntmp[1], ("ntmp", 1)), (ps[5], ("ps", 5)), (ps[6], ("ps", 6)), (pst, ("ps", 7))]
            for c in range(KC):
                nt, ntk = temps[ntc[0] % 5]
                ntc[0] += 1
                P.op("dve", (lambda nt, c, t0, t1, srcap, rstd: lambda e: e.tensor_tensor(out=nt[:, 0:t1 - t0], in0=srcap(c, t0, t1),
                                                                             in1=rstd[:, 0:t1 - t0], op=ALU.mult))(nt, c, t0, t1, srcap, rstd),
                     reads=srck(c, t0, t1) + rk_ + tile_reads, writes=[ntk])
                P.op("act", (lambda nt, c, t0, t1: lambda e: e.activation(out=h3[:, c, dst0 + t0:dst0 + t1], in_=nt[:, 0:t1 - t0],
                                                                          func=AF.Identity, bias=sh_c(c), scale=gs_c(c)))(nt, c, t0, t1),
                     reads=[ntk, "gsb"] + tile_reads, writes=tk("hT", c, dst0 + t0, dst0 + t1))

    def ffn(l, t, ntok, x0):
        h3 = v3(hT, KC, NE)
        a3 = v3(aT, FC, NM)
        x3 = v3(x, KC, NM)
        toks = tblocks(0, ntok)
        n = FC
        nb = 4
        tl = []
        for f in range(FC):
            tl.append(w1_d[l, f])
            tl.append(w3_d[l, f])
        for f in range(FC):
            s1_ = get_tile("ringA", tl[2 * f])
            s3_ = get_tile("ringA", tl[2 * f + 1], pending=1)
            r1 = v3(ringA[s1_], KC, 128)
            r3 = v3(ringA[s3_], KC, 128)
            k1 = ("ringA", s1_)
            k3 = ("ringA", s3_)
            for (t0, t1) in toks:
                w = t1 - t0
                b1 = bank()
                b3 = bank()
                for (bb, rr, kk) in ((b1, r1, k1), (b3, r3, k3)):
                    for kc in range(KC):
                        P.op("pe", (lambda bb, rr, kc, t0, t1: lambda e: e.matmul(ps[bb][:, 0:t1 - t0], rr[:, kc, :], h3[:, kc, t0:t1],
                                                                                  start=(kc == 0), stop=(kc == KC - 1)))(bb, rr, kc, t0, t1),
                             reads=[kk] + tk("hT", kc, t0, t1), writes=[("ps", bb)])
                s = sg[f % 2]
                P.op("act", (lambda s, b1, w: lambda e: e.activation(out=s[:, 0:w], in_=ps[b1][:, 0:w], func=AF.Sigmoid))(s, b1, w),
                     reads=[("ps", b1), OVL], writes=[("sg", f % 2)])
                P.op("dve", (lambda s, b1, w: lambda e: e.tensor_tensor(out=s[:, 0:w], in0=s[:, 0:w], in1=ps[b1][:, 0:w], op=ALU.mult))(s, b1, w),
                     reads=[("ps", b1), ("sg", f % 2), OVL], writes=[("sg", f % 2)])
                P.op("dve", (lambda s, b3, f, t0, t1: lambda e: e.tensor_tensor(out=a3[:, f, t0:t1], in0=s[:, 0:t1 - t0], in1=ps[b3][:, 0:t1 - t0],
                                                                               op=ALU.mult))(s, b3, f, t0, t1),
                     reads=[("sg", f % 2), ("ps", b3), OVL], writes=tk("aT", f, t0, t1))

        def rhs_fn(kc, t0, t1):
            return a3[:, kc, t0:t1], tk("aT", kc, t0, t1)

        def evac(j, t0, t1, b):
            P.op("dve", lambda e: e.scalar_tensor_tensor(out=x3[:, j, x0 + t0:x0 + t1], in0=ps[b][:, 0:t1 - t0], scalar=mod_ap(l, 40 + j, t),
                                                         in1=x3[:, j, x0 + t0:x0 + t1], op0=ALU.mult, op1=ALU.add),
                 reads=[("ps", b), ("modT", l, 40 + j)] + tk("x", j, x0 + t0, x0 + t1), writes=tk("x", j, x0 + t0, x0 + t1))

        dense_A([w2_d[l, j] for j in range(8)], rhs_fn, toks, evac, kcn=FC, ring=ring2, ringname="ring2")

    def make_gs(l, which, t):
        sc0 = 8 if which == 0 else 32
        sh0 = 0 if which == 0 else 24
        col = (l * 2 + which) * 8
        sc = modT3[:, l * 48 + sc0:l * 48 + sc0 + 8, t]
        P.op("dve", lambda e: e.tensor_scalar(out=gsb[:, col:col + 8], in0=sc, scalar1=1.0, scalar2=None, op0=ALU.add),
             reads=[("modT", l, sc0 + i) for i in range(8)] + ["gsb_r"], writes=["gsb"])
        P.op("dve", lambda e: e.tensor_tensor(out=gsb[:, col:col + 8], in0=gsb[:, col:col + 8], in1=ng[:, col:col + 8], op=ALU.mult),
             reads=["gsb", "ng"], writes=["gsb"])
        return (lambda c: gsb[:, col + c:col + c + 1]), (lambda c: mod_ap(l, sh0 + c, t))

    def phase_switch():
        return

    CO1 = 128
    x3 = v3(x, KC, NM)
    h3 = v3(hT, KC, NE)
    xT4 = xT.rearrange("t p c n -> t p c n")

    def tile_body(t):
        phase_switch()
        P.op("pool", lambda e: e.dma_start(out=validb[:], in_=valid_d[t].partition_broadcast(128)),
             writes=["validb"], dma=True, grp="validb")
        if "mix0" in stages:
            gs_c, sh_c = make_gs(0, 0, t)
            norm_mod(None, 0, NE, gs_c, sh_c, sq, 0, [OVL], "xe", stream=xT4[t])
            for c in range(KC):
                P.op("sp", (lambda c: lambda e: e.dma_start(out=x3[:, c, :], in_=xT4[t, :, c, MID0:MID0 + NM]))(c),
                     writes=tk("x", c, 0, NM), dma=True, grp=("x", c))
            if stop_after == "h0":
                return
            q3 = v3(qT, 4, NM)
            k3 = v3(kT, 4, NE)
            u3 = v3(u, 4, NU)
            V4 = Vx[:].rearrange("p (b h d) -> p b h d", b=14, h=8, d=65)
            phase_switch()
            P.op("dve", lambda e: e.memset(v3(Vx, 14 * 8, 65)[:, :, 64:65], 1.0), writes=["Vx_ones", ("Vx", "ones")])

            def rhs_h(kc, t0, t1):
                return h3[:, kc, t0:t1], tk("hT", kc, t0, t1)

            P.op("pool", lambda e: e.dma_start(out=wv[:], in_=wv0_d), reads=[OVL], writes=["wv"], dma=True, grp="wv")

            def ev_q(j, t0, t1, b):
                P.op("act", lambda e: e.activation(out=q3[:, j, t0 - MID0:t1 - MID0], in_=ps[b][:, 0:t1 - t0], func=AF.Identity, scale=0.125),
                     reads=[("ps", b), OVL], writes=tk("qT", j, t0 - MID0, t1 - MID0))
            dense_A([wa0_d[j] for j in range(4)], rhs_h, tblocks(MID0, MID0 + NM), ev_q)

            def ev_k(j, t0, t1, b):
                en = evac_eng()
                if en == "act":
                    P.op("act", lambda e: e.activation(out=k3[:, j, t0:t1], in_=ps[b][:, 0:t1 - t0], func=AF.Copy),
                         reads=[("ps", b), OVL], writes=tk("kT", j, t0, t1))
                else:
                    P.op("dve", lambda e: e.tensor_copy(out=k3[:, j, t0:t1], in_=ps[b][:, 0:t1 - t0]),
                         reads=[("ps", b), OVL], writes=tk("kT", j, t0, t1))
            dense_A([wa0_d[4 + j] for j in range(4)], rhs_h, tblocks(0, NE), ev_k)
            U0 = MID0 - UO

            def ev_u(j, t0, t1, b):
                P.op("dve", lambda e: e.tensor_tensor(out=u3[:, j, t0 - U0:t1 - U0], in0=ps[b][:, 0:t1 - t0], in1=validb[:, t0:t1], op=ALU.mult),
                     reads=[("ps", b), "validb", OVL], writes=tk("u", j, t0 - U0, t1 - U0))
            dense_A([wa0_d[8 + j] for j in range(4)], rhs_h, tblocks(U0, U0 + NU), ev_u)
            wv3 = v3(wv, KC, 512)
            for bl in range(14):
                b = bank()
                for kc in range(KC):
                    P.op("pe", (lambda b, kc, bl: lambda e: e.matmul(ps[b][:, 0:512], h3[:, kc, bl * 128:(bl + 1) * 128], wv3[:, kc, :],
                                                                     start=(kc == 0), stop=(kc == KC - 1)))(b, kc, bl),
                         reads=["wv"] + tk("hT", kc, bl * 128, bl * 128 + 128), writes=[("ps", b)])
                P.op("act", (lambda b, bl: lambda e: e.activation(out=V4[:, bl, :, 0:64], in_=ps[b][:, 0:512].rearrange("p (h d) -> p h d", h=8, d=64),
                                                                  func=AF.Copy))(b, bl),
                     reads=[("ps", b), OVL, "Vx_ones"], writes=[("Vx", bl)])
            phase_switch()
            for g in range(4):
                P.op("pool", (lambda g: lambda e: e.dma_start(out=invg[:], in_=invc_d[t][:, g * NM:(g + 1) * NM].partition_broadcast(128)))(g),
                     writes=["invc"], dma=True, grp="invc")
                src = u3[:, g, :]
                cur = None
                steps = [(1, 0)] if g == 0 else None
                pa, pb = ptmp[0], ptmp[1]
                P.op("dve", (lambda src, pa: lambda e: e.tensor_tensor(out=pa[:, 1:NU], in0=src[:, 0:NU - 1], in1=src[:, 1:NU], op=ALU.add))(src, pa),
                     reads=tk("u", g, 0, NU) + [OVL], writes=[("ptmp", 0)])
                cur, other, lo, hi, sh = pa, pb, 1, NU, 1
                for _ in range(g):
                    nlo, nhi = lo + sh, hi - sh
                    P.op("dve", (lambda cur, other, nlo, nhi, sh: lambda e: e.tensor_tensor(
                        out=other[:, nlo:nhi], in0=cur[:, nlo - sh:nhi - sh], in1=cur[:, nlo + sh:nhi + sh], op=ALU.add))(cur, other, nlo, nhi, sh),
                        reads=[("ptmp", 0), ("ptmp", 1), OVL], writes=[("ptmp", 0), ("ptmp", 1)])
                    cur, other = other, cur
                    lo, hi, sh = nlo, nhi, sh * 2
                P.op("dve", (lambda cur, g: lambda e: e.tensor_tensor(out=cur[:, UO:UO + NM], in0=cur[:, UO:UO + NM], in1=invg[:], op=ALU.mult))(cur, g),
                     reads=[("ptmp", 0), ("ptmp", 1), "invc", OVL], writes=[("ptmp", 0), ("ptmp", 1)])
                P.op("dve", (lambda cur, g, src: lambda e: e.tensor_tensor(out=h3[:, 4 + g, 0:NM], in0=cur[:, UO:UO + NM], in1=src[:, UO:UO + NM], op=ALU.subtract))(cur, g, src),
                     reads=[("ptmp", 0), ("ptmp", 1), OVL] + tk("u", g, 0, NU), writes=tk("hT", 4 + g, 0, NM))
                pw3 = v3(poolw, 4, 128)
                for (t0, t1) in tblocks(0, NM):
                    b = bank()
                    P.op("pe", (lambda b, g, t0, t1: lambda e: e.matmul(ps[b][:, 0:t1 - t0], pw3[:, g, :], h3[:, 4 + g, t0:t1], start=True, stop=True))(b, g, t0, t1),
                         reads=["poolw"] + tk("hT", 4 + g, t0, t1), writes=[("ps", b)])
                    P.op("act", (lambda b, g, t0, t1: lambda e: e.activation(out=h3[:, 4 + g, t0:t1], in_=ps[b][:, 0:t1 - t0], func=AF.Identity,
                                                                            scale=small[:, g:g + 1]))(b, g, t0, t1),
                         reads=[("ps", b), "small"], writes=tk("hT", 4 + g, t0, t1))
            phase_switch()
            P.op("dve", lambda e: e.memset(namq[:], 0.0), writes=["namq"])
            P.op("pool", lambda e: e.dma_start(out=namq[0:28, :], in_=nam_d[t]), writes=["namq"], dma=True, grp="namq")
            P.op("dve", lambda e: e.memset(qst[0][64:128, :], 0.0), writes=[("qst", 0)])
            P.op("dve", lambda e: e.memset(qst[1][0:64, :], 0.0), writes=[("qst", 1)])
            tt4 = ttab[:].rearrange("p (h i q) -> p h i q", h=8, i=16, q=64)
            ttf = ttab[:].rearrange("p (h x) -> p h x", h=8, x=1024)
            Oa = ps[5][:, 0:455].rearrange("p (m d) -> p m d", m=7, d=65)
            Ob = ps[6][:, 0:195].rearrange("p (m d) -> p m d", m=3, d=65)
            ona4 = On_all[:].rearrange("p (m h d) -> p m h d", m=10, h=8, d=64)
            first_blk = {m: na_blocks(m)[0] for m in range(10)}
            last_blk = {m: na_blocks(m)[-1] for m in range(10)}
            for hd in range(8):
                c, po = hd // 2, (hd % 2) * 64
                qs = qst[hd % 2]
                P.op("act", (lambda qs, c, po: lambda e: e.activation(out=qs[po:po + 64, 0:NM], in_=q3[po:po + 64, c, :], func=AF.Copy))(qs, c, po),
                     reads=tk("qT", c, 0, NM), writes=[("qst", hd % 2)])
                for j in range(14):
                    pairs = [m for m in range(10) if j in na_blocks(m)]
                    m0, nq = pairs[0], len(pairs)
                    assert pairs == list(range(m0, m0 + nq))
                    n1 = min(nq, 4) * 128
                    n2 = (nq - 4) * 128 if nq > 4 else 0
                    bA = bank()
                    bB = bank() if n2 else None
                    pt = PT[j % 7]
                    ptk = ("PT", j % 7)
                    i0_ = 12 - 2 * j + 2 * m0
                    for (bb, qo, nn) in ((bA, 0, n1), (bB, 512, n2)):
                        if not nn:
                            continue
                        qa = m0 * 128 + qo
                        P.op("pe", (lambda bb, c, qs, j, qa, nn: lambda e: e.matmul(ps[bb][:, 0:nn], k3[:, c, j * 128:(j + 1) * 128],
                                                                                   qs[:, qa:qa + nn], start=True, stop=False))(bb, c, qs, j, qa, nn),
                             reads=tk("kT", c, j * 128, j * 128 + 128) + [("qst", hd % 2)], writes=[("ps", bb)])
                        xo = i0_ * 64 + qo
                        P.op("pe", (lambda bb, hd, xo, nn: lambda e: e.matmul(ps[bb][:, 0:nn], ident[:], ttf[:, hd, xo:xo + nn],
                                                                             start=False, stop=False))(bb, hd, xo, nn),
                             reads=["ident", "ttab"], writes=[("ps", bb)])
                        P.op("pe", (lambda bb, j, qa, nn: lambda e: e.matmul(ps[bb][:, 0:nn], ebig[:, j * 128:(j + 1) * 128], namq[:, qa:qa + nn],
                                                                            start=False, stop=True))(bb, j, qa, nn),
                             reads=["ebig", "namq"], writes=[("ps", bb)])
                        P.op("act", (lambda pt, bb, qo, nn: lambda e: e.activation(out=pt[:, qo:qo + nn], in_=ps[bb][:, 0:nn], func=AF.Exp))(pt, bb, qo, nn),
                             reads=[("ps", bb)], writes=[ptk + (qo // 512,)])
                    def emit_pv(jd, hd=hd):
                        for m in range(10):
                            if last_blk[m] != jd:
                                continue
                            ob, oc = (5, m * 65) if m < 7 else (6, (m - 7) * 65)
                            for jj in na_blocks(m):
                                mm0 = [mm for mm in range(10) if jj in na_blocks(mm)][0]
                                s_ = m - mm0
                                P.op("pe", (lambda ob, oc, s_, jj, hd, m: lambda e: e.matmul(ps[ob][:, oc:oc + 65], PT[jj % 7][:, s_ * 128:(s_ + 1) * 128], V4[:, jj, hd, :],
                                                                                           start=(jj == first_blk[m]), stop=(jj == last_blk[m])))(ob, oc, s_, jj, hd, m),
                                     reads=[("PT", jj % 7, 0), ("PT", jj % 7, 1), ("Vx", jj), "Vx_ones"], writes=[("ps", ob)])
                    if j >= 1:
                        emit_pv(j - 1)
                emit_pv(13)
                P.op("dve", lambda e: e.reciprocal(out=rec[:, 0:7], in_=Oa[:, :, 64]), reads=[("ps", 5)], writes=["rec"])
                P.op("dve", lambda e: e.reciprocal(out=rec[:, 7:10], in_=Ob[:, :, 64]), reads=[("ps", 6)], writes=["rec"])
                for m in range(10):
                    src = Oa[:, m, 0:64] if m < 7 else Ob[:, m - 7, 0:64]
                    en = "dve"
                    if en == "dve":
                        P.op("dve", (lambda src, m, hd: lambda e: e.tensor_scalar(out=ona4[:, m, hd, :], in0=src, scalar1=rec[:, m:m + 1], scalar2=None, op0=ALU.mult))(src, m, hd),
                             reads=[("ps", 5 if m < 7 else 6), "rec"], writes=[("On_all", m)])
                    else:
                        P.op("act", (lambda src, m, hd: lambda e: e.activation(out=ona4[:, m, hd, :], in_=src, func=AF.Identity, scale=rec[:, m:m + 1]))(src, m, hd),
                             reads=[("ps", 5 if m < 7 else 6), "rec"], writes=[("On_all", m)])
            for m in range(10):
                bt = bank()
                for cc in range(4):
                    P.op("pe", (lambda cc, bt, m: lambda e: e.matmul(ps[bt][:, cc * 128:(cc + 1) * 128], On_all[:, m * 512 + cc * 128:m * 512 + (cc + 1) * 128], ident[:],
                                                                    start=True, stop=True))(cc, bt, m),
                         reads=[("On_all", m), "ident"], writes=[("ps", bt)])
                P.op("act", (lambda bt, m: lambda e: e.activation(out=h3[:, 0:4, m * 128:(m + 1) * 128],
                                                                  in_=ps[bt][:, 0:512].rearrange("p (c n) -> p c n", c=4, n=128),
                                                                  func=AF.Copy))(bt, m),
                     reads=[("ps", bt)], writes=[("hT", c, m) for c in range(4)])
            def rhs_cat(kc, t0, t1):
                return h3[:, kc, t0:t1], tk("hT", kc, t0, t1)

            def ev_o0(j, t0, t1, b):
                P.op("dve", lambda e: e.scalar_tensor_tensor(out=x3[:, j, t0:t1], in0=ps[b][:, 0:t1 - t0], scalar=mod_ap(0, 16 + j, t),
                                                             in1=x3[:, j, t0:t1], op0=ALU.mult, op1=ALU.add),
                     reads=[("ps", b), ("modT", 0, 16 + j)] + tk("x", j, t0, t1), writes=tk("x", j, t0, t1))
            dense_A([wo0_d[j] for j in range(8)], rhs_cat, tblocks(0, NM), ev_o0)
            if stop_after == "mix0":
                return
        if "ffn0" in stages:
            phase_switch()
            gs_c, sh_c = make_gs(0, 1, t)
            norm_mod(x3, 0, NM, gs_c, sh_c, sq2, 0, [OVL], "x")
            ffn(0, t, NM, 0)
            if stop_after == "l0":
                return
        if "mix1" in stages:
            phase_switch()
            gs_c, sh_c = make_gs(1, 0, t)
            norm_mod(x3, 0, NM, gs_c, sh_c, sq1, 0, [OVL], "x")
            phase_switch()
            P.op("pool", lambda e: e.dma_start(out=ropet[:], in_=rope_d[t]), reads=[OVL], writes=["ropet"], dma=True, grp="ropet")
            P.op("pool", lambda e: e.dma_start(out=wv1[:], in_=wv1_d), reads=[OVL], writes=["wv1"], dma=True, grp="wv1")
            P.op("dve", lambda e: e.memset(v3(V1, 20, 65)[:, :, 64:65], 1.0), reads=[OVL], writes=["V1_ones", ("V1", "ones")])
            q13 = v3(q1T, 4, NC)
            k13 = v3(k1T, 2, NM)
            V14 = V1[:].rearrange("p (b g d) -> p b g d", b=10, g=2, d=65)
            c13 = v3(cat1, KC, NC)
            CK, SK = 0, NM
            CQ, SQ = CK, SK

            def rope_proj(tile_a, tile_b, toks, coff, soff, tbase, dst_fn, dst_keys_fn):
                sa = get_tile("ringA", tile_a)
                sb_ = get_tile("ringA", tile_b, pending=1)
                for (t0, t1) in toks:
                    w = t1 - t0
                    ba, bb = bank(), bank()
                    for (bk, sl) in ((ba, sa), (bb, sb_)):
                        r3 = v3(ringA[sl], KC, 128)
                        for kc in range(KC):
                            P.op("pe", (lambda bk, r3, kc, t0, t1: lambda e: e.matmul(ps[bk][:, 0:t1 - t0], r3[:, kc, :], h3[:, kc, t0:t1],
                                                                                      start=(kc == 0), stop=(kc == KC - 1)))(bk, r3, kc, t0, t1),
                                 reads=[("ringA", sl)] + tk("hT", kc, t0, t1), writes=[("ps", bk)])
                    ra, rb_ = rt[0], rt[1]
                    ka, kb = ("rt", 0), ("rt", 1)
                    rti[0] += 2
                    P.op("dve", (lambda ra, ba, t0, t1: lambda e: e.tensor_tensor(out=ra[:, 0:t1 - t0], in0=ps[ba][:, 0:t1 - t0],
                                                                                 in1=ropet[:, coff + t0 - tbase:coff + t1 - tbase], op=ALU.mult))(ra, ba, t0, t1),
                         reads=[("ps", ba), "ropet", OVL], writes=[ka])
                    P.op("dve", (lambda rb_, bb, t0, t1: lambda e: e.tensor_tensor(out=rb_[:, 0:t1 - t0], in0=ps[bb][:, 0:t1 - t0],
                                                                                  in1=ropet[:, soff + t0 - tbase:soff + t1 - tbase], op=ALU.mult))(rb_, bb, t0, t1),
                         reads=[("ps", bb), "ropet", OVL], writes=[kb])
                    P.op("dve", (lambda ra, rb_, t0, t1: lambda e: e.tensor_tensor(out=dst_fn(t0, t1), in0=ra[:, 0:t1 - t0], in1=rb_[:, 0:t1 - t0], op=ALU.add))(ra, rb_, t0, t1),
                         reads=[ka, kb, OVL], writes=dst_keys_fn(t0, t1))

            rope_slot = [0]
            rti = [0]
            for c in range(4):
                rope_proj(wa1_d[c], wa1_d[4 + c], tblocks(CO1, CO1 + NC), CQ, SQ, 0,
                          (lambda c: lambda t0, t1: q13[:, c, t0 - CO1:t1 - CO1])(c),
                          (lambda c: lambda t0, t1: tk("q1T", c, t0 - CO1, t1 - CO1))(c))
            for g in range(2):
                rope_proj(wa1_d[8 + g], wa1_d[10 + g], tblocks(0, NM), CK, SK, 0,
                          (lambda g: lambda t0, t1: k13[:, g, t0:t1])(g),
                          (lambda g: lambda t0, t1: tk("k1T", g, t0, t1))(g))
            wv13 = v3(wv1, KC, 128)
            for bl in range(10):
                b = bank()
                for kc in range(KC):
                    P.op("pe", (lambda b, kc, bl: lambda e: e.matmul(ps[b][:, 0:128], h3[:, kc, bl * 128:(bl + 1) * 128], wv13[:, kc, :],
                                                                     start=(kc == 0), stop=(kc == KC - 1)))(b, kc, bl),
                         reads=["wv1"] + tk("hT", kc, bl * 128, bl * 128 + 128), writes=[("ps", b)])
                P.op("act", (lambda b, bl: lambda e: e.activation(out=V14[:, bl, :, 0:64], in_=ps[b][:, 0:128].rearrange("p (g d) -> p g d", g=2, d=64),
                                                                  func=AF.Copy))(b, bl),
                     reads=[("ps", b), OVL, "V1_ones"], writes=[("V1", bl)])
            for c in range(4):
                trip = [wa1_d[12 + c], wa1_d[16 + c], wa1_d[20 + c]]
                slots = []
                for pi_, tl_ in enumerate(trip):
                    sl = get_tile("ringA", tl_, pending=pi_)
                    slots.append(sl)
                for (t0, t1) in tblocks(0, NM):
                    w = t1 - t0
                    bks = [bank(), bank(), bank()]
                    for bk, sl in zip(bks, slots):
                        r3 = v3(ringA[sl], KC, 128)
                        for kc in range(KC):
                            P.op("pe", (lambda bk, r3, kc, t0, t1: lambda e: e.matmul(ps[bk][:, 0:t1 - t0], r3[:, kc, :], h3[:, kc, t0:t1],
                                                                                      start=(kc == 0), stop=(kc == KC - 1)))(bk, r3, kc, t0, t1),
                                 reads=[("ringA", sl)] + tk("hT", kc, t0, t1), writes=[("ps", bk)])
                    P.op("act", (lambda bk, t0, t1: lambda e: e.activation(out=bgc[:, t0:t1], in_=ps[bk][:, 0:t1 - t0], func=AF.Copy))(bks[0], t0, t1),
                         reads=[("ps", bks[0]), OVL], writes=tk("bgc", 0, t0, t1))
                    cb = cgs[(t0 // 512) % 2]
                    P.op("act", (lambda bk, cb, w: lambda e: e.activation(out=cb[:, 0:w], in_=ps[bk][:, 0:w], func=AF.Copy))(bks[1], cb, w),
                         reads=[("ps", bks[1]), OVL], writes=[("cgs", (t0 // 512) % 2)])
                    P.op("dve", (lambda bk, cb, t0, t1: lambda e: e.tensor_tensor(out=u1c[:, t0:t1], in0=cb[:, 0:t1 - t0], in1=ps[bk][:, 0:t1 - t0], op=ALU.mult))(bks[2], cb, t0, t1),
                         reads=[("ps", bks[2]), ("cgs", (t0 // 512) % 2), OVL], writes=tk("u1c", 0, t0, t1))
                    P.op("dve", (lambda t0, t1: lambda e: e.tensor_tensor(out=u1c[:, t0:t1], in0=u1c[:, t0:t1], in1=validb[:, MID0 + t0:MID0 + t1], op=ALU.mult))(t0, t1),
                         reads=tk("u1c", 0, t0, t1) + ["validb", OVL], writes=tk("u1c", 0, t0, t1))
                a0, a1 = CO1, CO1 + NC
                P.op("dve", (lambda c: lambda e: e.tensor_scalar(out=cv[:], in0=u1c[:, a0 - 1:a1 - 1], scalar1=small[:, 4 + c:5 + c], scalar2=small[:, 16 + c:17 + c],
                                                                  op0=ALU.mult, op1=ALU.add))(c),
                     reads=tk("u1c", 0, 0, NM) + ["small", OVL], writes=["cv"])
                P.op("dve", (lambda c: lambda e: e.scalar_tensor_tensor(out=cv[:], in0=u1c[:, a0:a1], scalar=small[:, 8 + c:9 + c], in1=cv[:], op0=ALU.mult, op1=ALU.add))(c),
                     reads=tk("u1c", 0, 0, NM) + ["small", "cv", OVL], writes=["cv"])
                P.op("dve", (lambda c: lambda e: e.scalar_tensor_tensor(out=cv[:], in0=u1c[:, a0 + 1:a1 + 1], scalar=small[:, 12 + c:13 + c], in1=cv[:], op0=ALU.mult, op1=ALU.add))(c),
                     reads=tk("u1c", 0, 0, NM) + ["small", "cv", OVL], writes=["cv"])
                P.op("dve", (lambda c: lambda e: e.tensor_tensor(out=c13[:, 4 + c, :], in0=cv[:], in1=bgc[:, a0:a1], op=ALU.mult))(c),
                     reads=["cv", OVL] + tk("bgc", 0, 0, NM), writes=tk("cat1", 4 + c, 0, NC))
            P.op("dve", lambda e: e.memset(kmask[:], 0.0), writes=["kmask"])
            P.op("pool", lambda e: e.dma_start(out=kmask[0:1, :], in_=swm_d[t]), writes=["kmask"], dma=True, grp="kmask")
            P.op("dve", lambda e: e.memset(qst[0][64:128, :], 0.0), writes=[("qst", 0)])
            P.op("dve", lambda e: e.memset(qst[1][0:64, :], 0.0), writes=[("qst", 1)])
            O1a = ps[5][:, 0:455].rearrange("p (m d) -> p m d", m=7, d=65)
            O1b = ps[6][:, 0:65].rearrange("p (m d) -> p m d", m=1, d=65)
            on14 = On1_all[:].rearrange("p (m h d) -> p m h d", m=8, h=8, d=64)
            for hd in range(8):
                g, c, po = hd // 4, hd // 2, (hd % 2) * 64
                qs = qst[hd % 2]
                P.op("act", (lambda qs, c, po: lambda e: e.activation(out=qs[po:po + 64, 0:NC], in_=q13[po:po + 64, c, :], func=AF.Copy))(qs, c, po),
                     reads=tk("q1T", c, 0, NC), writes=[("qst", hd % 2)])
                for j in range(10):
                    qbs = [qb for qb in range(8) if qb <= j <= qb + 2]
                    qb0, nq = qbs[0], len(qbs)
                    nn = nq * 128
                    bS = bank()
                    pt = PT1[j % 4]
                    ptk = ("PT1", j % 4)
                    P.op("pe", (lambda bS, g, qs, j, qb0, nn: lambda e: e.matmul(ps[bS][:, 0:nn], k13[:, g, j * 128:(j + 1) * 128],
                                                                                   qs[:, qb0 * 128:qb0 * 128 + nn], start=True, stop=False))(bS, g, qs, j, qb0, nn),
                         reads=tk("k1T", g, j * 128, j * 128 + 128) + [("qst", hd % 2)], writes=[("ps", bS)])
                    to = (qb0 - (j - 2)) * 128
                    P.op("pe", (lambda bS, to, nn: lambda e: e.matmul(ps[bS][:, 0:nn], ident[:], tri[:, to:to + nn], start=False, stop=False))(bS, to, nn),
                         reads=["ident", "tri"], writes=[("ps", bS)])
                    P.op("pe", (lambda bS, j, nn: lambda e: e.matmul(ps[bS][:, 0:nn], kmask[:, j * 128:(j + 1) * 128], onesw[:, 0:nn],
                                                                    start=False, stop=True))(bS, j, nn),
                         reads=["kmask", "onesw"], writes=[("ps", bS)])
                    P.op("act", (lambda pt, bS, nn: lambda e: e.activation(out=pt[:, 0:nn], in_=ps[bS][:, 0:nn], func=AF.Exp, scale=0.125))(pt, bS, nn),
                         reads=[("ps", bS)], writes=[ptk])
                    def emit_pv1(qb, g=g):
                        ob, oc = (5, qb * 65) if qb < 7 else (6, 0)
                        for jj in (qb, qb + 1, qb + 2):
                            qb0_ = max(0, jj - 2)
                            s_ = qb - qb0_
                            P.op("pe", (lambda ob, oc, s_, jj, g, qb: lambda e: e.matmul(ps[ob][:, oc:oc + 65], PT1[jj % 4][:, s_ * 128:(s_ + 1) * 128], V14[:, jj, g, :],
                                                                                       start=(jj == qb), stop=(jj == qb + 2)))(ob, oc, s_, jj, g, qb),
                                 reads=[("PT1", jj % 4), ("V1", jj), "V1_ones"], writes=[("ps", ob)])
                    if j >= 3:
                        emit_pv1(j - 3)
                emit_pv1(7)
                if DBG < 3:
                    continue
                P.op("dve", (lambda hd: lambda e: e.tensor_scalar(out=rec1[:, 0:7], in0=O1a[:, :, 64], scalar1=esink[:, hd:hd + 1], scalar2=None, op0=ALU.add))(hd),
                     reads=[("ps", 5), "esink"], writes=["rec1"])
                P.op("dve", (lambda hd: lambda e: e.tensor_scalar(out=rec1[:, 7:8], in0=O1b[:, :, 64], scalar1=esink[:, hd:hd + 1], scalar2=None, op0=ALU.add))(hd),
                     reads=[("ps", 6), "esink"], writes=["rec1"])
                P.op("dve", lambda e: e.reciprocal(out=rec1[:, 0:8], in_=rec1[:, 0:8]), reads=["rec1"], writes=["rec1"])
                for qb in range(8):
                    src = O1a[:, qb, 0:64] if qb < 7 else O1b[:, 0, 0:64]
                    if True:
                        P.op("dve", (lambda src, qb, hd: lambda e: e.tensor_scalar(out=on14[:, qb, hd, :], in0=src, scalar1=rec1[:, qb:qb + 1], scalar2=None, op0=ALU.mult))(src, qb, hd),
                             reads=[("ps", 5 if qb < 7 else 6), "rec1"], writes=[("On1_all", qb)])
                    else:
                        P.op("act", (lambda src, qb, hd: lambda e: e.activation(out=on14[:, qb, hd, :], in_=src, func=AF.Identity, scale=rec1[:, qb:qb + 1]))(src, qb, hd),
                             reads=[("ps", 5 if qb < 7 else 6), "rec1"], writes=[("On1_all", qb)])
            for qb in range(8 if DBG >= 4 else 0):
                bt = bank()
                for cc in range(4):
                    P.op("pe", (lambda cc, bt, qb: lambda e: e.matmul(ps[bt][:, cc * 128:(cc + 1) * 128], On1_all[:, qb * 512 + cc * 128:qb * 512 + (cc + 1) * 128], ident[:],
                                                                     start=True, stop=True))(cc, bt, qb),
                         reads=[("On1_all", qb), "ident"], writes=[("ps", bt)])
                P.op("act", (lambda bt, qb: lambda e: e.activation(out=c13[:, 0:4, qb * 128:(qb + 1) * 128],
                                                                   in_=ps[bt][:, 0:512].rearrange("p (c n) -> p c n", c=4, n=128),
                                                                   func=AF.Copy))(bt, qb),
                     reads=[("ps", bt)], writes=[("cat1", c, qb) for c in range(4)])

            def rhs_cat1(kc, t0, t1):
                return c13[:, kc, t0:t1], tk("cat1", kc, t0, t1)

            def ev_o1(j, t0, t1, b):
                P.op("dve", lambda e: e.scalar_tensor_tensor(out=x3[:, j, CO1 + t0:CO1 + t1], in0=ps[b][:, 0:t1 - t0], scalar=mod_ap(1, 16 + j, t),
                                                             in1=x3[:, j, CO1 + t0:CO1 + t1], op0=ALU.mult, op1=ALU.add),
                     reads=[("ps", b), ("modT", 1, 16 + j), OVL] + tk("x", j, CO1 + t0, CO1 + t1), writes=tk("x", j, CO1 + t0, CO1 + t1))
            dense_A([wo1_d[j] for j in range(8)], rhs_cat1, tblocks(0, NC), ev_o1)
        if "ffn1" in stages:
            phase_switch()
            gs_c, sh_c = make_gs(1, 1, t)
            norm_mod(x3, CO1, NC, gs_c, sh_c, sq2, 0, [OVL], "x")
            ffn(1, t, NC, CO1)
        phase_switch()
        sq3 = v3(sq2, KC, 512)
        for bi_, (t0, t1) in enumerate(tblocks(0, NC)):
            w = t1 - t0
            b = bank()
            rstd = rstdp[bi_ % 2]
            rk_ = [("rstd", bi_ % 2)]
            for c in range(KC):
                P.op("act", (lambda c, t0, t1: lambda e: e.activation(out=sq3[:, c, 0:t1 - t0], in_=x3[:, c, CO1 + t0:CO1 + t1], func=AF.Square))(c, t0, t1),
                     reads=tk("x", c, CO1 + t0, CO1 + t1) + [OVL], writes=[("sq", c)])
                P.op("pe", (lambda c, b, w: lambda e: e.matmul(ps[b][:, 0:w], onesb[:], sq3[:, c, 0:w], start=(c == 0), stop=(c == KC - 1)))(c, b, w),
                     reads=[("sq", c), "onesb"], writes=[("ps", b)])
            P.op("act", (lambda b, w, rstd: lambda e: e.activation(out=rstd[:, 0:w], in_=ps[b][:, 0:w], func=AF.Sqrt, bias=epsb[:, 0:1], scale=1.0 / D))(b, w, rstd),
                 reads=[("ps", b), "epsb"], writes=rk_)
            P.op("dve", (lambda w, rstd: lambda e: e.reciprocal(out=rstd[:, 0:w], in_=rstd[:, 0:w]))(w, rstd),
                 reads=rk_, writes=rk_)
            for c in range(KC):
                yi = bi_ * KC + c
                ys = yst[:, yi * 512:(yi + 1) * 512]
                P.op("dve", (lambda ys, c, t0, t1, rstd: lambda e: e.scalar_tensor_tensor(out=ys[:, 0:t1 - t0], in0=x3[:, c, CO1 + t0:CO1 + t1], scalar=ng[:, 32 + c:33 + c],
                                                                                    in1=rstd[:, 0:t1 - t0], op0=ALU.mult, op1=ALU.mult))(ys, c, t0, t1, rstd),
                     reads=tk("x", c, CO1 + t0, CO1 + t1) + rk_ + ["ng", OVL], writes=[("yst", yi)])
                P.op("sp", (lambda ys, c, t0, t1: lambda e: e.dma_start(out=yT[t, :, c, t0:t1], in_=ys[:, 0:t1 - t0]))(ys, c, t0, t1),
                     reads=[("yst", yi)], dma=True, grp=("yst", yi))
    for t_ in range(NT):
        tile_body(t_)
    if "dbgmod" in stages:
        phase_switch()
        P.op("sp", lambda e: e.dma_start(out=yT[0, :, 0, 0:768], in_=modT[:, 0:768]), dma=True, grp="dbg")
    if stop_after is not None:
        pass
    if _record:
        st.close()
        return {k: list(v) for k, v in req.items()}
    P.emit()
    return nc, st


def _A(W):
    K, N = W.shape
    return np.ascontiguousarray(W.reshape(K // 128, 128, N // 128, 128).transpose(2, 1, 0, 3)).reshape(N // 128, 128, (K // 128) * 128)


def _B(W):
    K, N = W.shape
    return np.ascontiguousarray(W.reshape(K // 128, 128, N).transpose(1, 0, 2)).reshape(128, (K // 128) * N)


def _vecT(v):
    return np.ascontiguousarray(v.reshape(-1, 128).T)


def prep_shared(ada_w, ada_b, norm_g, final_g, ffn_w1, ffn_w3, ffn_w2, even_w_in, na_rpb, pool_w, pool_scale, even_w_out,
                odd_w_in, swa_sink, conv_w, conv_b, odd_w_out):
    f32 = np.float32
    sh = {}
    sh["ada"] = np.stack([_A(ada_w[l]) for l in range(2)])
    sh["adab"] = np.concatenate([_vecT(ada_b[0]), _vecT(ada_b[1])], axis=1).astype(f32)
    ngs = [_vecT(norm_g[0, 0]), _vecT(norm_g[0, 1]), _vecT(norm_g[1, 0]), _vecT(norm_g[1, 1]), _vecT(final_g)]
    sh["ng"] = np.ascontiguousarray(np.concatenate(ngs, axis=1)).astype(f32)
    wi = even_w_in[0]
    sh["wa0"] = _A(np.concatenate([wi[:, 0:512], wi[:, 512:1024], wi[:, 1536:2048]], axis=1))
    sh["wv0"] = _B(wi[:, 1024:1536])
    sh["wo0"] = _A(even_w_out[0])
    wo = odd_w_in[0]
    perm = np.concatenate([np.arange(h * 64, h * 64 + 64).reshape(2, 32)[::-1].reshape(-1) for h in range(8)])
    wq = wo[:, 0:512]
    wk = wo[:, 512:640]
    kperm = np.concatenate([np.arange(h * 64, h * 64 + 64).reshape(2, 32)[::-1].reshape(-1) for h in range(2)])
    wks = wk[:, kperm]
    kdup = np.concatenate([wk[:, 0:64], wk[:, 0:64], wk[:, 64:128], wk[:, 64:128]], axis=1)
    ksdup = np.concatenate([wks[:, 0:64], wks[:, 0:64], wks[:, 64:128], wks[:, 64:128]], axis=1)
    sh["wa1"] = _A(np.concatenate([wq, wq[:, perm], kdup, ksdup, wo[:, 768:1280], wo[:, 1280:1792], wo[:, 1792:2304]], axis=1))
    sh["wv1"] = _B(wo[:, 640:768])
    sh["wo1"] = _A(odd_w_out[0])
    sh["w1"] = np.stack([_A(ffn_w1[l]) for l in range(2)])
    sh["w3"] = np.stack([_A(ffn_w3[l]) for l in range(2)])
    sh["w2"] = np.stack([_A(ffn_w2[l]) for l in range(2)])
    sh["poolw"] = np.ascontiguousarray(pool_w[0].transpose(1, 0, 2)).reshape(128, 512)
    small = np.zeros((128, 32), f32)
    small[:, 0:4] = _vecT(pool_scale[0])
    for j in range(3):
        small[:, 4 + 4 * j:8 + 4 * j] = _vecT(conv_w[0, j])
    small[:, 16:20] = _vecT(conv_b[0])
    small[:, 20:28] = swa_sink[0][None, :]
    sh["small"] = small
    rpb_ext = np.concatenate([na_rpb[0].reshape(8, -1), np.full((8, 1), NEG, f32)], axis=1)
    p = np.arange(128)
    a = (p // 64)[:, None, None]
    kc = (p % 64)[:, None, None]
    i = np.arange(16)[None, :, None]
    qc = np.arange(64)[None, None, :]
    dr = a - i + 8
    cs = np.clip(qc - 8, 0, 48)
    ok = (np.abs(dr) <= 7) & (kc >= cs) & (kc < cs + 16)
    dc = np.clip(kc - qc, -15, 15) + 15
    idx = np.where(ok, (np.clip(dr, -7, 7) + 7) * 31 + dc, 15 * 31)
    sh["ttab"] = np.ascontiguousarray(rpb_ext[:, idx].transpose(1, 0, 2, 3)).reshape(128, 8 * 16 * 64).astype(f32)
    cm = np.zeros((128, 768), f32)
    cm[:, 0:128] = np.eye(128)
    cm[0, 128:192] = 1.0
    cm[1, 192:256] = 1.0
    cm[:, 256:384] = 1.0
    k_ = np.arange(128)[:, None]
    q_ = np.arange(128)[None, :]
    cm[:, 384:512] = np.where(k_ <= q_, 0.0, NEG)
    cm[:, 640:768] = np.where(k_ >= q_, 0.0, NEG)
    sh["cmat"] = cm
    eb = np.zeros((128, 14 * 128), f32)
    for j in range(14):
        for k in range(128):
            eb[2 * j + k // 64, j * 128 + k] = 1.0
    sh["ebig"] = eb
    return {k: np.ascontiguousarray(v, dtype=f32) for k, v in sh.items()}


def prep_tile(xseq, R):
    f32 = np.float32
    L = xseq.shape[0]
    rows = L // GW
    e0 = (R - 6) * GW
    xt = np.zeros((NE, D), f32)
    lo, hi = max(e0, 0), min(e0 + NE, L)
    xt[lo - e0:hi - e0] = xseq[lo:hi]
    xT = np.ascontiguousarray(xt.T.reshape(KC, 128, NE).transpose(1, 0, 2))
    pos_e = e0 + np.arange(NE)
    valid = ((pos_e >= 0) & (pos_e < L)).astype(f32)[None, :]
    pos_m = (R - 2) * GW + np.arange(NM)
    invc = np.ones((4, NM), f32)
    for g, w in enumerate((2, 4, 8, 16)):
        lo_ = np.clip(pos_m - w // 2, 0, L)
        hi_ = np.clip(pos_m - w // 2 + w, 0, L)
        cnt = np.maximum(hi_ - lo_, 1).astype(f32)
        invc[g] = (1.0 / cnt).astype(f32)
    nam = np.zeros((28, 20, 64), f32)
    for r in range(28):
        kr = R - 6 + r
        for qi in range(20):
            qr = R - 2 + qi
            if 0 <= qr < rows:
                w0 = min(max(qr - 4, 0), rows - 8)
            else:
                w0 = qr - 4
            nam[r, qi, :] = 0.0 if (w0 <= kr < w0 + 8) else NEG
    swm = np.where((pos_m >= 0) & (pos_m < L), 0.0, NEG).astype(f32)[None, :]
    inv = (1.0 / (np.float32(10000.0) ** (np.arange(0, 64, 2, dtype=f32) / np.float32(64)))).astype(f32)
    pp = np.arange(128)
    jj = pp % 32
    sgn = np.where((pp % 64) < 32, -1.0, 1.0).astype(f32)

    def tabs(pos, scale):
        ang = (pos.astype(f32)[None, :] * inv[jj][:, None]).astype(f32)
        return (np.cos(ang) * f32(scale)).astype(f32), (np.sin(ang) * sgn[:, None] * f32(scale)).astype(f32)
    ck, sk = tabs(pos_m, 1.0)
    rope = np.concatenate([ck, sk], axis=1)
    return dict(xT=xT, valid=valid, invc=invc.reshape(1, 4 * NM), nam=nam.reshape(28, NM), swm=swm, rope=rope)


def run_tiles(seqs, cs, shared, n_cores, NT, stop_after=None, stages=("mix0", "ffn0", "mix1", "ffn1")):
    tiles = []
    for si, xs in enumerate(seqs):
        for R in range(0, xs.shape[0] // GW, TR):
            tiles.append((si, R))
    assert len(tiles) == n_cores * NT, (len(tiles), n_cores, NT)
    nc, st = build_program(NT, stop_after, stages)
    in_maps = []
    for core in range(n_cores):
        mine = tiles[core * NT:(core + 1) * NT]
        per = [prep_tile(seqs[si], R) for (si, R) in mine]
        m = {k: np.ascontiguousarray(np.stack([p[k] for p in per])) for k in per[0]}
        cTa = np.zeros((128, KC, 8), np.float32)
        for ti_, (si, R) in enumerate(mine):
            cTa[:, :, ti_] = _vecT(cs[si])
        m["cT"] = cTa
        m.update(shared)
        in_maps.append(m)
    with st:
        res = run_bass_kernel_spmd(nc, in_maps, core_ids=list(range(n_cores)))
    outs = [np.zeros_like(xs) for xs in seqs]
    for core in range(n_cores):
        y = res.results[core]["yT"]
        for ti, (si, R) in enumerate(tiles[core * NT:(core + 1) * NT]):
            outs[si][R * GW:(R + TR) * GW] = y[ti].transpose(2, 1, 0).reshape(NC, D)
    return outs


def kernel(x_prompt, x_sample, c_prompt, c_sample, ada_w, ada_b, norm_g, final_g,
           ffn_w1, ffn_w3, ffn_w2, even_w_in, na_rpb, pool_w, pool_scale, even_w_out,
           odd_w_in, swa_sink, conv_w, conv_b, odd_w_out):
    a = lambda v: np.asarray(v, dtype=np.float32)
    shared = prep_shared(a(ada_w), a(ada_b), a(norm_g), a(final_g), a(ffn_w1), a(ffn_w3), a(ffn_w2), a(even_w_in), a(na_rpb),
                         a(pool_w), a(pool_scale), a(even_w_out), a(odd_w_in), a(swa_sink), a(conv_w), a(conv_b), a(odd_w_out))
    xp, xs = a(x_prompt), a(x_sample)
    cp, cs_ = a(c_prompt), a(c_sample)
    seqs = [xs[i] for i in range(xs.shape[0])] + [xp[i] for i in range(xp.shape[0])]
    cs = [cs_[i] for i in range(xs.shape[0])] + [cp[i] for i in range(xp.shape[0])]
    outs = run_tiles(seqs, cs, shared, 8, 6)
    ns = xs.shape[0]
    y_sample = np.stack(outs[:ns]).astype(np.float32)
    y_prompt = np.stack(outs[ns:]).astype(np.float32)
    return (y_prompt, y_sample)
```

```python
import contextlib
import os
import numpy as np
DBG = int(os.environ.get('KDBG', '9'))
import concourse.bass as bass
import concourse.mybir as mybir
from concourse.bass_utils import run_bass_kernel_spmd

F32 = mybir.dt.float32
BF16 = mybir.dt.bfloat16
AF = mybir.ActivationFunctionType
ALU = mybir.AluOpType

ENG_NAMES = ("pe", "act", "dve", "pool", "sp")
D = 1024
KC = 8
DFF = 2816
FC = 22
GW = 64
TR = 16
NE, NM, NC = 28 * 64, 20 * 64, 16 * 64
MID0, CORE0 = 4 * 64, 6 * 64
NEG = -30000.0
EPS = 1e-6


class _Op:
    __slots__ = ("eng", "fn", "reads", "writes", "dma", "grp", "deps", "sig", "cnt", "idx")


class Prog:
    def __init__(self, nc):
        self.nc = nc
        self.ops = []
        self.last_w = {}
        self.readers = {}
        self.bar = None
        self.since = []
        self.ranges = {}
        self.overl = None
        self.acc = {}

    def register(self, name, lo, hi):
        self.ranges.setdefault(name, []).append((lo, hi))
        self.overl = None

    def _overlaps(self):
        names = list(self.ranges)
        ov = {n: [] for n in names}
        for i, a in enumerate(names):
            for b in names[i + 1:]:
                hit = any(l1 < h2 and l2 < h1 for (l1, h1) in self.ranges[a] for (l2, h2) in self.ranges[b])
                if hit:
                    ov[a].append(b)
                    ov[b].append(a)
        self.overl = ov

    def _alias_deps(self, o, reads, writes):
        if self.overl is None:
            self._overlaps()
        rn = set((k if isinstance(k, str) else k[0]) for k in reads)
        wn = set((k if isinstance(k, str) else k[0]) for k in writes)
        deps = set()
        for n in wn:
            for c in self.overl.get(n, ()):
                a = self.acc.get(c)
                if a:
                    deps.update(a["r"].values()); deps.update(a["w"].values()); deps.update(a["rd"]); deps.update(a["wd"])
                    self.acc[c] = None
        for n in rn:
            if n in wn:
                continue
            for c in self.overl.get(n, ()):
                a = self.acc.get(c)
                if a:
                    deps.update(a["w"].values()); deps.update(a["wd"])
        for n, kind in [(n, "w") for n in wn] + [(n, "r") for n in rn if n not in wn]:
            if n not in self.ranges:
                continue
            a = self.acc.get(n)
            if not a:
                a = {"r": {}, "w": {}, "rd": set(), "wd": set()}
                self.acc[n] = a
            if o.dma:
                a[kind + "d"].add(o.idx)
            else:
                a[kind][o.eng] = o.idx
        return deps

    def barrier(self, fn):
        deps = set(self.since)
        if self.bar is not None:
            deps.add(self.bar)
        o = self.op("pool", fn)
        o.deps |= deps
        o.deps.discard(o.idx)
        self.bar = o.idx
        self.since = []
        return o

    def op(self, eng, fn, reads=(), writes=(), dma=False, grp=None):
        o = _Op()
        o.eng, o.fn, o.dma, o.grp = eng, fn, dma, grp
        o.idx = len(self.ops)
        deps = set()
        for r in reads:
            w = self.last_w.get(r)
            if w is not None:
                deps.add(w)
        for r in writes:
            w = self.last_w.get(r)
            if w is not None:
                deps.add(w)
            rl = self.readers.get(r)
            if rl:
                deps.update(rl)
        if self.bar is not None:
            deps.add(self.bar)
        deps |= self._alias_deps(o, reads, writes)
        deps.discard(o.idx)
        o.deps = deps
        self.since.append(o.idx)
        for r in reads:
            self.readers.setdefault(r, []).append(o.idx)
        for r in writes:
            self.last_w[r] = o.idx
            self.readers[r] = []
        self.ops.append(o)
        return o

    def emit(self):
        nc = self.nc
        ops = self.ops
        for o in ops:
            o.sig = bool(o.dma)
        for o in ops:
            for d in o.deps:
                p = ops[d]
                if p.dma or p.eng != o.eng or o.eng != "pe" or o.dma:
                    p.sig = True
        cnt = {}
        for o in ops:
            if not o.sig:
                continue
            key = ("dma", o.grp) if o.dma else ("eng", o.eng)
            cnt[key] = cnt.get(key, 0) + (16 if o.dma else 1)
            o.cnt = (key, cnt[key])
        self.final_counts = dict(cnt)
        stack = contextlib.ExitStack()
        sems = {}
        for i, k in enumerate(cnt.keys()):
            sems[k] = stack.enter_context(nc.semaphore("s%d" % i))
        know = {e: {} for e in ENG_NAMES}
        snap = {}
        streams = {e: [] for e in ENG_NAMES}
        for o in ops:
            kn = know[o.eng]
            waits = {}
            for d in sorted(o.deps):
                p = ops[d]
                if (not p.dma) and p.eng == o.eng and o.eng == "pe" and not o.dma:
                    continue
                key, val = p.cnt
                if kn.get(key, 0) >= val:
                    continue
                waits[key] = max(waits.get(key, 0), val)
                for k2, v2 in snap[d].items():
                    if kn.get(k2, 0) < v2:
                        kn[k2] = v2
                kn[key] = max(kn.get(key, 0), val)
            streams[o.eng].append((list(waits.items()), o))
            if o.sig:
                s = dict(kn)
                s[o.cnt[0]] = max(s.get(o.cnt[0], 0), o.cnt[1])
                snap[o.idx] = s
        with stack:
            with nc.Block() as block:
                def mk(ename):
                    def body(eng):
                        for wl, o in streams[ename]:
                            for k, v in wl:
                                eng.wait_ge(sems[k], v)
                            ins = o.fn(eng)
                            if o.sig:
                                ins.then_inc(sems[o.cnt[0]], 16 if o.dma else 1)
                        if ename == "sp":
                            for k, v in self.final_counts.items():
                                eng.wait_ge(sems[k], v)
                    return body
                block.tensor(mk("pe"))
                block.scalar(mk("act"))
                block.vector(mk("dve"))
                block.gpsimd(mk("pool"))
                block.sync(mk("sp"))


def tk(name, c, t0, t1):
    return [(name, c, b) for b in range(t0 // 128, (t1 - 1) // 128 + 1)]


def tblocks(t0, t1, step=512):
    out = []
    t = t0
    while t < t1:
        out.append((t, min(t + step, t1)))
        t += step
    return out


def na_blocks(m):
    if m == 1:
        return list(range(1, 7))
    if m == 8:
        return list(range(7, 13))
    return list(range(m, m + 5))


def build_program(NT, stop_after=None, stages=("mix0", "ffn0", "mix1", "ffn1"), _plans=None, _record=False):
    if _plans is None and not _record:
        _plans = build_program(NT, stop_after, stages, _record=True)
    nc = bass.Bass("TRN2", target_bir_lowering=False)

    def din(name, shape):
        return nc.dram_tensor(name, list(shape), F32, kind="ExternalInput").ap()

    xT = din("xT", [NT, 128, KC, NE])
    NTP = 8
    cT = din("cT", [128, KC, NTP])
    valid_d = din("valid", [NT, 1, NE])
    invc_d = din("invc", [NT, 1, 4 * NM])
    nam_d = din("nam", [NT, 28, NM])
    swm_d = din("swm", [NT, 1, NM])
    rope_d = din("rope", [NT, 128, 2 * NM])
    ada_d = din("ada", [2, 48, 128, KC * 128])
    adab_d = din("adab", [128, 96])
    ng_d = din("ng", [128, 40])
    wa0_d = din("wa0", [12, 128, KC * 128])
    wv0_d = din("wv0", [128, KC * 512])
    wo0_d = din("wo0", [8, 128, KC * 128])
    wa1_d = din("wa1", [24, 128, KC * 128])
    wv1_d = din("wv1", [128, KC * 128])
    wo1_d = din("wo1", [8, 128, KC * 128])
    w1_d = din("w1", [2, FC, 128, KC * 128])
    w3_d = din("w3", [2, FC, 128, KC * 128])
    w2_d = din("w2", [2, 8, 128, FC * 128])
    poolw_d = din("poolw", [128, 512])
    small_d = din("small", [128, 32])
    ttab_d = din("ttab", [128, 8 * 16 * 64])
    cmat_d = din("cmat", [128, 6 * 128])
    ebig_d = din("ebig", [128, 14 * 128])
    yT = nc.dram_tensor("yT", [NT, 128, KC, NC], F32, kind="ExternalOutput").ap()

    P = Prog(nc)
    st = contextlib.ExitStack()
    off = [16576]
    KEYNAME = {"sq1": "sq", "sq2": "sq", "sq": "sq", "rstd": "rstd", "PT1": "PT1", "V1": "V1", "q1T": "q1T", "k1T": "k1T",
               "wv1": "wv1", "rec1": "rec1", "cat1": "cat1", "u1c": "u1c", "On1_all": "On1_all", "ring2": "ring2", "ystage": "ntmp", "PTL": "PT1", "ring": "ring2", "qst": "qst",
               "Vx": "Vx", "xs": "xs", "invg": "invc"}

    def sb(name, shape, dt):
        nbytes = int(np.prod(shape[1:])) * (4 if dt == F32 else 2)
        nbytes = (nbytes + 31) // 32 * 32
        t = nc.alloc_sbuf_tensor_at(name, list(shape), dt, offset=off[0])
        kn = KEYNAME.get(name.rstrip("0123456789"), name.rstrip("0123456789"))
        kn = KEYNAME.get(name, kn)
        P.register(kn, off[0], off[0] + nbytes)
        off[0] += nbytes
        return t

    ident = sb("ident", [128, 128], BF16)
    emat = sb("emat", [128, 128], BF16)
    onesb = sb("onesb", [128, 128], BF16)
    tri = sb("tri", [128, 384], BF16)
    ebig = sb("ebig", [128, 14 * 128], BF16)
    onesw = sb("onesw", [128, 512], BF16)
    ttab = sb("ttab", [128, 8 * 16 * 64], BF16)
    poolw = sb("poolw", [128, 512], BF16)
    small = sb("small", [128, 32], F32)
    esink = sb("esink", [128, 8], F32)
    adab = sb("adab", [128, 96], F32)
    ng = sb("ng", [128, 40], F32)
    epsb = sb("epsb", [128, 1], F32)
    cact = sb("cact", [128, KC * NTP], BF16)
    cin = sb("cin", [128, KC * NTP], F32)
    modT = sb("modT", [128, 96 * NTP], F32)
    gsb = sb("gsb", [128, 32], F32)
    bscr = sb("bscr", [128, 8], F32)
    _csig = sb("csig", [128, KC * 8], F32)
    sb_late = lambda name, shape, dt: _csig
    x = sb("x", [128, KC * NM], F32)
    hT = sb("hT", [128, KC * NE], BF16)
    rstd_off = off[0]
    rstdp = [sb("rstd%d" % i, [128, 512], F32) for i in range(2)]
    validb = sb("validb", [128, NE], BF16)
    ntmp_off = off[0]
    ntmp = [sb("ntmp%d" % i, [128, 512], F32) for i in range(2)]
    ringA = [sb("ringA%d" % i, [128, KC * 128], BF16) for i in range(4)]
    ring2 = [sb("ring2%d" % i, [128, FC * 128], BF16) for i in range(2)]
    _keep = off[0]
    off[0] = rstd_off
    qst0 = sb("qst0", [128, NM], BF16)
    assert off[0] <= rstd_off + 4096
    off[0] = ntmp_off
    qst1 = sb("qst1", [128, NM], BF16)
    assert off[0] <= ntmp_off + 4096
    off[0] = _keep
    qst = [qst0, qst1]
    ov0 = off[0]
    xs = [sb("xs%d" % i, [128, KC * 512], F32) for i in range(2)]
    sq = sb("sq", [128, KC * 512], BF16)
    end_l0a = off[0]
    off[0] = ov0
    qT = sb("qT", [128, 4 * NM], BF16)
    kT = sb("kT", [128, 4 * NE], BF16)
    Vx = sb("Vx", [128, 14 * 8 * 65], BF16)
    PT = [sb("PT%d" % i, [128, 768], BF16) for i in range(2)]
    rec = sb("rec", [128, 16], F32)
    namq = sb("namq", [128, NM], BF16)
    UO = 16
    NU = NM + 2 * UO
    u = sb("u", [128, 4 * NU], F32)
    ov2 = off[0]
    wv = sb("wv", [128, KC * 512], BF16)
    end_l0b = off[0]
    off[0] = ov2
    ptmp = [sb("ptmp%d" % i, [128, NU], F32) for i in range(2)]
    invg = sb("invg", [128, NM], F32)
    end_l0c = off[0]
    off[0] = ov2
    On_all = sb("On_all", [128, 10 * 512], BF16)
    PT = PT + [sb("PT%d" % i, [128, 768], BF16) for i in range(2, 7)]
    end_l0d = off[0]
    off[0] = ov0
    aT = sb("aT", [128, FC * NM], BF16)
    sq2 = sb("sq2", [128, KC * 512], BF16)
    sg = [sb("sg%d" % i, [128, 512], F32) for i in range(2)]
    end_ffn = off[0]
    off[0] = ov0
    yst = sb("yst", [128, 16 * 512], F32)
    assert off[0] <= 229344, off[0]
    end_ffn = max(end_ffn, off[0])
    off[0] = ov0
    q1T = sb("q1T", [128, 4 * NC], BF16)
    k1T = sb("k1T", [128, 2 * NM], BF16)
    V1 = sb("V1", [128, 10 * 2 * 65], BF16)
    wv1 = sb("wv1", [128, KC * 128], BF16)
    PT1 = [sb("PTL%d" % i, [128, 384], BF16) for i in range(4)]
    On1_all = sb("On1_all", [128, 8 * 512], BF16)
    rec1 = sb("rec1", [128, 16], F32)
    kmask = sb("kmask", [128, NM], BF16)
    rt = [sb("rt%d" % i, [128, 512], F32) for i in range(2)]
    cat1 = sb("cat1", [128, KC * NC], BF16)
    u1c = sb("u1c", [128, NM], F32)
    bgc = sb("bgc", [128, NM], F32)
    cgs = [sb("cgs%d" % i, [128, 512], F32) for i in range(2)]
    cv = sb("cv", [128, NC], F32)
    ov3 = off[0]
    sq1 = sb("sq1", [128, KC * 512], BF16)
    end_l1a = off[0]
    off[0] = ov3
    ropet = sb("ropet", [128, 2 * NM], F32)
    end_l1 = max(off[0], end_l1a)
    ystage = ntmp
    end_l0a = max(end_l0a, end_l0c, end_l0d)
    top = max(end_l0a, end_l0b, end_ffn, end_l1)
    assert top <= 229344, (top, end_l0a, end_l0b, end_ffn, end_l1)
    OVL = "ovl"

    ps = [st.enter_context(nc.psum_tensor("ps%d" % i, [128, 512], F32)) for i in range(7)]
    pst = st.enter_context(nc.psum_tensor("pst", [128, 512], F32))
    psi = [0]

    nrot = [5]

    def bank():
        i = psi[0] % nrot[0]
        psi[0] += 1
        return i

    evi = [0]

    def evac_eng():
        evi[0] += 1
        return "act" if evi[0] % 2 else "dve"

    def v3(t, a, b):
        return t[:].rearrange("p (a b) -> p a b", a=a, b=b)

    P.op("pool", lambda e: e.dma_start(out=ident[:], in_=cmat_d[:, 0:128]), writes=["ident"], dma=True, grp="c0")
    P.op("pool", lambda e: e.dma_start(out=emat[:], in_=cmat_d[:, 128:256]), writes=["emat"], dma=True, grp="c1")
    P.op("pool", lambda e: e.dma_start(out=onesb[:], in_=cmat_d[:, 256:384]), writes=["onesb"], dma=True, grp="c2")
    P.op("pool", lambda e: e.dma_start(out=tri[:], in_=cmat_d[:, 384:768]), writes=["tri"], dma=True, grp="c3")
    P.op("pool", lambda e: e.dma_start(out=ebig[:], in_=ebig_d), writes=["ebig"], dma=True, grp="c10")
    P.op("dve", lambda e: e.memset(onesw[:], 1.0), writes=["onesw"])
    P.op("pool", lambda e: e.dma_start(out=ttab[:], in_=ttab_d), writes=["ttab"], dma=True, grp="c4")
    P.op("pool", lambda e: e.dma_start(out=poolw[:], in_=poolw_d), writes=["poolw"], dma=True, grp="c5")
    P.op("sp", lambda e: e.dma_start(out=small[:], in_=small_d), writes=["small"], dma=True, grp="c6")
    P.op("sp", lambda e: e.dma_start(out=adab[:], in_=adab_d), writes=["adab"], dma=True, grp="c7")
    P.op("sp", lambda e: e.dma_start(out=ng[:], in_=ng_d), writes=["ng"], dma=True, grp="c8")
    P.op("sp", lambda e: e.dma_start(out=cin[:], in_=cT.rearrange("p a b -> p (a b)")), writes=["cin"], dma=True, grp="c9")
    P.op("dve", lambda e: e.memset(epsb[:], EPS), writes=["epsb"])
    P.op("act", lambda e: e.activation(out=esink[:], in_=small[:, 20:28], func=AF.Exp), reads=["small"], writes=["esink"])
    csig = sb_late("csig", [128, KC * 8], F32)
    P.op("act", lambda e: e.activation(out=csig[:], in_=cin[:], func=AF.Sigmoid), reads=["cin"], writes=["csig"])
    P.op("dve", lambda e: e.tensor_tensor(out=cact[:], in0=cin[:], in1=csig[:], op=ALU.mult), reads=["cin", "csig"], writes=["cact"])

    rings = {"ringA": ringA, "ring2": ring2}
    req = {"ringA": [], "ring2": []}
    issued = {"ringA": 0, "ring2": 0}

    def _issue(ringname, i, src):
        ring = rings[ringname]
        rb = ring[i % len(ring)]
        rk = (ringname, i % len(ring))
        P.op("pool", lambda e: e.dma_start(out=rb[:], in_=src), writes=[rk], dma=True, grp=rk)

    def get_tile(ringname, src, pending=0):
        ring = rings[ringname]
        nb = len(ring)
        k = len(req[ringname])
        req[ringname].append(src)
        plan = _plans[ringname] if _plans is not None else req[ringname]
        limit = min(len(plan) - 1, k - pending + nb - 1)
        while issued[ringname] <= limit:
            i = issued[ringname]
            _issue(ringname, i, plan[i])
            issued[ringname] += 1
        return k % nb

    cact3 = v3(cact, KC, NTP)
    for l in range(2):
        for j in range(48):
            sl_ = get_tile("ringA", ada_d[l, j])
            rb = ringA[sl_]
            rk = ("ringA", sl_)
            b = bank()
            rb3 = v3(rb, KC, 128)
            for kc in range(KC):
                P.op("pe", (lambda b, rb3, kc: lambda e: e.matmul(ps[b][:, 0:NTP], rb3[:, kc, :], cact3[:, kc, :],
                                                                     start=(kc == 0), stop=(kc == KC - 1)))(b, rb3, kc),
                     reads=[rk, "cact"], writes=[("ps", b)])
            P.op("act", (lambda b, l, j: lambda e: e.activation(
                out=modT[:, (l * 48 + j) * NTP:(l * 48 + j + 1) * NTP], in_=ps[b][:, 0:NTP], func=AF.Identity,
                bias=adab[:, l * 48 + j:l * 48 + j + 1], scale=1.0))(b, l, j),
                reads=[("ps", b), "adab"], writes=[("modT", l, j)])
    modT3 = v3(modT, 96, NTP)

    def mod_ap(l, j, t):
        return modT3[:, l * 48 + j, t:t + 1]


    def dense_A(tiles, rhs_fn, toks, evac_fn, kcn=KC, ring=None, ringname="ringA", two=False):
        ring = ring or ringA
        nb = len(ring)
        n = len(tiles)
        for j in range(n):
            sl_ = get_tile(ringname, tiles[j])
            rb3 = v3(ring[sl_], kcn, 128)
            rk = (ringname, sl_)
            for (t0, t1) in toks:
                b = bank()
                for kc in range(kcn):
                    rap, rkeys = rhs_fn(kc, t0, t1)
                    P.op("pe", (lambda b, rb3, kc, rap, t0, t1: lambda e: e.matmul(
                        ps[b][:, 0:t1 - t0], rb3[:, kc, :], rap, start=(kc == 0), stop=(kc == kcn - 1)))(b, rb3, kc, rap, t0, t1),
                        reads=[rk] + rkeys, writes=[("ps", b)])
                evac_fn(j, t0, t1, b)

    ntc = [0]

    def norm_mod(src3, s0, n, gs_c, sh_c, sqbuf, dst0, tile_reads, src_name, stream=None):
        sq3 = v3(sqbuf, KC, 512)
        h3 = v3(hT, KC, NE)
        for bi_, (t0, t1) in enumerate(tblocks(0, n)):
            w = t1 - t0
            if stream is not None:
                xb = xs[bi_ % 2]
                xb3 = v3(xb, KC, 512)
                P.op("sp", (lambda xb3, t0, t1: lambda e: e.dma_start(out=xb3[:, :, 0:t1 - t0], in_=stream[:, :, t0:t1]))(xb3, t0, t1),
                     writes=[("xs", bi_ % 2)], dma=True, grp=("xs", bi_ % 2))
                srcap = (lambda xb3: lambda c, t0, t1: xb3[:, c, 0:t1 - t0])(xb3)
                srck = (lambda k: lambda c, t0, t1: [k])(("xs", bi_ % 2))
            else:
                srcap = lambda c, t0, t1: src3[:, c, s0 + t0:s0 + t1]
                srck = lambda c, t0, t1: tk(src_name, c, s0 + t0, s0 + t1)
            b = bank()
            rstd = rstdp[bi_ % 2]
            rk_ = [("rstd", bi_ % 2)]
            for c in range(KC):
                P.op("act", (lambda c, t0, t1, srcap: lambda e: e.activation(out=sq3[:, c, 0:t1 - t0], in_=srcap(c, t0, t1),
                                                                      func=AF.Square))(c, t0, t1, srcap),
                     reads=srck(c, t0, t1) + tile_reads, writes=[("sq", c)])
                P.op("pe", (lambda c, b, w: lambda e: e.matmul(ps[b][:, 0:w], onesb[:], sq3[:, c, 0:w], start=(c == 0), stop=(c == KC - 1)))(c, b, w),
                     reads=[("sq", c), "onesb"], writes=[("ps", b)])
            P.op("act", (lambda b, w, rstd: lambda e: e.activation(out=rstd[:, 0:w], in_=ps[b][:, 0:w], func=AF.Sqrt,
                                                                  bias=epsb[:, 0:1], scale=1.0 / D))(b, w, rstd),
                 reads=[("ps", b), "epsb"], writes=rk_)
            P.op("dve", (lambda w, rstd: lambda e: e.reciprocal(out=rstd[:, 0:w], in_=rstd[:, 0:w]))(w, rstd),
                 reads=rk_, writes=rk_)
            temps = [(ntmp[0], ("ntmp", 0)), (ntmp[1], ("ntmp", 1)), (ps[5], ("ps", 5)), (ps[6], ("ps", 6)), (pst, ("ps", 7))]
            for c in range(KC):
                nt, ntk = temps[ntc[0] % 5]
                ntc[0] += 1
                P.op("dve", (lambda nt, c, t0, t1, srcap, rstd: lambda e: e.tensor_tensor(out=nt[:, 0:t1 - t0], in0=srcap(c, t0, t1),
                                                                             in1=rstd[:, 0:t1 - t0], op=ALU.mult))(nt, c, t0, t1, srcap, rstd),
                     reads=srck(c, t0, t1) + rk_ + tile_reads, writes=[ntk])
                P.op("act", (lambda nt, c, t0, t1: lambda e: e.activation(out=h3[:, c, dst0 + t0:dst0 + t1], in_=nt[:, 0:t1 - t0],
                                                                          func=AF.Identity, bias=sh_c(c), scale=gs_c(c)))(nt, c, t0, t1),
                     reads=[ntk, "gsb"] + tile_reads, writes=tk("hT", c, dst0 + t0, dst0 + t1))

    def ffn(l, t, ntok, x0):
        h3 = v3(hT, KC, NE)
        a3 = v3(aT, FC, NM)
        x3 = v3(x, KC, NM)
        toks = tblocks(0, ntok)
        n = FC
        nb = 4
        tl = []
        for f in range(FC):
            tl.append(w1_d[l, f])
            tl.append(w3_d[l, f])
        for f in range(FC):
            s1_ = get_tile("ringA", tl[2 * f])
            s3_ = get_tile("ringA", tl[2 * f + 1], pending=1)
            r1 = v3(ringA[s1_], KC, 128)
            r3 = v3(ringA[s3_], KC, 128)
            k1 = ("ringA", s1_)
            k3 = ("ringA", s3_)
            for (t0, t1) in toks:
                w = t1 - t0
                b1 = bank()
                b3 = bank()
                for (bb, rr, kk) in ((b1, r1, k1), (b3, r3, k3)):
                    for kc in range(KC):
                        P.op("pe", (lambda bb, rr, kc, t0, t1: lambda e: e.matmul(ps[bb][:, 0:t1 - t0], rr[:, kc, :], h3[:, kc, t0:t1],
                                                                                  start=(kc == 0), stop=(kc == KC - 1)))(bb, rr, kc, t0, t1),
                             reads=[kk] + tk("hT", kc, t0, t1), writes=[("ps", bb)])
                s = sg[f % 2]
                P.op("act", (lambda s, b1, w: lambda e: e.activation(out=s[:, 0:w], in_=ps[b1][:, 0:w], func=AF.Sigmoid))(s, b1, w),
                     reads=[("ps", b1), OVL], writes=[("sg", f % 2)])
                P.op("dve", (lambda s, b1, w: lambda e: e.tensor_tensor(out=s[:, 0:w], in0=s[:, 0:w], in1=ps[b1][:, 0:w], op=ALU.mult))(s, b1, w),
                     reads=[("ps", b1), ("sg", f % 2), OVL], writes=[("sg", f % 2)])
                P.op("dve", (lambda s, b3, f, t0, t1: lambda e: e.tensor_tensor(out=a3[:, f, t0:t1], in0=s[:, 0:t1 - t0], in1=ps[b3][:, 0:t1 - t0],
                                                                               op=ALU.mult))(s, b3, f, t0, t1),
                     reads=[("sg", f % 2), ("ps", b3), OVL], writes=tk("aT", f, t0, t1))

        def rhs_fn(kc, t0, t1):
            return a3[:, kc, t0:t1], tk("aT", kc, t0, t1)

        def evac(j, t0, t1, b):
            P.op("dve", lambda e: e.scalar_tensor_tensor(out=x3[:, j, x0 + t0:x0 + t1], in0=ps[b][:, 0:t1 - t0], scalar=mod_ap(l, 40 + j, t),
                                                         in1=x3[:, j, x0 + t0:x0 + t1], op0=ALU.mult, op1=ALU.add),
                 reads=[("ps", b), ("modT", l, 40 + j)] + tk("x", j, x0 + t0, x0 + t1), writes=tk("x", j, x0 + t0, x0 + t1))

        dense_A([w2_d[l, j] for j in range(8)], rhs_fn, toks, evac, kcn=FC, ring=ring2, ringname="ring2")

    def make_gs(l, which, t):
        sc0 = 8 if which == 0 else 32
        sh0 = 0 if which == 0 else 24
        col = (l * 2 + which) * 8
        sc = modT3[:, l * 48 + sc0:l * 48 + sc0 + 8, t]
        P.op("dve", lambda e: e.tensor_scalar(out=gsb[:, col:col + 8], in0=sc, scalar1=1.0, scalar2=None, op0=ALU.add),
             reads=[("modT", l, sc0 + i) for i in range(8)] + ["gsb_r"], writes=["gsb"])
        P.op("dve", lambda e: e.tensor_tensor(out=gsb[:, col:col + 8], in0=gsb[:, col:col + 8], in1=ng[:, col:col + 8], op=ALU.mult),
             reads=["gsb", "ng"], writes=["gsb"])
        return (lambda c: gsb[:, col + c:col + c + 1]), (lambda c: mod_ap(l, sh0 + c, t))

    def phase_switch():
        return

    CO1 = 128
    x3 = v3(x, KC, NM)
    h3 = v3(hT, KC, NE)
    xT4 = xT.rearrange("t p c n -> t p c n")

    def tile_body(t):
        phase_switch()
        P.op("pool", lambda e: e.dma_start(out=validb[:], in_=valid_d[t].partition_broadcast(128)),
             writes=["validb"], dma=True, grp="validb")
        if "mix0" in stages:
            gs_c, sh_c = make_gs(0, 0, t)
            norm_mod(None, 0, NE, gs_c, sh_c, sq, 0, [OVL], "xe", stream=xT4[t])
            for c in range(KC):
                P.op("sp", (lambda c: lambda e: e.dma_start(out=x3[:, c, :], in_=xT4[t, :, c, MID0:MID0 + NM]))(c),
                     writes=tk("x", c, 0, NM), dma=True, grp=("x", c))
            if stop_after == "h0":
                return
            q3 = v3(qT, 4, NM)
            k3 = v3(kT, 4, NE)
            u3 = v3(u, 4, NU)
            V4 = Vx[:].rearrange("p (b h d) -> p b h d", b=14, h=8, d=65)
            phase_switch()
            P.op("dve", lambda e: e.memset(v3(Vx, 14 * 8, 65)[:, :, 64:65], 1.0), writes=["Vx_ones", ("Vx", "ones")])

            def rhs_h(kc, t0, t1):
                return h3[:, kc, t0:t1], tk("hT", kc, t0, t1)

            P.op("pool", lambda e: e.dma_start(out=wv[:], in_=wv0_d), reads=[OVL], writes=["wv"], dma=True, grp="wv")

            def ev_q(j, t0, t1, b):
                P.op("act", lambda e: e.activation(out=q3[:, j, t0 - MID0:t1 - MID0], in_=ps[b][:, 0:t1 - t0], func=AF.Identity, scale=0.125),
                     reads=[("ps", b), OVL], writes=tk("qT", j, t0 - MID0, t1 - MID0))
            dense_A([wa0_d[j] for j in range(4)], rhs_h, tblocks(MID0, MID0 + NM), ev_q)

            def ev_k(j, t0, t1, b):
                en = evac_eng()
                if en == "act":
                    P.op("act", lambda e: e.activation(out=k3[:, j, t0:t1], in_=ps[b][:, 0:t1 - t0], func=AF.Copy),
                         reads=[("ps", b), OVL], writes=tk("kT", j, t0, t1))
                else:
                    P.op("dve", lambda e: e.tensor_copy(out=k3[:, j, t0:t1], in_=ps[b][:, 0:t1 - t0]),
                         reads=[("ps", b), OVL], writes=tk("kT", j, t0, t1))
            dense_A([wa0_d[4 + j] for j in range(4)], rhs_h, tblocks(0, NE), ev_k)
            U0 = MID0 - UO

            def ev_u(j, t0, t1, b):
                P.op("dve", lambda e: e.tensor_tensor(out=u3[:, j, t0 - U0:t1 - U0], in0=ps[b][:, 0:t1 - t0], in1=validb[:, t0:t1], op=ALU.mult),
                     reads=[("ps", b), "validb", OVL], writes=tk("u", j, t0 - U0, t1 - U0))
            dense_A([wa0_d[8 + j] for j in range(4)], rhs_h, tblocks(U0, U0 + NU), ev_u)
            wv3 = v3(wv, KC, 512)
            for bl in range(14):
                b = bank()
                for kc in range(KC):
                    P.op("pe", (lambda b, kc, bl: lambda e: e.matmul(ps[b][:, 0:512], h3[:, kc, bl * 128:(bl + 1) * 128], wv3[:, kc, :],
                                                                     start=(kc == 0), stop=(kc == KC - 1)))(b, kc, bl),
                         reads=["wv"] + tk("hT", kc, bl * 128, bl * 128 + 128), writes=[("ps", b)])
                P.op("act", (lambda b, bl: lambda e: e.activation(out=V4[:, bl, :, 0:64], in_=ps[b][:, 0:512].rearrange("p (h d) -> p h d", h=8, d=64),
                                                                  func=AF.Copy))(b, bl),
                     reads=[("ps", b), OVL, "Vx_ones"], writes=[("Vx", bl)])
            phase_switch()
            for g in range(4):
                P.op("pool", (lambda g: lambda e: e.dma_start(out=invg[:], in_=invc_d[t][:, g * NM:(g + 1) * NM].partition_broadcast(128)))(g),
                     writes=["invc"], dma=True, grp="invc")
                src = u3[:, g, :]
                cur = None
                steps = [(1, 0)] if g == 0 else None
                pa, pb = ptmp[0], ptmp[1]
                P.op("dve", (lambda src, pa: lambda e: e.tensor_tensor(out=pa[:, 1:NU], in0=src[:, 0:NU - 1], in1=src[:, 1:NU], op=ALU.add))(src, pa),
                     reads=tk("u", g, 0, NU) + [OVL], writes=[("ptmp", 0)])
                cur, other, lo, hi, sh = pa, pb, 1, NU, 1
                for _ in range(g):
                    nlo, nhi = lo + sh, hi - sh
                    P.op("dve", (lambda cur, other, nlo, nhi, sh: lambda e: e.tensor_tensor(
                        out=other[:, nlo:nhi], in0=cur[:, nlo - sh:nhi - sh], in1=cur[:, nlo + sh:nhi + sh], op=ALU.add))(cur, other, nlo, nhi, sh),
                        reads=[("ptmp", 0), ("ptmp", 1), OVL], writes=[("ptmp", 0), ("ptmp", 1)])
                    cur, other = other, cur
                    lo, hi, sh = nlo, nhi, sh * 2
                P.op("dve", (lambda cur, g: lambda e: e.tensor_tensor(out=cur[:, UO:UO + NM], in0=cur[:, UO:UO + NM], in1=invg[:], op=ALU.mult))(cur, g),
                     reads=[("ptmp", 0), ("ptmp", 1), "invc", OVL], writes=[("ptmp", 0), ("ptmp", 1)])
                P.op("dve", (lambda cur, g, src: lambda e: e.tensor_tensor(out=h3[:, 4 + g, 0:NM], in0=cur[:, UO:UO + NM], in1=src[:, UO:UO + NM], op=ALU.subtract))(cur, g, src),
                     reads=[("ptmp", 0), ("ptmp", 1), OVL] + tk("u", g, 0, NU), writes=tk("hT", 4 + g, 0, NM))
                pw3 = v3(poolw, 4, 128)
                for (t0, t1) in tblocks(0, NM):
                    b = bank()
                    P.op("pe", (lambda b, g, t0, t1: lambda e: e.matmul(ps[b][:, 0:t1 - t0], pw3[:, g, :], h3[:, 4 + g, t0:t1], start=True, stop=True))(b, g, t0, t1),
                         reads=["poolw"] + tk("hT", 4 + g, t0, t1), writes=[("ps", b)])
                    P.op("act", (lambda b, g, t0, t1: lambda e: e.activation(out=h3[:, 4 + g, t0:t1], in_=ps[b][:, 0:t1 - t0], func=AF.Identity,
                                                                            scale=small[:, g:g + 1]))(b, g, t0, t1),
                         reads=[("ps", b), "small"], writes=tk("hT", 4 + g, t0, t1))
            phase_switch()
            P.op("dve", lambda e: e.memset(namq[:], 0.0), writes=["namq"])
            P.op("pool", lambda e: e.dma_start(out=namq[0:28, :], in_=nam_d[t]), writes=["namq"], dma=True, grp="namq")
            P.op("dve", lambda e: e.memset(qst[0][64:128, :], 0.0), writes=[("qst", 0)])
            P.op("dve", lambda e: e.memset(qst[1][0:64, :], 0.0), writes=[("qst", 1)])
            tt4 = ttab[:].rearrange("p (h i q) -> p h i q", h=8, i=16, q=64)
            ttf = ttab[:].rearrange("p (h x) -> p h x", h=8, x=1024)
            Oa = ps[5][:, 0:455].rearrange("p (m d) -> p m d", m=7, d=65)
            Ob = ps[6][:, 0:195].rearrange("p (m d) -> p m d", m=3, d=65)
            ona4 = On_all[:].rearrange("p (m h d) -> p m h d", m=10, h=8, d=64)
            first_blk = {m: na_blocks(m)[0] for m in range(10)}
            last_blk = {m: na_blocks(m)[-1] for m in range(10)}
            for hd in range(8):
                c, po = hd // 2, (hd % 2) * 64
                qs = qst[hd % 2]
                P.op("act", (lambda qs, c, po: lambda e: e.activation(out=qs[po:po + 64, 0:NM], in_=q3[po:po + 64, c, :], func=AF.Copy))(qs, c, po),
                     reads=tk("qT", c, 0, NM), writes=[("qst", hd % 2)])
                for j in range(14):
                    pairs = [m for m in range(10) if j in na_blocks(m)]
                    m0, nq = pairs[0], len(pairs)
                    assert pairs == list(range(m0, m0 + nq))
                    n1 = min(nq, 4) * 128
                    n2 = (nq - 4) * 128 if nq > 4 else 0
                    bA = bank()
                    bB = bank() if n2 else None
                    pt = PT[j % 7]
                    ptk = ("PT", j % 7)
                    i0_ = 12 - 2 * j + 2 * m0
                    for (bb, qo, nn) in ((bA, 0, n1), (bB, 512, n2)):
                        if not nn:
                            continue
                        qa = m0 * 128 + qo
                        P.op("pe", (lambda bb, c, qs, j, qa, nn: lambda e: e.matmul(ps[bb][:, 0:nn], k3[:, c, j * 128:(j + 1) * 128],
                                                                                   qs[:, qa:qa + nn], start=True, stop=False))(bb, c, qs, j, qa, nn),
                             reads=tk("kT", c, j * 128, j * 128 + 128) + [("qst", hd % 2)], writes=[("ps", bb)])
                        xo = i0_ * 64 + qo
                        P.op("pe", (lambda bb, hd, xo, nn: lambda e: e.matmul(ps[bb][:, 0:nn], ident[:], ttf[:, hd, xo:xo + nn],
                                                                             start=False, stop=False))(bb, hd, xo, nn),
                             reads=["ident", "ttab"], writes=[("ps", bb)])
                        P.op("pe", (lambda bb, j, qa, nn: lambda e: e.matmul(ps[bb][:, 0:nn], ebig[:, j * 128:(j + 1) * 128], namq[:, qa:qa + nn],
                                                                            start=False, stop=True))(bb, j, qa, nn),
                             reads=["ebig", "namq"], writes=[("ps", bb)])
                        P.op("act", (lambda pt, bb, qo, nn: lambda e: e.activation(out=pt[:, qo:qo + nn], in_=ps[bb][:, 0:nn], func=AF.Exp))(pt, bb, qo, nn),
                             reads=[("ps", bb)], writes=[ptk + (qo // 512,)])
                    def emit_pv(jd, hd=hd):
                        for m in range(10):
                            if last_blk[m] != jd:
                                continue
                            ob, oc = (5, m * 65) if m < 7 else (6, (m - 7) * 65)
                            for jj in na_blocks(m):
                                mm0 = [mm for mm in range(10) if jj in na_blocks(mm)][0]
                                s_ = m - mm0
                                P.op("pe", (lambda ob, oc, s_, jj, hd, m: lambda e: e.matmul(ps[ob][:, oc:oc + 65], PT[jj % 7][:, s_ * 128:(s_ + 1) * 128], V4[:, jj, hd, :],
                                                                                           start=(jj == first_blk[m]), stop=(jj == last_blk[m])))(ob, oc, s_, jj, hd, m),
                                     reads=[("PT", jj % 7, 0), ("PT", jj % 7, 1), ("Vx", jj), "Vx_ones"], writes=[("ps", ob)])
                    if j >= 1:
                        emit_pv(j - 1)
                emit_pv(13)
                P.op("dve", lambda e: e.reciprocal(out=rec[:, 0:7], in_=Oa[:, :, 64]), reads=[("ps", 5)], writes=["rec"])
                P.op("dve", lambda e: e.reciprocal(out=rec[:, 7:10], in_=Ob[:, :, 64]), reads=[("ps", 6)], writes=["rec"])
                for m in range(10):
                    src = Oa[:, m, 0:64] if m < 7 else Ob[:, m - 7, 0:64]
                    en = "dve"
                    if en == "dve":
                        P.op("dve", (lambda src, m, hd: lambda e: e.tensor_scalar(out=ona4[:, m, hd, :], in0=src, scalar1=rec[:, m:m + 1], scalar2=None, op0=ALU.mult))(src, m, hd),
                             reads=[("ps", 5 if m < 7 else 6), "rec"], writes=[("On_all", m)])
                    else:
                        P.op("act", (lambda src, m, hd: lambda e: e.activation(out=ona4[:, m, hd, :], in_=src, func=AF.Identity, scale=rec[:, m:m + 1]))(src, m, hd),
                             reads=[("ps", 5 if m < 7 else 6), "rec"], writes=[("On_all", m)])
            for m in range(10):
                bt = bank()
                for cc in range(4):
                    P.op("pe", (lambda cc, bt, m: lambda e: e.matmul(ps[bt][:, cc * 128:(cc + 1) * 128], On_all[:, m * 512 + cc * 128:m * 512 + (cc + 1) * 128], ident[:],
                                                                    start=True, stop=True))(cc, bt, m),
                         reads=[("On_all", m), "ident"], writes=[("ps", bt)])
                P.op("act", (lambda bt, m: lambda e: e.activation(out=h3[:, 0:4, m * 128:(m + 1) * 128],
                                                                  in_=ps[bt][:, 0:512].rearrange("p (c n) -> p c n", c=4, n=128),
                                                                  func=AF.Copy))(bt, m),
                     reads=[("ps", bt)], writes=[("hT", c, m) for c in range(4)])
            def rhs_cat(kc, t0, t1):
                return h3[:, kc, t0:t1], tk("hT", kc, t0, t1)

            def ev_o0(j, t0, t1, b):
                P.op("dve", lambda e: e.scalar_tensor_tensor(out=x3[:, j, t0:t1], in0=ps[b][:, 0:t1 - t0], scalar=mod_ap(0, 16 + j, t),
                                                             in1=x3[:, j, t0:t1], op0=ALU.mult, op1=ALU.add),
                     reads=[("ps", b), ("modT", 0, 16 + j)] + tk("x", j, t0, t1), writes=tk("x", j, t0, t1))
            dense_A([wo0_d[j] for j in range(8)], rhs_cat, tblocks(0, NM), ev_o0)
            if stop_after == "mix0":
                return
        if "ffn0" in stages:
            phase_switch()
            gs_c, sh_c = make_gs(0, 1, t)
            norm_mod(x3, 0, NM, gs_c, sh_c, sq2, 0, [OVL], "x")
            ffn(0, t, NM, 0)
            if stop_after == "l0":
                return
        if "mix1" in stages:
            phase_switch()
            gs_c, sh_c = make_gs(1, 0, t)
            norm_mod(x3, 0, NM, gs_c, sh_c, sq1, 0, [OVL], "x")
            phase_switch()
            P.op("pool", lambda e: e.dma_start(out=ropet[:], in_=rope_d[t]), reads=[OVL], writes=["ropet"], dma=True, grp="ropet")
            P.op("pool", lambda e: e.dma_start(out=wv1[:], in_=wv1_d), reads=[OVL], writes=["wv1"], dma=True, grp="wv1")
            P.op("dve", lambda e: e.memset(v3(V1, 20, 65)[:, :, 64:65], 1.0), reads=[OVL], writes=["V1_ones", ("V1", "ones")])
            q13 = v3(q1T, 4, NC)
            k13 = v3(k1T, 2, NM)
            V14 = V1[:].rearrange("p (b g d) -> p b g d", b=10, g=2, d=65)
            c13 = v3(cat1, KC, NC)
            CK, SK = 0, NM
            CQ, SQ = CK, SK

            def rope_proj(tile_a, tile_b, toks, coff, soff, tbase, dst_fn, dst_keys_fn):
                sa = get_tile("ringA", tile_a)
                sb_ = get_tile("ringA", tile_b, pending=1)
                for (t0, t1) in toks:
                    w = t1 - t0
                    ba, bb = bank(), bank()
                    for (bk, sl) in ((ba, sa), (bb, sb_)):
                        r3 = v3(ringA[sl], KC, 128)
                        for kc in range(KC):
                            P.op("pe", (lambda bk, r3, kc, t0, t1: lambda e: e.matmul(ps[bk][:, 0:t1 - t0], r3[:, kc, :], h3[:, kc, t0:t1],
                                                                                      start=(kc == 0), stop=(kc == KC - 1)))(bk, r3, kc, t0, t1),
                                 reads=[("ringA", sl)] + tk("hT", kc, t0, t1), writes=[("ps", bk)])
                    ra, rb_ = rt[0], rt[1]
                    ka, kb = ("rt", 0), ("rt", 1)
                    rti[0] += 2
                    P.op("dve", (lambda ra, ba, t0, t1: lambda e: e.tensor_tensor(out=ra[:, 0:t1 - t0], in0=ps[ba][:, 0:t1 - t0],
                                                                                 in1=ropet[:, coff + t0 - tbase:coff + t1 - tbase], op=ALU.mult))(ra, ba, t0, t1),
                         reads=[("ps", ba), "ropet", OVL], writes=[ka])
                    P.op("dve", (lambda rb_, bb, t0, t1: lambda e: e.tensor_tensor(out=rb_[:, 0:t1 - t0], in0=ps[bb][:, 0:t1 - t0],
                                                                                  in1=ropet[:, soff + t0 - tbase:soff + t1 - tbase], op=ALU.mult))(rb_, bb, t0, t1),
                         reads=[("ps", bb), "ropet", OVL], writes=[kb])
                    P.op("dve", (lambda ra, rb_, t0, t1: lambda e: e.tensor_tensor(out=dst_fn(t0, t1), in0=ra[:, 0:t1 - t0], in1=rb_[:, 0:t1 - t0], op=ALU.add))(ra, rb_, t0, t1),
                         reads=[ka, kb, OVL], writes=dst_keys_fn(t0, t1))

            rope_slot = [0]
            nrot[0] = 7
            rti = [0]
            for c in range(4):
                rope_proj(wa1_d[c], wa1_d[4 + c], tblocks(CO1, CO1 + NC), CQ, SQ, 0,
                          (lambda c: lambda t0, t1: q13[:, c, t0 - CO1:t1 - CO1])(c),
                          (lambda c: lambda t0, t1: tk("q1T", c, t0 - CO1, t1 - CO1))(c))
            for g in range(2):
                rope_proj(wa1_d[8 + g], wa1_d[10 + g], tblocks(0, NM), CK, SK, 0,
                          (lambda g: lambda t0, t1: k13[:, g, t0:t1])(g),
                          (lambda g: lambda t0, t1: tk("k1T", g, t0, t1))(g))
            wv13 = v3(wv1, KC, 128)
            for bl in range(10):
                b = bank()
                for kc in range(KC):
                    P.op("pe", (lambda b, kc, bl: lambda e: e.matmul(ps[b][:, 0:128], h3[:, kc, bl * 128:(bl + 1) * 128], wv13[:, kc, :],
                                                                     start=(kc == 0), stop=(kc == KC - 1)))(b, kc, bl),
                         reads=["wv1"] + tk("hT", kc, bl * 128, bl * 128 + 128), writes=[("ps", b)])
                P.op("act", (lambda b, bl: lambda e: e.activation(out=V14[:, bl, :, 0:64], in_=ps[b][:, 0:128].rearrange("p (g d) -> p g d", g=2, d=64),
                                                                  func=AF.Copy))(b, bl),
                     reads=[("ps", b), OVL, "V1_ones"], writes=[("V1", bl)])
            for c in range(4):
                trip = [wa1_d[12 + c], wa1_d[16 + c], wa1_d[20 + c]]
                slots = []
                for pi_, tl_ in enumerate(trip):
                    sl = get_tile("ringA", tl_, pending=pi_)
                    slots.append(sl)
                for (t0, t1) in tblocks(0, NM):
                    w = t1 - t0
                    bks = [bank(), bank(), bank()]
                    for bk, sl in zip(bks, slots):
                        r3 = v3(ringA[sl], KC, 128)
                        for kc in range(KC):
                            P.op("pe", (lambda bk, r3, kc, t0, t1: lambda e: e.matmul(ps[bk][:, 0:t1 - t0], r3[:, kc, :], h3[:, kc, t0:t1],
                                                                                      start=(kc == 0), stop=(kc == KC - 1)))(bk, r3, kc, t0, t1),
                                 reads=[("ringA", sl)] + tk("hT", kc, t0, t1), writes=[("ps", bk)])
                    P.op("act", (lambda bk, t0, t1: lambda e: e.activation(out=bgc[:, t0:t1], in_=ps[bk][:, 0:t1 - t0], func=AF.Copy))(bks[0], t0, t1),
                         reads=[("ps", bks[0]), OVL], writes=tk("bgc", 0, t0, t1))
                    cb = cgs[(t0 // 512) % 2]
                    P.op("act", (lambda bk, cb, w: lambda e: e.activation(out=cb[:, 0:w], in_=ps[bk][:, 0:w], func=AF.Copy))(bks[1], cb, w),
                         reads=[("ps", bks[1]), OVL], writes=[("cgs", (t0 // 512) % 2)])
                    P.op("dve", (lambda bk, cb, t0, t1: lambda e: e.tensor_tensor(out=u1c[:, t0:t1], in0=cb[:, 0:t1 - t0], in1=ps[bk][:, 0:t1 - t0], op=ALU.mult))(bks[2], cb, t0, t1),
                         reads=[("ps", bks[2]), ("cgs", (t0 // 512) % 2), OVL], writes=tk("u1c", 0, t0, t1))
                    P.op("dve", (lambda t0, t1: lambda e: e.tensor_tensor(out=u1c[:, t0:t1], in0=u1c[:, t0:t1], in1=validb[:, MID0 + t0:MID0 + t1], op=ALU.mult))(t0, t1),
                         reads=tk("u1c", 0, t0, t1) + ["validb", OVL], writes=tk("u1c", 0, t0, t1))
                a0, a1 = CO1, CO1 + NC
                P.op("dve", (lambda c: lambda e: e.tensor_scalar(out=cv[:], in0=u1c[:, a0 - 1:a1 - 1], scalar1=small[:, 4 + c:5 + c], scalar2=small[:, 16 + c:17 + c],
                                                                  op0=ALU.mult, op1=ALU.add))(c),
                     reads=tk("u1c", 0, 0, NM) + ["small", OVL], writes=["cv"])
                P.op("dve", (lambda c: lambda e: e.scalar_tensor_tensor(out=cv[:], in0=u1c[:, a0:a1], scalar=small[:, 8 + c:9 + c], in1=cv[:], op0=ALU.mult, op1=ALU.add))(c),
                     reads=tk("u1c", 0, 0, NM) + ["small", "cv", OVL], writes=["cv"])
                P.op("dve", (lambda c: lambda e: e.scalar_tensor_tensor(out=cv[:], in0=u1c[:, a0 + 1:a1 + 1], scalar=small[:, 12 + c:13 + c], in1=cv[:], op0=ALU.mult, op1=ALU.add))(c),
                     reads=tk("u1c", 0, 0, NM) + ["small", "cv", OVL], writes=["cv"])
                P.op("dve", (lambda c: lambda e: e.tensor_tensor(out=c13[:, 4 + c, :], in0=cv[:], in1=bgc[:, a0:a1], op=ALU.mult))(c),
                     reads=["cv", OVL] + tk("bgc", 0, 0, NM), writes=tk("cat1", 4 + c, 0, NC))
            nrot[0] = 5
            P.op("dve", lambda e: e.memset(kmask[:], 0.0), writes=["kmask"])
            P.op("pool", lambda e: e.dma_start(out=kmask[0:1, :], in_=swm_d[t]), writes=["kmask"], dma=True, grp="kmask")
            P.op("dve", lambda e: e.memset(qst[0][64:128, :], 0.0), writes=[("qst", 0)])
            P.op("dve", lambda e: e.memset(qst[1][0:64, :], 0.0), writes=[("qst", 1)])
            O1a = ps[5][:, 0:455].rearrange("p (m d) -> p m d", m=7, d=65)
            O1b = ps[6][:, 0:65].rearrange("p (m d) -> p m d", m=1, d=65)
            on14 = On1_all[:].rearrange("p (m h d) -> p m h d", m=8, h=8, d=64)
            for hd in range(8):
                g, c, po = hd // 4, hd // 2, (hd % 2) * 64
                qs = qst[hd % 2]
                P.op("act", (lambda qs, c, po: lambda e: e.activation(out=qs[po:po + 64, 0:NC], in_=q13[po:po + 64, c, :], func=AF.Copy))(qs, c, po),
                     reads=tk("q1T", c, 0, NC), writes=[("qst", hd % 2)])
                for j in range(10):
                    qbs = [qb for qb in range(8) if qb <= j <= qb + 2]
                    qb0, nq = qbs[0], len(qbs)
                    nn = nq * 128
                    bS = bank()
                    pt = PT1[j % 4]
                    ptk = ("PT1", j % 4)
                    P.op("pe", (lambda bS, g, qs, j, qb0, nn: lambda e: e.matmul(ps[bS][:, 0:nn], k13[:, g, j * 128:(j + 1) * 128],
                                                                                   qs[:, qb0 * 128:qb0 * 128 + nn], start=True, stop=False))(bS, g, qs, j, qb0, nn),
                         reads=tk("k1T", g, j * 128, j * 128 + 128) + [("qst", hd % 2)], writes=[("ps", bS)])
                    to = (qb0 - (j - 2)) * 128
                    P.op("pe", (lambda bS, to, nn: lambda e: e.matmul(ps[bS][:, 0:nn], ident[:], tri[:, to:to + nn], start=False, stop=False))(bS, to, nn),
                         reads=["ident", "tri"], writes=[("ps", bS)])
                    P.op("pe", (lambda bS, j, nn: lambda e: e.matmul(ps[bS][:, 0:nn], kmask[:, j * 128:(j + 1) * 128], onesw[:, 0:nn],
                                                                    start=False, stop=True))(bS, j, nn),
                         reads=["kmask", "onesw"], writes=[("ps", bS)])
                    P.op("act", (lambda pt, bS, nn: lambda e: e.activation(out=pt[:, 0:nn], in_=ps[bS][:, 0:nn], func=AF.Exp, scale=0.125))(pt, bS, nn),
                         reads=[("ps", bS)], writes=[ptk])
                    def emit_pv1(qb, g=g):
                        ob, oc = (5, qb * 65) if qb < 7 else (6, 0)
                        for jj in (qb, qb + 1, qb + 2):
                            qb0_ = max(0, jj - 2)
                            s_ = qb - qb0_
                            P.op("pe", (lambda ob, oc, s_, jj, g, qb: lambda e: e.matmul(ps[ob][:, oc:oc + 65], PT1[jj % 4][:, s_ * 128:(s_ + 1) * 128], V14[:, jj, g, :],
                                                                                       start=(jj == qb), stop=(jj == qb + 2)))(ob, oc, s_, jj, g, qb),
                                 reads=[("PT1", jj % 4), ("V1", jj), "V1_ones"], writes=[("ps", ob)])
                    if j >= 3:
                        emit_pv1(j - 3)
                emit_pv1(7)
                if DBG < 3:
                    continue
                P.op("dve", (lambda hd: lambda e: e.tensor_scalar(out=rec1[:, 0:7], in0=O1a[:, :, 64], scalar1=esink[:, hd:hd + 1], scalar2=None, op0=ALU.add))(hd),
                     reads=[("ps", 5), "esink"], writes=["rec1"])
                P.op("dve", (lambda hd: lambda e: e.tensor_scalar(out=rec1[:, 7:8], in0=O1b[:, :, 64], scalar1=esink[:, hd:hd + 1], scalar2=None, op0=ALU.add))(hd),
                     reads=[("ps", 6), "esink"], writes=["rec1"])
                P.op("dve", lambda e: e.reciprocal(out=rec1[:, 0:8], in_=rec1[:, 0:8]), reads=["rec1"], writes=["rec1"])
                for qb in range(8):
                    src = O1a[:, qb, 0:64] if qb < 7 else O1b[:, 0, 0:64]
                    if True:
                        P.op("dve", (lambda src, qb, hd: lambda e: e.tensor_scalar(out=on14[:, qb, hd, :], in0=src, scalar1=rec1[:, qb:qb + 1], scalar2=None, op0=ALU.mult))(src, qb, hd),
                             reads=[("ps", 5 if qb < 7 else 6), "rec1"], writes=[("On1_all", qb)])
                    else:
                        P.op("act", (lambda src, qb, hd: lambda e: e.activation(out=on14[:, qb, hd, :], in_=src, func=AF.Identity, scale=rec1[:, qb:qb + 1]))(src, qb, hd),
                             reads=[("ps", 5 if qb < 7 else 6), "rec1"], writes=[("On1_all", qb)])
            for qb in range(8 if DBG >= 4 else 0):
                bt = bank()
                for cc in range(4):
                    P.op("pe", (lambda cc, bt, qb: lambda e: e.matmul(ps[bt][:, cc * 128:(cc + 1) * 128], On1_all[:, qb * 512 + cc * 128:qb * 512 + (cc + 1) * 128], ident[:],
                                                                     start=True, stop=True))(cc, bt, qb),
                         reads=[("On1_all", qb), "ident"], writes=[("ps", bt)])
                P.op("act", (lambda bt, qb: lambda e: e.activation(out=c13[:, 0:4, qb * 128:(qb + 1) * 128],
                                                                   in_=ps[bt][:, 0:512].rearrange("p (c n) -> p c n", c=4, n=128),
                                                                   func=AF.Copy))(bt, qb),
                     reads=[("ps", bt)], writes=[("cat1", c, qb) for c in range(4)])

            def rhs_cat1(kc, t0, t1):
                return c13[:, kc, t0:t1], tk("cat1", kc, t0, t1)

            def ev_o1(j, t0, t1, b):
                P.op("dve", lambda e: e.scalar_tensor_tensor(out=x3[:, j, CO1 + t0:CO1 + t1], in0=ps[b][:, 0:t1 - t0], scalar=mod_ap(1, 16 + j, t),
                                                             in1=x3[:, j, CO1 + t0:CO1 + t1], op0=ALU.mult, op1=ALU.add),
                     reads=[("ps", b), ("modT", 1, 16 + j), OVL] + tk("x", j, CO1 + t0, CO1 + t1), writes=tk("x", j, CO1 + t0, CO1 + t1))
            dense_A([wo1_d[j] for j in range(8)], rhs_cat1, tblocks(0, NC), ev_o1)
        if "ffn1" in stages:
            phase_switch()
            gs_c, sh_c = make_gs(1, 1, t)
            norm_mod(x3, CO1, NC, gs_c, sh_c, sq2, 0, [OVL], "x")
            ffn(1, t, NC, CO1)
        phase_switch()
        sq3 = v3(sq2, KC, 512)
        for bi_, (t0, t1) in enumerate(tblocks(0, NC)):
            w = t1 - t0
            b = bank()
            rstd = rstdp[bi_ % 2]
            rk_ = [("rstd", bi_ % 2)]
            for c in range(KC):
                P.op("act", (lambda c, t0, t1: lambda e: e.activation(out=sq3[:, c, 0:t1 - t0], in_=x3[:, c, CO1 + t0:CO1 + t1], func=AF.Square))(c, t0, t1),
                     reads=tk("x", c, CO1 + t0, CO1 + t1) + [OVL], writes=[("sq", c)])
                P.op("pe", (lambda c, b, w: lambda e: e.matmul(ps[b][:, 0:w], onesb[:], sq3[:, c, 0:w], start=(c == 0), stop=(c == KC - 1)))(c, b, w),
                     reads=[("sq", c), "onesb"], writes=[("ps", b)])
            P.op("act", (lambda b, w, rstd: lambda e: e.activation(out=rstd[:, 0:w], in_=ps[b][:, 0:w], func=AF.Sqrt, bias=epsb[:, 0:1], scale=1.0 / D))(b, w, rstd),
                 reads=[("ps", b), "epsb"], writes=rk_)
            P.op("dve", (lambda w, rstd: lambda e: e.reciprocal(out=rstd[:, 0:w], in_=rstd[:, 0:w]))(w, rstd),
                 reads=rk_, writes=rk_)
            for c in range(KC):
                yi = bi_ * KC + c
                ys = yst[:, yi * 512:(yi + 1) * 512]
                P.op("dve", (lambda ys, c, t0, t1, rstd: lambda e: e.scalar_tensor_tensor(out=ys[:, 0:t1 - t0], in0=x3[:, c, CO1 + t0:CO1 + t1], scalar=ng[:, 32 + c:33 + c],
                                                                                    in1=rstd[:, 0:t1 - t0], op0=ALU.mult, op1=ALU.mult))(ys, c, t0, t1, rstd),
                     reads=tk("x", c, CO1 + t0, CO1 + t1) + rk_ + ["ng", OVL], writes=[("yst", yi)])
                P.op("sp", (lambda ys, c, t0, t1: lambda e: e.dma_start(out=yT[t, :, c, t0:t1], in_=ys[:, 0:t1 - t0]))(ys, c, t0, t1),
                     reads=[("yst", yi)], dma=True, grp=("yst", yi))
    for t_ in range(NT):
        tile_body(t_)
    if "dbgmod" in stages:
        phase_switch()
        P.op("sp", lambda e: e.dma_start(out=yT[0, :, 0, 0:768], in_=modT[:, 0:768]), dma=True, grp="dbg")
    if stop_after is not None:
        pass
    if _record:
        st.close()
        return {k: list(v) for k, v in req.items()}
    P.emit()
    return nc, st


def _A(W):
    K, N = W.shape
    return np.ascontiguousarray(W.reshape(K // 128, 128, N // 128, 128).transpose(2, 1, 0, 3)).reshape(N // 128, 128, (K // 128) * 128)


def _B(W):
    K, N = W.shape
    return np.ascontiguousarray(W.reshape(K // 128, 128, N).transpose(1, 0, 2)).reshape(128, (K // 128) * N)


def _vecT(v):
    return np.ascontiguousarray(v.reshape(-1, 128).T)


def prep_shared(ada_w, ada_b, norm_g, final_g, ffn_w1, ffn_w3, ffn_w2, even_w_in, na_rpb, pool_w, pool_scale, even_w_out,
                odd_w_in, swa_sink, conv_w, conv_b, odd_w_out):
    f32 = np.float32
    sh = {}
    sh["ada"] = np.stack([_A(ada_w[l]) for l in range(2)])
    sh["adab"] = np.concatenate([_vecT(ada_b[0]), _vecT(ada_b[1])], axis=1).astype(f32)
    ngs = [_vecT(norm_g[0, 0]), _vecT(norm_g[0, 1]), _vecT(norm_g[1, 0]), _vecT(norm_g[1, 1]), _vecT(final_g)]
    sh["ng"] = np.ascontiguousarray(np.concatenate(ngs, axis=1)).astype(f32)
    wi = even_w_in[0]
    sh["wa0"] = _A(np.concatenate([wi[:, 0:512], wi[:, 512:1024], wi[:, 1536:2048]], axis=1))
    sh["wv0"] = _B(wi[:, 1024:1536])
    sh["wo0"] = _A(even_w_out[0])
    wo = odd_w_in[0]
    perm = np.concatenate([np.arange(h * 64, h * 64 + 64).reshape(2, 32)[::-1].reshape(-1) for h in range(8)])
    wq = wo[:, 0:512]
    wk = wo[:, 512:640]
    kperm = np.concatenate([np.arange(h * 64, h * 64 + 64).reshape(2, 32)[::-1].reshape(-1) for h in range(2)])
    wks = wk[:, kperm]
    kdup = np.concatenate([wk[:, 0:64], wk[:, 0:64], wk[:, 64:128], wk[:, 64:128]], axis=1)
    ksdup = np.concatenate([wks[:, 0:64], wks[:, 0:64], wks[:, 64:128], wks[:, 64:128]], axis=1)
    sh["wa1"] = _A(np.concatenate([wq, wq[:, perm], kdup, ksdup, wo[:, 768:1280], wo[:, 1280:1792], wo[:, 1792:2304]], axis=1))
    sh["wv1"] = _B(wo[:, 640:768])
    sh["wo1"] = _A(odd_w_out[0])
    sh["w1"] = np.stack([_A(ffn_w1[l]) for l in range(2)])
    sh["w3"] = np.stack([_A(ffn_w3[l]) for l in range(2)])
    sh["w2"] = np.stack([_A(ffn_w2[l]) for l in range(2)])
    sh["poolw"] = np.ascontiguousarray(pool_w[0].transpose(1, 0, 2)).reshape(128, 512)
    small = np.zeros((128, 32), f32)
    small[:, 0:4] = _vecT(pool_scale[0])
    for j in range(3):
        small[:, 4 + 4 * j:8 + 4 * j] = _vecT(conv_w[0, j])
    small[:, 16:20] = _vecT(conv_b[0])
    small[:, 20:28] = swa_sink[0][None, :]
    sh["small"] = small
    rpb_ext = np.concatenate([na_rpb[0].reshape(8, -1), np.full((8, 1), NEG, f32)], axis=1)
    p = np.arange(128)
    a = (p // 64)[:, None, None]
    kc = (p % 64)[:, None, None]
    i = np.arange(16)[None, :, None]
    qc = np.arange(64)[None, None, :]
    dr = a - i + 8
    cs = np.clip(qc - 8, 0, 48)
    ok = (np.abs(dr) <= 7) & (kc >= cs) & (kc < cs + 16)
    dc = np.clip(kc - qc, -15, 15) + 15
    idx = np.where(ok, (np.clip(dr, -7, 7) + 7) * 31 + dc, 15 * 31)
    sh["ttab"] = np.ascontiguousarray(rpb_ext[:, idx].transpose(1, 0, 2, 3)).reshape(128, 8 * 16 * 64).astype(f32)
    cm = np.zeros((128, 768), f32)
    cm[:, 0:128] = np.eye(128)
    cm[0, 128:192] = 1.0
    cm[1, 192:256] = 1.0
    cm[:, 256:384] = 1.0
    k_ = np.arange(128)[:, None]
    q_ = np.arange(128)[None, :]
    cm[:, 384:512] = np.where(k_ <= q_, 0.0, NEG)
    cm[:, 640:768] = np.where(k_ >= q_, 0.0, NEG)
    sh["cmat"] = cm
    eb = np.zeros((128, 14 * 128), f32)
    for j in range(14):
        for k in range(128):
            eb[2 * j + k // 64, j * 128 + k] = 1.0
    sh["ebig"] = eb
    return {k: np.ascontiguousarray(v, dtype=f32) for k, v in sh.items()}


def prep_tile(xseq, R):
    f32 = np.float32
    L = xseq.shape[0]
    rows = L // GW
    e0 = (R - 6) * GW
    xt = np.zeros((NE, D), f32)
    lo, hi = max(e0, 0), min(e0 + NE, L)
    xt[lo - e0:hi - e0] = xseq[lo:hi]
    xT = np.ascontiguousarray(xt.T.reshape(KC, 128, NE).transpose(1, 0, 2))
    pos_e = e0 + np.arange(NE)
    valid = ((pos_e >= 0) & (pos_e < L)).astype(f32)[None, :]
    pos_m = (R - 2) * GW + np.arange(NM)
    invc = np.ones((4, NM), f32)
    for g, w in enumerate((2, 4, 8, 16)):
        lo_ = np.clip(pos_m - w // 2, 0, L)
        hi_ = np.clip(pos_m - w // 2 + w, 0, L)
        cnt = np.maximum(hi_ - lo_, 1).astype(f32)
        invc[g] = (1.0 / cnt).astype(f32)
    nam = np.zeros((28, 20, 64), f32)
    for r in range(28):
        kr = R - 6 + r
        for qi in range(20):
            qr = R - 2 + qi
            if 0 <= qr < rows:
                w0 = min(max(qr - 4, 0), rows - 8)
            else:
                w0 = qr - 4
            nam[r, qi, :] = 0.0 if (w0 <= kr < w0 + 8) else NEG
    swm = np.where((pos_m >= 0) & (pos_m < L), 0.0, NEG).astype(f32)[None, :]
    inv = (1.0 / (np.float32(10000.0) ** (np.arange(0, 64, 2, dtype=f32) / np.float32(64)))).astype(f32)
    pp = np.arange(128)
    jj = pp % 32
    sgn = np.where((pp % 64) < 32, -1.0, 1.0).astype(f32)

    def tabs(pos, scale):
        ang = (pos.astype(f32)[None, :] * inv[jj][:, None]).astype(f32)
        return (np.cos(ang) * f32(scale)).astype(f32), (np.sin(ang) * sgn[:, None] * f32(scale)).astype(f32)
    ck, sk = tabs(pos_m, 1.0)
    rope = np.concatenate([ck, sk], axis=1)
    return dict(xT=xT, valid=valid, invc=invc.reshape(1, 4 * NM), nam=nam.reshape(28, NM), swm=swm, rope=rope)


def run_tiles(seqs, cs, shared, n_cores, NT, stop_after=None, stages=("mix0", "ffn0", "mix1", "ffn1")):
    tiles = []
    for si, xs in enumerate(seqs):
        for R in range(0, xs.shape[0] // GW, TR):
            tiles.append((si, R))
    assert len(tiles) == n_cores * NT, (len(tiles), n_cores, NT)
    nc, st = build_program(NT, stop_after, stages)
    in_maps = []
    for core in range(n_cores):
        mine = tiles[core * NT:(core + 1) * NT]
        per = [prep_tile(seqs[si], R) for (si, R) in mine]
        m = {k: np.ascontiguousarray(np.stack([p[k] for p in per])) for k in per[0]}
        cTa = np.zeros((128, KC, 8), np.float32)
        for ti_, (si, R) in enumerate(mine):
            cTa[:, :, ti_] = _vecT(cs[si])
        m["cT"] = cTa
        m.update(shared)
        in_maps.append(m)
    with st:
        res = run_bass_kernel_spmd(nc, in_maps, core_ids=list(range(n_cores)))
    outs = [np.zeros_like(xs) for xs in seqs]
    for core in range(n_cores):
        y = res.results[core]["yT"]
        for ti, (si, R) in enumerate(tiles[core * NT:(core + 1) * NT]):
            outs[si][R * GW:(R + TR) * GW] = y[ti].transpose(2, 1, 0).reshape(NC, D)
    return outs


def kernel(x_prompt, x_sample, c_prompt, c_sample, ada_w, ada_b, norm_g, final_g,
           ffn_w1, ffn_w3, ffn_w2, even_w_in, na_rpb, pool_w, pool_scale, even_w_out,
           odd_w_in, swa_sink, conv_w, conv_b, odd_w_out):
    a = lambda v: np.asarray(v, dtype=np.float32)
    shared = prep_shared(a(ada_w), a(ada_b), a(norm_g), a(final_g), a(ffn_w1), a(ffn_w3), a(ffn_w2), a(even_w_in), a(na_rpb),
                         a(pool_w), a(pool_scale), a(even_w_out), a(odd_w_in), a(swa_sink), a(conv_w), a(conv_b), a(odd_w_out))
    xp, xs = a(x_prompt), a(x_sample)
    cp, cs_ = a(c_prompt), a(c_sample)
    seqs = [xs[i] for i in range(xs.shape[0])] + [xp[i] for i in range(xp.shape[0])]
    cs = [cs_[i] for i in range(xs.shape[0])] + [cp[i] for i in range(xp.shape[0])]
    outs = run_tiles(seqs, cs, shared, 8, 6)
    ns = xs.shape[0]
    y_sample = np.stack(outs[:ns]).astype(np.float32)
    y_prompt = np.stack(outs[ns:]).astype(np.float32)
    return (y_prompt, y_sample)
```

```python
import contextlib
import os
import numpy as np
DBG = int(os.environ.get('KDBG', '9'))
import concourse.bass as bass
import concourse.mybir as mybir
from concourse.bass_utils import run_bass_kernel_spmd

F32 = mybir.dt.float32
BF16 = mybir.dt.bfloat16
AF = mybir.ActivationFunctionType
ALU = mybir.AluOpType

ENG_NAMES = ("pe", "act", "dve", "pool", "sp")
D = 1024
KC = 8
DFF = 2816
FC = 22
GW = 64
TR = 16
NE, NM, NC = 28 * 64, 20 * 64, 16 * 64
MID0, CORE0 = 4 * 64, 6 * 64
NEG = -30000.0
EPS = 1e-6


class _Op:
    __slots__ = ("eng", "fn", "reads", "writes", "dma", "grp", "deps", "sig", "cnt", "idx")


class Prog:
    def __init__(self, nc):
        self.nc = nc
        self.ops = []
        self.last_w = {}
        self.readers = {}
        self.bar = None
        self.since = []
        self.ranges = {}
        self.overl = None
        self.acc = {}

    def register(self, name, lo, hi):
        self.ranges.setdefault(name, []).append((lo, hi))
        self.overl = None

    def _overlaps(self):
        names = list(self.ranges)
        ov = {n: [] for n in names}
        for i, a in enumerate(names):
            for b in names[i + 1:]:
                hit = any(l1 < h2 and l2 < h1 for (l1, h1) in self.ranges[a] for (l2, h2) in self.ranges[b])
                if hit:
                    ov[a].append(b)
                    ov[b].append(a)
        self.overl = ov

    def _alias_deps(self, o, reads, writes):
        if self.overl is None:
            self._overlaps()
        rn = set((k if isinstance(k, str) else k[0]) for k in reads)
        wn = set((k if isinstance(k, str) else k[0]) for k in writes)
        deps = set()
        for n in wn:
            for c in self.overl.get(n, ()):
                a = self.acc.get(c)
                if a:
                    deps.update(a["r"].values()); deps.update(a["w"].values()); deps.update(a["rd"]); deps.update(a["wd"])
                    self.acc[c] = None
        for n in rn:
            if n in wn:
                continue
            for c in self.overl.get(n, ()):
                a = self.acc.get(c)
                if a:
                    deps.update(a["w"].values()); deps.update(a["wd"])
        for n, kind in [(n, "w") for n in wn] + [(n, "r") for n in rn if n not in wn]:
            if n not in self.ranges:
                continue
            a = self.acc.get(n)
            if not a:
                a = {"r": {}, "w": {}, "rd": set(), "wd": set()}
                self.acc[n] = a
            if o.dma:
                a[kind + "d"].add(o.idx)
            else:
                a[kind][o.eng] = o.idx
        return deps

    def barrier(self, fn):
        deps = set(self.since)
        if self.bar is not None:
            deps.add(self.bar)
        o = self.op("pool", fn)
        o.deps |= deps
        o.deps.discard(o.idx)
        self.bar = o.idx
        self.since = []
        return o

    def op(self, eng, fn, reads=(), writes=(), dma=False, grp=None):
        o = _Op()
        o.eng, o.fn, o.dma, o.grp = eng, fn, dma, grp
        o.idx = len(self.ops)
        deps = set()
        for r in reads:
            w = self.last_w.get(r)
            if w is not None:
                deps.add(w)
        for r in writes:
            w = self.last_w.get(r)
            if w is not None:
                deps.add(w)
            rl = self.readers.get(r)
            if rl:
                deps.update(rl)
        if self.bar is not None:
            deps.add(self.bar)
        deps |= self._alias_deps(o, reads, writes)
        deps.discard(o.idx)
        o.deps = deps
        self.since.append(o.idx)
        for r in reads:
            self.readers.setdefault(r, []).append(o.idx)
        for r in writes:
            self.last_w[r] = o.idx
            self.readers[r] = []
        self.ops.append(o)
        return o

    def emit(self):
        nc = self.nc
        ops = self.ops
        for o in ops:
            o.sig = bool(o.dma)
        for o in ops:
            for d in o.deps:
                p = ops[d]
                if p.dma or p.eng != o.eng or o.eng != "pe" or o.dma:
                    p.sig = True
        cnt = {}
        for o in ops:
            if not o.sig:
                continue
            key = ("dma", o.grp) if o.dma else ("eng", o.eng)
            cnt[key] = cnt.get(key, 0) + (16 if o.dma else 1)
            o.cnt = (key, cnt[key])
        self.final_counts = dict(cnt)
        stack = contextlib.ExitStack()
        sems = {}
        for i, k in enumerate(cnt.keys()):
            sems[k] = stack.enter_context(nc.semaphore("s%d" % i))
        know = {e: {} for e in ENG_NAMES}
        snap = {}
        streams = {e: [] for e in ENG_NAMES}
        for o in ops:
            kn = know[o.eng]
            waits = {}
            for d in sorted(o.deps):
                p = ops[d]
                if (not p.dma) and p.eng == o.eng and o.eng == "pe" and not o.dma:
                    continue
                key, val = p.cnt
                if kn.get(key, 0) >= val:
                    continue
                waits[key] = max(waits.get(key, 0), val)
                for k2, v2 in snap[d].items():
                    if kn.get(k2, 0) < v2:
                        kn[k2] = v2
                kn[key] = max(kn.get(key, 0), val)
            streams[o.eng].append((list(waits.items()), o))
            if o.sig:
                s = dict(kn)
                s[o.cnt[0]] = max(s.get(o.cnt[0], 0), o.cnt[1])
                snap[o.idx] = s
        with stack:
            with nc.Block() as block:
                def mk(ename):
                    def body(eng):
                        for wl, o in streams[ename]:
                            for k, v in wl:
                                eng.wait_ge(sems[k], v)
                            ins = o.fn(eng)
                            if o.sig:
                                ins.then_inc(sems[o.cnt[0]], 16 if o.dma else 1)
                        if ename == "sp":
                            for k, v in self.final_counts.items():
                                eng.wait_ge(sems[k], v)
                    return body
                block.tensor(mk("pe"))
                block.scalar(mk("act"))
                block.vector(mk("dve"))
                block.gpsimd(mk("pool"))
                block.sync(mk("sp"))


def tk(name, c, t0, t1):
    return [(name, c, b) for b in range(t0 // 128, (t1 - 1) // 128 + 1)]


def tblocks(t0, t1, step=512):
    out = []
    t = t0
    while t < t1:
        out.append((t, min(t + step, t1)))
        t += step
    return out


def na_blocks(m):
    if m == 1:
        return list(range(1, 7))
    if m == 8:
        return list(range(7, 13))
    return list(range(m, m + 5))


def build_program(NT, stop_after=None, stages=("mix0", "ffn0", "mix1", "ffn1"), _plans=None, _record=False):
    if _plans is None and not _record:
        _plans = build_program(NT, stop_after, stages, _record=True)
    nc = bass.Bass("TRN2", target_bir_lowering=False)

    def din(name, shape):
        return nc.dram_tensor(name, list(shape), F32, kind="ExternalInput").ap()

    xT = din("xT", [NT, 128, KC, NE])
    NTP = 8
    cT = din("cT", [128, KC, NTP])
    valid_d = din("valid", [NT, 1, NE])
    invc_d = din("invc", [NT, 1, 4 * NM])
    nam_d = din("nam", [NT, 28, NM])
    swm_d = din("swm", [NT, 1, NM])
    rope_d = din("rope", [NT, 128, 2 * NM])
    ada_d = din("ada", [2, 48, 128, KC * 128])
    adab_d = din("adab", [128, 96])
    ng_d = din("ng", [128, 40])
    wa0_d = din("wa0", [12, 128, KC * 128])
    wv0_d = din("wv0", [128, KC * 512])
    wo0_d = din("wo0", [8, 128, KC * 128])
    wa1_d = din("wa1", [24, 128, KC * 128])
    wv1_d = din("wv1", [128, KC * 128])
    wo1_d = din("wo1", [8, 128, KC * 128])
    w1_d = din("w1", [2, FC, 128, KC * 128])
    w3_d = din("w3", [2, FC, 128, KC * 128])
    w2_d = din("w2", [2, 8, 128, FC * 128])
    poolw_d = din("poolw", [128, 512])
    small_d = din("small", [128, 32])
    ttab_d = din("ttab", [128, 8 * 16 * 64])
    cmat_d = din("cmat", [128, 6 * 128])
    ebig_d = din("ebig", [128, 14 * 128])
    yT = nc.dram_tensor("yT", [NT, 128, KC, NC], F32, kind="ExternalOutput").ap()

    P = Prog(nc)
    st = contextlib.ExitStack()
    off = [16576]
    KEYNAME = {"sq1": "sq", "sq2": "sq", "sq": "sq", "rstd": "rstd", "PT1": "PT1", "V1": "V1", "q1T": "q1T", "k1T": "k1T",
               "wv1": "wv1", "rec1": "rec1", "cat1": "cat1", "u1c": "u1c", "On1_all": "On1_all", "ring2": "ring2", "ystage": "ntmp", "PTL": "PT1", "ring": "ring2", "qst": "qst",
               "Vx": "Vx", "xs": "xs", "invg": "invc"}

    def sb(name, shape, dt):
        nbytes = int(np.prod(shape[1:])) * (4 if dt == F32 else 2)
        nbytes = (nbytes + 31) // 32 * 32
        t = nc.alloc_sbuf_tensor_at(name, list(shape), dt, offset=off[0])
        kn = KEYNAME.get(name.rstrip("0123456789"), name.rstrip("0123456789"))
        kn = KEYNAME.get(name, kn)
        P.register(kn, off[0], off[0] + nbytes)
        off[0] += nbytes
        return t

    ident = sb("ident", [128, 128], BF16)
    emat = sb("emat", [128, 128], BF16)
    onesb = sb("onesb", [128, 128], BF16)
    tri = sb("tri", [128, 384], BF16)
    ebig = sb("ebig", [128, 14 * 128], BF16)
    onesw = sb("onesw", [128, 512], BF16)
    ttab = sb("ttab", [128, 8 * 16 * 64], BF16)
    poolw = sb("poolw", [128, 512], BF16)
    small = sb("small", [128, 32], F32)
    esink = sb("esink", [128, 8], F32)
    adab = sb("adab", [128, 96], F32)
    ng = sb("ng", [128, 40], F32)
    epsb = sb("epsb", [128, 1], F32)
    cact = sb("cact", [128, KC * NTP], BF16)
    cin = sb("cin", [128, KC * NTP], F32)
    modT = sb("modT", [128, 96 * NTP], F32)
    gsb = sb("gsb", [128, 32], F32)
    bscr = sb("bscr", [128, 8], F32)
    _csig = sb("csig", [128, KC * 8], F32)
    sb_late = lambda name, shape, dt: _csig
    x = sb("x", [128, KC * NM], F32)
    hT = sb("hT", [128, KC * NE], BF16)
    rstd_off = off[0]
    rstdp = [sb("rstd%d" % i, [128, 512], F32) for i in range(2)]
    validb = sb("validb", [128, NE], BF16)
    ntmp_off = off[0]
    ntmp = [sb("ntmp%d" % i, [128, 512], F32) for i in range(2)]
    ringA = [sb("ringA%d" % i, [128, KC * 128], BF16) for i in range(4)]
    ring2 = [sb("ring2%d" % i, [128, FC * 128], BF16) for i in range(2)]
    _keep = off[0]
    off[0] = rstd_off
    qst0 = sb("qst0", [128, NM], BF16)
    assert off[0] <= rstd_off + 4096
    off[0] = ntmp_off
    qst1 = sb("qst1", [128, NM], BF16)
    assert off[0] <= ntmp_off + 4096
    off[0] = _keep
    qst = [qst0, qst1]
    ov0 = off[0]
    xs = [sb("xs%d" % i, [128, KC * 512], F32) for i in range(2)]
    sq = sb("sq", [128, KC * 512], BF16)
    end_l0a = off[0]
    off[0] = ov0
    qT = sb("qT", [128, 4 * NM], BF16)
    kT = sb("kT", [128, 4 * NE], BF16)
    Vx = sb("Vx", [128, 14 * 8 * 65], BF16)
    PT = [sb("PT%d" % i, [128, 768], BF16) for i in range(2)]
    rec = sb("rec", [128, 16], F32)
    namq = sb("namq", [128, NM], BF16)
    UO = 16
    NU = NM + 2 * UO
    u = sb("u", [128, 4 * NU], F32)
    ov2 = off[0]
    wv = sb("wv", [128, KC * 512], BF16)
    end_l0b = off[0]
    off[0] = ov2
    ptmp = [sb("ptmp%d" % i, [128, NU], F32) for i in range(2)]
    invg = sb("invg", [128, NM], F32)
    end_l0c = off[0]
    off[0] = ov2
    On_all = sb("On_all", [128, 10 * 512], BF16)
    PT = PT + [sb("PT%d" % i, [128, 768], BF16) for i in range(2, 7)]
    end_l0d = off[0]
    off[0] = ov0
    aT = sb("aT", [128, FC * NM], BF16)
    sq2 = sb("sq2", [128, KC * 512], BF16)
    sg = [sb("sg%d" % i, [128, 512], F32) for i in range(2)]
    end_ffn = off[0]
    off[0] = ov0
    yst = sb("yst", [128, 16 * 512], F32)
    assert off[0] <= 229344, off[0]
    end_ffn = max(end_ffn, off[0])
    off[0] = ov0
    q1T = sb("q1T", [128, 4 * NC], BF16)
    k1T = sb("k1T", [128, 2 * NM], BF16)
    V1 = sb("V1", [128, 10 * 2 * 65], BF16)
    wv1 = sb("wv1", [128, KC * 128], BF16)
    PT1 = [sb("PTL%d" % i, [128, 384], BF16) for i in range(4)]
    On1_all = sb("On1_all", [128, 8 * 512], BF16)
    rec1 = sb("rec1", [128, 16], F32)
    kmask = sb("kmask", [128, NM], BF16)
    rt = [sb("rt%d" % i, [128, 512], F32) for i in range(2)]
    cat1 = sb("cat1", [128, KC * NC], BF16)
    u1c = sb("u1c", [128, NM], F32)
    bgc = sb("bgc", [128, NM], F32)
    cgs = [sb("cgs%d" % i, [128, 512], F32) for i in range(2)]
    cv = sb("cv", [128, NC], F32)
    ov3 = off[0]
    sq1 = sb("sq1", [128, KC * 512], BF16)
    end_l1a = off[0]
    off[0] = ov3
    ropet = sb("ropet", [128, 2 * NM], F32)
    end_l1 = max(off[0], end_l1a)
    ystage = ntmp
    end_l0a = max(end_l0a, end_l0c, end_l0d)
    top = max(end_l0a, end_l0b, end_ffn, end_l1)
    assert top <= 229344, (top, end_l0a, end_l0b, end_ffn, end_l1)
    OVL = "ovl"

    ps = [st.enter_context(nc.psum_tensor("ps%d" % i, [128, 512], F32)) for i in range(7)]
    pst = st.enter_context(nc.psum_tensor("pst", [128, 512], F32))
    psi = [0]

    def bank():
        i = psi[0] % 5
        psi[0] += 1
        return i

    evi = [0]

    def evac_eng():
        evi[0] += 1
        return "act" if evi[0] % 2 else "dve"

    def v3(t, a, b):
        return t[:].rearrange("p (a b) -> p a b", a=a, b=b)

    P.op("pool", lambda e: e.dma_start(out=ident[:], in_=cmat_d[:, 0:128]), writes=["ident"], dma=True, grp="c0")
    P.op("pool", lambda e: e.dma_start(out=emat[:], in_=cmat_d[:, 128:256]), writes=["emat"], dma=True, grp="c1")
    P.op("pool", lambda e: e.dma_start(out=onesb[:], in_=cmat_d[:, 256:384]), writes=["onesb"], dma=True, grp="c2")
    P.op("pool", lambda e: e.dma_start(out=tri[:], in_=cmat_d[:, 384:768]), writes=["tri"], dma=True, grp="c3")
    P.op("pool", lambda e: e.dma_start(out=ebig[:], in_=ebig_d), writes=["ebig"], dma=True, grp="c10")
    P.op("dve", lambda e: e.memset(onesw[:], 1.0), writes=["onesw"])
    P.op("pool", lambda e: e.dma_start(out=ttab[:], in_=ttab_d), writes=["ttab"], dma=True, grp="c4")
    P.op("pool", lambda e: e.dma_start(out=poolw[:], in_=poolw_d), writes=["poolw"], dma=True, grp="c5")
    P.op("sp", lambda e: e.dma_start(out=small[:], in_=small_d), writes=["small"], dma=True, grp="c6")
    P.op("sp", lambda e: e.dma_start(out=adab[:], in_=adab_d), writes=["adab"], dma=True, grp="c7")
    P.op("sp", lambda e: e.dma_start(out=ng[:], in_=ng_d), writes=["ng"], dma=True, grp="c8")
    P.op("sp", lambda e: e.dma_start(out=cin[:], in_=cT.rearrange("p a b -> p (a b)")), writes=["cin"], dma=True, grp="c9")
    P.op("dve", lambda e: e.memset(epsb[:], EPS), writes=["epsb"])
    P.op("act", lambda e: e.activation(out=esink[:], in_=small[:, 20:28], func=AF.Exp), reads=["small"], writes=["esink"])
    csig = sb_late("csig", [128, KC * 8], F32)
    P.op("act", lambda e: e.activation(out=csig[:], in_=cin[:], func=AF.Sigmoid), reads=["cin"], writes=["csig"])
    P.op("dve", lambda e: e.tensor_tensor(out=cact[:], in0=cin[:], in1=csig[:], op=ALU.mult), reads=["cin", "csig"], writes=["cact"])

    rings = {"ringA": ringA, "ring2": ring2}
    req = {"ringA": [], "ring2": []}
    issued = {"ringA": 0, "ring2": 0}

    def _issue(ringname, i, src):
        ring = rings[ringname]
        rb = ring[i % len(ring)]
        rk = (ringname, i % len(ring))
        P.op("pool", lambda e: e.dma_start(out=rb[:], in_=src), writes=[rk], dma=True, grp=rk)

    def get_tile(ringname, src, pending=0):
        ring = rings[ringname]
        nb = len(ring)
        k = len(req[ringname])
        req[ringname].append(src)
        plan = _plans[ringname] if _plans is not None else req[ringname]
        limit = min(len(plan) - 1, k - pending + nb - 1)
        while issued[ringname] <= limit:
            i = issued[ringname]
            _issue(ringname, i, plan[i])
            issued[ringname] += 1
        return k % nb

    cact3 = v3(cact, KC, NTP)
    for l in range(2):
        for j in range(48):
            sl_ = get_tile("ringA", ada_d[l, j])
            rb = ringA[sl_]
            rk = ("ringA", sl_)
            b = bank()
            rb3 = v3(rb, KC, 128)
            for kc in range(KC):
                P.op("pe", (lambda b, rb3, kc: lambda e: e.matmul(ps[b][:, 0:NTP], rb3[:, kc, :], cact3[:, kc, :],
                                                                     start=(kc == 0), stop=(kc == KC - 1)))(b, rb3, kc),
                     reads=[rk, "cact"], writes=[("ps", b)])
            P.op("act", (lambda b, l, j: lambda e: e.activation(
                out=modT[:, (l * 48 + j) * NTP:(l * 48 + j + 1) * NTP], in_=ps[b][:, 0:NTP], func=AF.Identity,
                bias=adab[:, l * 48 + j:l * 48 + j + 1], scale=1.0))(b, l, j),
                reads=[("ps", b), "adab"], writes=[("modT", l, j)])
    modT3 = v3(modT, 96, NTP)

    def mod_ap(l, j, t):
        return modT3[:, l * 48 + j, t:t + 1]


    def dense_A(tiles, rhs_fn, toks, evac_fn, kcn=KC, ring=None, ringname="ringA", two=False):
        ring = ring or ringA
        nb = len(ring)
        n = len(tiles)
        for j in range(n):
            sl_ = get_tile(ringname, tiles[j])
            rb3 = v3(ring[sl_], kcn, 128)
            rk = (ringname, sl_)
            for (t0, t1) in toks:
                b = bank()
                for kc in range(kcn):
                    rap, rkeys = rhs_fn(kc, t0, t1)
                    P.op("pe", (lambda b, rb3, kc, rap, t0, t1: lambda e: e.matmul(
                        ps[b][:, 0:t1 - t0], rb3[:, kc, :], rap, start=(kc == 0), stop=(kc == kcn - 1)))(b, rb3, kc, rap, t0, t1),
                        reads=[rk] + rkeys, writes=[("ps", b)])
                evac_fn(j, t0, t1, b)

    ntc = [0]

    def norm_mod(src3, s0, n, gs_c, sh_c, sqbuf, dst0, tile_reads, src_name, stream=None):
        sq3 = v3(sqbuf, KC, 512)
        h3 = v3(hT, KC, NE)
        for bi_, (t0, t1) in enumerate(tblocks(0, n)):
            w = t1 - t0
            if stream is not None:
                xb = xs[bi_ % 2]
                xb3 = v3(xb, KC, 512)
                P.op("sp", (lambda xb3, t0, t1: lambda e: e.dma_start(out=xb3[:, :, 0:t1 - t0], in_=stream[:, :, t0:t1]))(xb3, t0, t1),
                     writes=[("xs", bi_ % 2)], dma=True, grp=("xs", bi_ % 2))
                srcap = (lambda xb3: lambda c, t0, t1: xb3[:, c, 0:t1 - t0])(xb3)
                srck = (lambda k: lambda c, t0, t1: [k])(("xs", bi_ % 2))
            else:
                srcap = lambda c, t0, t1: src3[:, c, s0 + t0:s0 + t1]
                srck = lambda c, t0, t1: tk(src_name, c, s0 + t0, s0 + t1)
            b = bank()
            rstd = rstdp[bi_ % 2]
            rk_ = [("rstd", bi_ % 2)]
            for c in range(KC):
                P.op("act", (lambda c, t0, t1, srcap: lambda e: e.activation(out=sq3[:, c, 0:t1 - t0], in_=srcap(c, t0, t1),
                                                                      func=AF.Square))(c, t0, t1, srcap),
                     reads=srck(c, t0, t1) + tile_reads, writes=[("sq", c)])
                P.op("pe", (lambda c, b, w: lambda e: e.matmul(ps[b][:, 0:w], onesb[:], sq3[:, c, 0:w], start=(c == 0), stop=(c == KC - 1)))(c, b, w),
                     reads=[("sq", c), "onesb"], writes=[("ps", b)])
            P.op("act", (lambda b, w, rstd: lambda e: e.activation(out=rstd[:, 0:w], in_=ps[b][:, 0:w], func=AF.Sqrt,
                                                                  bias=epsb[:, 0:1], scale=1.0 / D))(b, w, rstd),
                 reads=[("ps", b), "epsb"], writes=rk_)
            P.op("dve", (lambda w, rstd: lambda e: e.reciprocal(out=rstd[:, 0:w], in_=rstd[:, 0:w]))(w, rstd),
                 reads=rk_, writes=rk_)
            temps = [(ntmp[0], ("ntmp", 0)), (ntmp[1], ("ntmp", 1)), (ps[5], ("ps", 5)), (ps[6], ("ps", 6)), (pst, ("ps", 7))]
            for c in range(KC):
                nt, ntk = temps[ntc[0] % 5]
                ntc[0] += 1
                P.op("dve", (lambda nt, c, t0, t1, srcap, rstd: lambda e: e.tensor_tensor(out=nt[:, 0:t1 - t0], in0=srcap(c, t0, t1),
                                                                             in1=rstd[:, 0:t1 - t0], op=ALU.mult))(nt, c, t0, t1, srcap, rstd),
                     reads=srck(c, t0, t1) + rk_ + tile_reads, writes=[ntk])
                P.op("act", (lambda nt, c, t0, t1: lambda e: e.activation(out=h3[:, c, dst0 + t0:dst0 + t1], in_=nt[:, 0:t1 - t0],
                                                                          func=AF.Identity, bias=sh_c(c), scale=gs_c(c)))(nt, c, t0, t1),
                     reads=[ntk, "gsb"] + tile_reads, writes=tk("hT", c, dst0 + t0, dst0 + t1))

    def ffn(l, t, ntok, x0):
        h3 = v3(hT, KC, NE)
        a3 = v3(aT, FC, NM)
        x3 = v3(x, KC, NM)
        toks = tblocks(0, ntok)
        n = FC
        nb = 4
        tl = []
        for f in range(FC):
            tl.append(w1_d[l, f])
            tl.append(w3_d[l, f])
        for f in range(FC):
            s1_ = get_tile("ringA", tl[2 * f])
            s3_ = get_tile("ringA", tl[2 * f + 1], pending=1)
            r1 = v3(ringA[s1_], KC, 128)
            r3 = v3(ringA[s3_], KC, 128)
            k1 = ("ringA", s1_)
            k3 = ("ringA", s3_)
            for (t0, t1) in toks:
                w = t1 - t0
                b1 = bank()
                b3 = bank()
                for (bb, rr, kk) in ((b1, r1, k1), (b3, r3, k3)):
                    for kc in range(KC):
                        P.op("pe", (lambda bb, rr, kc, t0, t1: lambda e: e.matmul(ps[bb][:, 0:t1 - t0], rr[:, kc, :], h3[:, kc, t0:t1],
                                                                                  start=(kc == 0), stop=(kc == KC - 1)))(bb, rr, kc, t0, t1),
                             reads=[kk] + tk("hT", kc, t0, t1), writes=[("ps", bb)])
                s = sg[f % 2]
                P.op("act", (lambda s, b1, w: lambda e: e.activation(out=s[:, 0:w], in_=ps[b1][:, 0:w], func=AF.Sigmoid))(s, b1, w),
                     reads=[("ps", b1), OVL], writes=[("sg", f % 2)])
                P.op("dve", (lambda s, b1, w: lambda e: e.tensor_tensor(out=s[:, 0:w], in0=s[:, 0:w], in1=ps[b1][:, 0:w], op=ALU.mult))(s, b1, w),
                     reads=[("ps", b1), ("sg", f % 2), OVL], writes=[("sg", f % 2)])
                P.op("dve", (lambda s, b3, f, t0, t1: lambda e: e.tensor_tensor(out=a3[:, f, t0:t1], in0=s[:, 0:t1 - t0], in1=ps[b3][:, 0:t1 - t0],
                                                                               op=ALU.mult))(s, b3, f, t0, t1),
                     reads=[("sg", f % 2), ("ps", b3), OVL], writes=tk("aT", f, t0, t1))

        def rhs_fn(kc, t0, t1):
            return a3[:, kc, t0:t1], tk("aT", kc, t0, t1)

        def evac(j, t0, t1, b):
            P.op("dve", lambda e: e.scalar_tensor_tensor(out=x3[:, j, x0 + t0:x0 + t1], in0=ps[b][:, 0:t1 - t0], scalar=mod_ap(l, 40 + j, t),
                                                         in1=x3[:, j, x0 + t0:x0 + t1], op0=ALU.mult, op1=ALU.add),
                 reads=[("ps", b), ("modT", l, 40 + j)] + tk("x", j, x0 + t0, x0 + t1), writes=tk("x", j, x0 + t0, x0 + t1))

        dense_A([w2_d[l, j] for j in range(8)], rhs_fn, toks, evac, kcn=FC, ring=ring2, ringname="ring2")

    def make_gs(l, which, t):
        sc0 = 8 if which == 0 else 32
        sh0 = 0 if which == 0 else 24
        col = (l * 2 + which) * 8
        sc = modT3[:, l * 48 + sc0:l * 48 + sc0 + 8, t]
        P.op("dve", lambda e: e.tensor_scalar(out=gsb[:, col:col + 8], in0=sc, scalar1=1.0, scalar2=None, op0=ALU.add),
             reads=[("modT", l, sc0 + i) for i in range(8)] + ["gsb_r"], writes=["gsb"])
        P.op("dve", lambda e: e.tensor_tensor(out=gsb[:, col:col + 8], in0=gsb[:, col:col + 8], in1=ng[:, col:col + 8], op=ALU.mult),
             reads=["gsb", "ng"], writes=["gsb"])
        return (lambda c: gsb[:, col + c:col + c + 1]), (lambda c: mod_ap(l, sh0 + c, t))

    def phase_switch():
        return

    CO1 = 128
    x3 = v3(x, KC, NM)
    h3 = v3(hT, KC, NE)
    xT4 = xT.rearrange("t p c n -> t p c n")

    def tile_body(t):
        phase_switch()
        P.op("pool", lambda e: e.dma_start(out=validb[:], in_=valid_d[t].partition_broadcast(128)),
             writes=["validb"], dma=True, grp="validb")
        if "mix0" in stages:
            gs_c, sh_c = make_gs(0, 0, t)
            norm_mod(None, 0, NE, gs_c, sh_c, sq, 0, [OVL], "xe", stream=xT4[t])
            for c in range(KC):
                P.op("sp", (lambda c: lambda e: e.dma_start(out=x3[:, c, :], in_=xT4[t, :, c, MID0:MID0 + NM]))(c),
                     writes=tk("x", c, 0, NM), dma=True, grp=("x", c))
            if stop_after == "h0":
                return
            q3 = v3(qT, 4, NM)
            k3 = v3(kT, 4, NE)
            u3 = v3(u, 4, NU)
            V4 = Vx[:].rearrange("p (b h d) -> p b h d", b=14, h=8, d=65)
            phase_switch()
            P.op("dve", lambda e: e.memset(v3(Vx, 14 * 8, 65)[:, :, 64:65], 1.0), writes=["Vx_ones", ("Vx", "ones")])

            def rhs_h(kc, t0, t1):
                return h3[:, kc, t0:t1], tk("hT", kc, t0, t1)

            P.op("pool", lambda e: e.dma_start(out=wv[:], in_=wv0_d), reads=[OVL], writes=["wv"], dma=True, grp="wv")

            def ev_q(j, t0, t1, b):
                P.op("act", lambda e: e.activation(out=q3[:, j, t0 - MID0:t1 - MID0], in_=ps[b][:, 0:t1 - t0], func=AF.Identity, scale=0.125),
                     reads=[("ps", b), OVL], writes=tk("qT", j, t0 - MID0, t1 - MID0))
            dense_A([wa0_d[j] for j in range(4)], rhs_h, tblocks(MID0, MID0 + NM), ev_q)

            def ev_k(j, t0, t1, b):
                en = evac_eng()
                if en == "act":
                    P.op("act", lambda e: e.activation(out=k3[:, j, t0:t1], in_=ps[b][:, 0:t1 - t0], func=AF.Copy),
                         reads=[("ps", b), OVL], writes=tk("kT", j, t0, t1))
                else:
                    P.op("dve", lambda e: e.tensor_copy(out=k3[:, j, t0:t1], in_=ps[b][:, 0:t1 - t0]),
                         reads=[("ps", b), OVL], writes=tk("kT", j, t0, t1))
            dense_A([wa0_d[4 + j] for j in range(4)], rhs_h, tblocks(0, NE), ev_k)
            U0 = MID0 - UO

            def ev_u(j, t0, t1, b):
                P.op("dve", lambda e: e.tensor_tensor(out=u3[:, j, t0 - U0:t1 - U0], in0=ps[b][:, 0:t1 - t0], in1=validb[:, t0:t1], op=ALU.mult),
                     reads=[("ps", b), "validb", OVL], writes=tk("u", j, t0 - U0, t1 - U0))
            dense_A([wa0_d[8 + j] for j in range(4)], rhs_h, tblocks(U0, U0 + NU), ev_u)
            wv3 = v3(wv, KC, 512)
            for bl in range(14):
                b = bank()
                for kc in range(KC):
                    P.op("pe", (lambda b, kc, bl: lambda e: e.matmul(ps[b][:, 0:512], h3[:, kc, bl * 128:(bl + 1) * 128], wv3[:, kc, :],
                                                                     start=(kc == 0), stop=(kc == KC - 1)))(b, kc, bl),
                         reads=["wv"] + tk("hT", kc, bl * 128, bl * 128 + 128), writes=[("ps", b)])
                P.op("act", (lambda b, bl: lambda e: e.activation(out=V4[:, bl, :, 0:64], in_=ps[b][:, 0:512].rearrange("p (h d) -> p h d", h=8, d=64),
                                                                  func=AF.Copy))(b, bl),
                     reads=[("ps", b), OVL, "Vx_ones"], writes=[("Vx", bl)])
            P.op("dve", lambda e: e.memset(namq[:], 0.0), writes=["namq"])
            P.op("pool", lambda e: e.dma_start(out=namq[0:28, :], in_=nam_d[t]), writes=["namq"], dma=True, grp="namq")
            P.op("dve", lambda e: e.memset(qst[0][64:128, :], 0.0), writes=[("qst", 0)])
            P.op("dve", lambda e: e.memset(qst[1][0:64, :], 0.0), writes=[("qst", 1)])
            phase_switch()
            for g in range(4):
                P.op("pool", (lambda g: lambda e: e.dma_start(out=invg[:], in_=invc_d[t][:, g * NM:(g + 1) * NM].partition_broadcast(128)))(g),
                     writes=["invc"], dma=True, grp="invc")
                src = u3[:, g, :]
                cur = None
                steps = [(1, 0)] if g == 0 else None
                pa, pb = ptmp[0], ptmp[1]
                P.op("dve", (lambda src, pa: lambda e: e.tensor_tensor(out=pa[:, 1:NU], in0=src[:, 0:NU - 1], in1=src[:, 1:NU], op=ALU.add))(src, pa),
                     reads=tk("u", g, 0, NU) + [OVL], writes=[("ptmp", 0)])
                cur, other, lo, hi, sh = pa, pb, 1, NU, 1
                for _ in range(g):
                    nlo, nhi = lo + sh, hi - sh
                    P.op("dve", (lambda cur, other, nlo, nhi, sh: lambda e: e.tensor_tensor(
                        out=other[:, nlo:nhi], in0=cur[:, nlo - sh:nhi - sh], in1=cur[:, nlo + sh:nhi + sh], op=ALU.add))(cur, other, nlo, nhi, sh),
                        reads=[("ptmp", 0), ("ptmp", 1), OVL], writes=[("ptmp", 0), ("ptmp", 1)])
                    cur, other = other, cur
                    lo, hi, sh = nlo, nhi, sh * 2
                P.op("dve", (lambda cur, g: lambda e: e.tensor_tensor(out=cur[:, UO:UO + NM], in0=cur[:, UO:UO + NM], in1=invg[:], op=ALU.mult))(cur, g),
                     reads=[("ptmp", 0), ("ptmp", 1), "invc", OVL], writes=[("ptmp", 0), ("ptmp", 1)])
                P.op("dve", (lambda cur, g, src: lambda e: e.tensor_tensor(out=h3[:, 4 + g, 0:NM], in0=cur[:, UO:UO + NM], in1=src[:, UO:UO + NM], op=ALU.subtract))(cur, g, src),
                     reads=[("ptmp", 0), ("ptmp", 1), OVL] + tk("u", g, 0, NU), writes=tk("hT", 4 + g, 0, NM))
                pw3 = v3(poolw, 4, 128)
                for (t0, t1) in tblocks(0, NM):
                    b = bank()
                    P.op("pe", (lambda b, g, t0, t1: lambda e: e.matmul(ps[b][:, 0:t1 - t0], pw3[:, g, :], h3[:, 4 + g, t0:t1], start=True, stop=True))(b, g, t0, t1),
                         reads=["poolw"] + tk("hT", 4 + g, t0, t1), writes=[("ps", b)])
                    P.op("act", (lambda b, g, t0, t1: lambda e: e.activation(out=h3[:, 4 + g, t0:t1], in_=ps[b][:, 0:t1 - t0], func=AF.Identity,
                                                                            scale=small[:, g:g + 1]))(b, g, t0, t1),
                         reads=[("ps", b), "small"], writes=tk("hT", 4 + g, t0, t1))
            phase_switch()
            tt4 = ttab[:].rearrange("p (h i q) -> p h i q", h=8, i=16, q=64)
            ttf = ttab[:].rearrange("p (h x) -> p h x", h=8, x=1024)
            Oa = ps[5][:, 0:455].rearrange("p (m d) -> p m d", m=7, d=65)
            Ob = ps[6][:, 0:195].rearrange("p (m d) -> p m d", m=3, d=65)
            ona4 = On_all[:].rearrange("p (m h d) -> p m h d", m=10, h=8, d=64)
            first_blk = {m: na_blocks(m)[0] for m in range(10)}
            last_blk = {m: na_blocks(m)[-1] for m in range(10)}
            for hd in range(8):
                c, po = hd // 2, (hd % 2) * 64
                qs = qst[hd % 2]
                P.op("act", (lambda qs, c, po: lambda e: e.activation(out=qs[po:po + 64, 0:NM], in_=q3[po:po + 64, c, :], func=AF.Copy))(qs, c, po),
                     reads=tk("qT", c, 0, NM), writes=[("qst", hd % 2)])
                for j in range(14):
                    pairs = [m for m in range(10) if j in na_blocks(m)]
                    m0, nq = pairs[0], len(pairs)
                    assert pairs == list(range(m0, m0 + nq))
                    n1 = min(nq, 4) * 128
                    n2 = (nq - 4) * 128 if nq > 4 else 0
                    bA = bank()
                    bB = bank() if n2 else None
                    pt = PT[j % 7]
                    ptk = ("PT", j % 7)
                    i0_ = 12 - 2 * j + 2 * m0
                    for (bb, qo, nn) in ((bA, 0, n1), (bB, 512, n2)):
                        if not nn:
                            continue
                        qa = m0 * 128 + qo
                        P.op("pe", (lambda bb, c, qs, j, qa, nn: lambda e: e.matmul(ps[bb][:, 0:nn], k3[:, c, j * 128:(j + 1) * 128],
                                                                                   qs[:, qa:qa + nn], start=True, stop=False))(bb, c, qs, j, qa, nn),
                             reads=tk("kT", c, j * 128, j * 128 + 128) + [("qst", hd % 2)], writes=[("ps", bb)])
                        xo = i0_ * 64 + qo
                        P.op("pe", (lambda bb, hd, xo, nn: lambda e: e.matmul(ps[bb][:, 0:nn], ident[:], ttf[:, hd, xo:xo + nn],
                                                                             start=False, stop=False))(bb, hd, xo, nn),
                             reads=["ident", "ttab"], writes=[("ps", bb)])
                        P.op("pe", (lambda bb, j, qa, nn: lambda e: e.matmul(ps[bb][:, 0:nn], ebig[:, j * 128:(j + 1) * 128], namq[:, qa:qa + nn],
                                                                            start=False, stop=True))(bb, j, qa, nn),
                             reads=["ebig", "namq"], writes=[("ps", bb)])
                        P.op("act", (lambda pt, bb, qo, nn: lambda e: e.activation(out=pt[:, qo:qo + nn], in_=ps[bb][:, 0:nn], func=AF.Exp))(pt, bb, qo, nn),
                             reads=[("ps", bb)], writes=[ptk + (qo // 512,)])
                    def emit_pv(jd, hd=hd):
                        for m in range(10):
                            if last_blk[m] != jd:
                                continue
                            ob, oc = (5, m * 65) if m < 7 else (6, (m - 7) * 65)
                            for jj in na_blocks(m):
                                mm0 = [mm for mm in range(10) if jj in na_blocks(mm)][0]
                                s_ = m - mm0
                                P.op("pe", (lambda ob, oc, s_, jj, hd, m: lambda e: e.matmul(ps[ob][:, oc:oc + 65], PT[jj % 7][:, s_ * 128:(s_ + 1) * 128], V4[:, jj, hd, :],
                                                                                           start=(jj == first_blk[m]), stop=(jj == last_blk[m])))(ob, oc, s_, jj, hd, m),
                                     reads=[("PT", jj % 7, 0), ("PT", jj % 7, 1), ("Vx", jj), "Vx_ones"], writes=[("ps", ob)])
                    if j >= 1:
                        emit_pv(j - 1)
                emit_pv(13)
                P.op("dve", lambda e: e.reciprocal(out=rec[:, 0:7], in_=Oa[:, :, 64]), reads=[("ps", 5)], writes=["rec"])
                P.op("dve", lambda e: e.reciprocal(out=rec[:, 7:10], in_=Ob[:, :, 64]), reads=[("ps", 6)], writes=["rec"])
                for m in range(10):
                    src = Oa[:, m, 0:64] if m < 7 else Ob[:, m - 7, 0:64]
                    en = "dve"
                    if en == "dve":
                        P.op("dve", (lambda src, m, hd: lambda e: e.tensor_scalar(out=ona4[:, m, hd, :], in0=src, scalar1=rec[:, m:m + 1], scalar2=None, op0=ALU.mult))(src, m, hd),
                             reads=[("ps", 5 if m < 7 else 6), "rec"], writes=[("On_all", m)])
                    else:
                        P.op("act", (lambda src, m, hd: lambda e: e.activation(out=ona4[:, m, hd, :], in_=src, func=AF.Identity, scale=rec[:, m:m + 1]))(src, m, hd),
                             reads=[("ps", 5 if m < 7 else 6), "rec"], writes=[("On_all", m)])
            for m in range(10):
                bt = bank()
                for cc in range(4):
                    P.op("pe", (lambda cc, bt, m: lambda e: e.matmul(ps[bt][:, cc * 128:(cc + 1) * 128], On_all[:, m * 512 + cc * 128:m * 512 + (cc + 1) * 128], ident[:],
                                                                    start=True, stop=True))(cc, bt, m),
                         reads=[("On_all", m), "ident"], writes=[("ps", bt)])
                P.op("act", (lambda bt, m: lambda e: e.activation(out=h3[:, 0:4, m * 128:(m + 1) * 128],
                                                                  in_=ps[bt][:, 0:512].rearrange("p (c n) -> p c n", c=4, n=128),
                                                                  func=AF.Copy))(bt, m),
                     reads=[("ps", bt)], writes=[("hT", c, m) for c in range(4)])
            def rhs_cat(kc, t0, t1):
                return h3[:, kc, t0:t1], tk("hT", kc, t0, t1)

            def ev_o0(j, t0, t1, b):
                P.op("dve", lambda e: e.scalar_tensor_tensor(out=x3[:, j, t0:t1], in0=ps[b][:, 0:t1 - t0], scalar=mod_ap(0, 16 + j, t),
                                                             in1=x3[:, j, t0:t1], op0=ALU.mult, op1=ALU.add),
                     reads=[("ps", b), ("modT", 0, 16 + j)] + tk("x", j, t0, t1), writes=tk("x", j, t0, t1))
            dense_A([wo0_d[j] for j in range(8)], rhs_cat, tblocks(0, NM), ev_o0)
            if stop_after == "mix0":
                return
        if "ffn0" in stages:
            phase_switch()
            gs_c, sh_c = make_gs(0, 1, t)
            norm_mod(x3, 0, NM, gs_c, sh_c, sq2, 0, [OVL], "x")
            ffn(0, t, NM, 0)
            if stop_after == "l0":
                return
        if "mix1" in stages:
            phase_switch()
            gs_c, sh_c = make_gs(1, 0, t)
            norm_mod(x3, 0, NM, gs_c, sh_c, sq1, 0, [OVL], "x")
            phase_switch()
            P.op("pool", lambda e: e.dma_start(out=ropet[:], in_=rope_d[t]), reads=[OVL], writes=["ropet"], dma=True, grp="ropet")
            P.op("pool", lambda e: e.dma_start(out=wv1[:], in_=wv1_d), reads=[OVL], writes=["wv1"], dma=True, grp="wv1")
            P.op("dve", lambda e: e.memset(v3(V1, 20, 65)[:, :, 64:65], 1.0), reads=[OVL], writes=["V1_ones", ("V1", "ones")])
            q13 = v3(q1T, 4, NC)
            k13 = v3(k1T, 2, NM)
            V14 = V1[:].rearrange("p (b g d) -> p b g d", b=10, g=2, d=65)
            c13 = v3(cat1, KC, NC)
            CK, SK = 0, NM
            CQ, SQ = CK, SK

            def rope_proj(tile_a, tile_b, toks, coff, soff, tbase, dst_fn, dst_keys_fn):
                sa = get_tile("ringA", tile_a)
                sb_ = get_tile("ringA", tile_b, pending=1)
                for (t0, t1) in toks:
                    w = t1 - t0
                    ba, bb = bank(), bank()
                    for (bk, sl) in ((ba, sa), (bb, sb_)):
                        r3 = v3(ringA[sl], KC, 128)
                        for kc in range(KC):
                            P.op("pe", (lambda bk, r3, kc, t0, t1: lambda e: e.matmul(ps[bk][:, 0:t1 - t0], r3[:, kc, :], h3[:, kc, t0:t1],
                                                                                      start=(kc == 0), stop=(kc == KC - 1)))(bk, r3, kc, t0, t1),
                                 reads=[("ringA", sl)] + tk("hT", kc, t0, t1), writes=[("ps", bk)])
                    ra, rb_ = rt[0], rt[1]
                    ka, kb = ("rt", 0), ("rt", 1)
                    rti[0] += 2
                    P.op("dve", (lambda ra, ba, t0, t1: lambda e: e.tensor_tensor(out=ra[:, 0:t1 - t0], in0=ps[ba][:, 0:t1 - t0],
                                                                                 in1=ropet[:, coff + t0 - tbase:coff + t1 - tbase], op=ALU.mult))(ra, ba, t0, t1),
                         reads=[("ps", ba), "ropet", OVL], writes=[ka])
                    P.op("dve", (lambda rb_, bb, t0, t1: lambda e: e.tensor_tensor(out=rb_[:, 0:t1 - t0], in0=ps[bb][:, 0:t1 - t0],
                                                                                  in1=ropet[:, soff + t0 - tbase:soff + t1 - tbase], op=ALU.mult))(rb_, bb, t0, t1),
                         reads=[("ps", bb), "ropet", OVL], writes=[kb])
                    P.op("dve", (lambda ra, rb_, t0, t1: lambda e: e.tensor_tensor(out=dst_fn(t0, t1), in0=ra[:, 0:t1 - t0], in1=rb_[:, 0:t1 - t0], op=ALU.add))(ra, rb_, t0, t1),
                         reads=[ka, kb, OVL], writes=dst_keys_fn(t0, t1))

            rope_slot = [0]
            rti = [0]
            for c in range(4):
                rope_proj(wa1_d[c], wa1_d[4 + c], tblocks(CO1, CO1 + NC), CQ, SQ, 0,
                          (lambda c: lambda t0, t1: q13[:, c, t0 - CO1:t1 - CO1])(c),
                          (lambda c: lambda t0, t1: tk("q1T", c, t0 - CO1, t1 - CO1))(c))
            for g in range(2):
                rope_proj(wa1_d[8 + g], wa1_d[10 + g], tblocks(0, NM), CK, SK, 0,
                          (lambda g: lambda t0, t1: k13[:, g, t0:t1])(g),
                          (lambda g: lambda t0, t1: tk("k1T", g, t0, t1))(g))
            wv13 = v3(wv1, KC, 128)
            for bl in range(10):
                b = bank()
                for kc in range(KC):
                    P.op("pe", (lambda b, kc, bl: lambda e: e.matmul(ps[b][:, 0:128], h3[:, kc, bl * 128:(bl + 1) * 128], wv13[:, kc, :],
                                                                     start=(kc == 0), stop=(kc == KC - 1)))(b, kc, bl),
                         reads=["wv1"] + tk("hT", kc, bl * 128, bl * 128 + 128), writes=[("ps", b)])
                P.op("act", (lambda b, bl: lambda e: e.activation(out=V14[:, bl, :, 0:64], in_=ps[b][:, 0:128].rearrange("p (g d) -> p g d", g=2, d=64),
                                                                  func=AF.Copy))(b, bl),
                     reads=[("ps", b), OVL, "V1_ones"], writes=[("V1", bl)])
            for c in range(4):
                trip = [wa1_d[12 + c], wa1_d[16 + c], wa1_d[20 + c]]
                slots = []
                for pi_, tl_ in enumerate(trip):
                    sl = get_tile("ringA", tl_, pending=pi_)
                    slots.append(sl)
                for (t0, t1) in tblocks(0, NM):
                    w = t1 - t0
                    bks = [bank(), bank(), bank()]
                    for bk, sl in zip(bks, slots):
                        r3 = v3(ringA[sl], KC, 128)
                        for kc in range(KC):
                            P.op("pe", (lambda bk, r3, kc, t0, t1: lambda e: e.matmul(ps[bk][:, 0:t1 - t0], r3[:, kc, :], h3[:, kc, t0:t1],
                                                                                      start=(kc == 0), stop=(kc == KC - 1)))(bk, r3, kc, t0, t1),
                                 reads=[("ringA", sl)] + tk("hT", kc, t0, t1), writes=[("ps", bk)])
                    P.op("act", (lambda bk, t0, t1: lambda e: e.activation(out=bgc[:, t0:t1], in_=ps[bk][:, 0:t1 - t0], func=AF.Copy))(bks[0], t0, t1),
                         reads=[("ps", bks[0]), OVL], writes=tk("bgc", 0, t0, t1))
                    cb = cgs[(t0 // 512) % 2]
                    P.op("act", (lambda bk, cb, w: lambda e: e.activation(out=cb[:, 0:w], in_=ps[bk][:, 0:w], func=AF.Copy))(bks[1], cb, w),
                         reads=[("ps", bks[1]), OVL], writes=[("cgs", (t0 // 512) % 2)])
                    P.op("dve", (lambda bk, cb, t0, t1: lambda e: e.tensor_tensor(out=u1c[:, t0:t1], in0=cb[:, 0:t1 - t0], in1=ps[bk][:, 0:t1 - t0], op=ALU.mult))(bks[2], cb, t0, t1),
                         reads=[("ps", bks[2]), ("cgs", (t0 // 512) % 2), OVL], writes=tk("u1c", 0, t0, t1))
                    P.op("dve", (lambda t0, t1: lambda e: e.tensor_tensor(out=u1c[:, t0:t1], in0=u1c[:, t0:t1], in1=validb[:, MID0 + t0:MID0 + t1], op=ALU.mult))(t0, t1),
                         reads=tk("u1c", 0, t0, t1) + ["validb", OVL], writes=tk("u1c", 0, t0, t1))
                a0, a1 = CO1, CO1 + NC
                P.op("dve", (lambda c: lambda e: e.tensor_scalar(out=cv[:], in0=u1c[:, a0 - 1:a1 - 1], scalar1=small[:, 4 + c:5 + c], scalar2=small[:, 16 + c:17 + c],
                                                                  op0=ALU.mult, op1=ALU.add))(c),
                     reads=tk("u1c", 0, 0, NM) + ["small", OVL], writes=["cv"])
                P.op("dve", (lambda c: lambda e: e.scalar_tensor_tensor(out=cv[:], in0=u1c[:, a0:a1], scalar=small[:, 8 + c:9 + c], in1=cv[:], op0=ALU.mult, op1=ALU.add))(c),
                     reads=tk("u1c", 0, 0, NM) + ["small", "cv", OVL], writes=["cv"])
                P.op("dve", (lambda c: lambda e: e.scalar_tensor_tensor(out=cv[:], in0=u1c[:, a0 + 1:a1 + 1], scalar=small[:, 12 + c:13 + c], in1=cv[:], op0=ALU.mult, op1=ALU.add))(c),
                     reads=tk("u1c", 0, 0, NM) + ["small", "cv", OVL], writes=["cv"])
                P.op("dve", (lambda c: lambda e: e.tensor_tensor(out=c13[:, 4 + c, :], in0=cv[:], in1=bgc[:, a0:a1], op=ALU.mult))(c),
                     reads=["cv", OVL] + tk("bgc", 0, 0, NM), writes=tk("cat1", 4 + c, 0, NC))
            P.op("dve", lambda e: e.memset(kmask[:], 0.0), writes=["kmask"])
            P.op("pool", lambda e: e.dma_start(out=kmask[0:1, :], in_=swm_d[t]), writes=["kmask"], dma=True, grp="kmask")
            P.op("dve", lambda e: e.memset(qst[0][64:128, :], 0.0), writes=[("qst", 0)])
            P.op("dve", lambda e: e.memset(qst[1][0:64, :], 0.0), writes=[("qst", 1)])
            O1a = ps[5][:, 0:455].rearrange("p (m d) -> p m d", m=7, d=65)
            O1b = ps[6][:, 0:65].rearrange("p (m d) -> p m d", m=1, d=65)
            on14 = On1_all[:].rearrange("p (m h d) -> p m h d", m=8, h=8, d=64)
            for hd in range(8):
                g, c, po = hd // 4, hd // 2, (hd % 2) * 64
                qs = qst[hd % 2]
                P.op("act", (lambda qs, c, po: lambda e: e.activation(out=qs[po:po + 64, 0:NC], in_=q13[po:po + 64, c, :], func=AF.Copy))(qs, c, po),
                     reads=tk("q1T", c, 0, NC), writes=[("qst", hd % 2)])
                for j in range(10):
                    qbs = [qb for qb in range(8) if qb <= j <= qb + 2]
                    qb0, nq = qbs[0], len(qbs)
                    nn = nq * 128
                    bS = bank()
                    pt = PT1[j % 4]
                    ptk = ("PT1", j % 4)
                    P.op("pe", (lambda bS, g, qs, j, qb0, nn: lambda e: e.matmul(ps[bS][:, 0:nn], k13[:, g, j * 128:(j + 1) * 128],
                                                                                   qs[:, qb0 * 128:qb0 * 128 + nn], start=True, stop=False))(bS, g, qs, j, qb0, nn),
                         reads=tk("k1T", g, j * 128, j * 128 + 128) + [("qst", hd % 2)], writes=[("ps", bS)])
                    to = (qb0 - (j - 2)) * 128
                    P.op("pe", (lambda bS, to, nn: lambda e: e.matmul(ps[bS][:, 0:nn], ident[:], tri[:, to:to + nn], start=False, stop=False))(bS, to, nn),
                         reads=["ident", "tri"], writes=[("ps", bS)])
                    P.op("pe", (lambda bS, j, nn: lambda e: e.matmul(ps[bS][:, 0:nn], kmask[:, j * 128:(j + 1) * 128], onesw[:, 0:nn],
                                                                    start=False, stop=True))(bS, j, nn),
                         reads=["kmask", "onesw"], writes=[("ps", bS)])
                    P.op("act", (lambda pt, bS, nn: lambda e: e.activation(out=pt[:, 0:nn], in_=ps[bS][:, 0:nn], func=AF.Exp, scale=0.125))(pt, bS, nn),
                         reads=[("ps", bS)], writes=[ptk])
                    def emit_pv1(qb, g=g):
                        ob, oc = (5, qb * 65) if qb < 7 else (6, 0)
                        for jj in (qb, qb + 1, qb + 2):
                            qb0_ = max(0, jj - 2)
                            s_ = qb - qb0_
                            P.op("pe", (lambda ob, oc, s_, jj, g, qb: lambda e: e.matmul(ps[ob][:, oc:oc + 65], PT1[jj % 4][:, s_ * 128:(s_ + 1) * 128], V14[:, jj, g, :],
                                                                                       start=(jj == qb), stop=(jj == qb + 2)))(ob, oc, s_, jj, g, qb),
                                 reads=[("PT1", jj % 4), ("V1", jj), "V1_ones"], writes=[("ps", ob)])
                    if j >= 3:
                        emit_pv1(j - 3)
                emit_pv1(7)
                if DBG < 3:
                    continue
                P.op("dve", (lambda hd: lambda e: e.tensor_scalar(out=rec1[:, 0:7], in0=O1a[:, :, 64], scalar1=esink[:, hd:hd + 1], scalar2=None, op0=ALU.add))(hd),
                     reads=[("ps", 5), "esink"], writes=["rec1"])
                P.op("dve", (lambda hd: lambda e: e.tensor_scalar(out=rec1[:, 7:8], in0=O1b[:, :, 64], scalar1=esink[:, hd:hd + 1], scalar2=None, op0=ALU.add))(hd),
                     reads=[("ps", 6), "esink"], writes=["rec1"])
                P.op("dve", lambda e: e.reciprocal(out=rec1[:, 0:8], in_=rec1[:, 0:8]), reads=["rec1"], writes=["rec1"])
                for qb in range(8):
                    src = O1a[:, qb, 0:64] if qb < 7 else O1b[:, 0, 0:64]
                    if True:
                        P.op("dve", (lambda src, qb, hd: lambda e: e.tensor_scalar(out=on14[:, qb, hd, :], in0=src, scalar1=rec1[:, qb:qb + 1], scalar2=None, op0=ALU.mult))(src, qb, hd),
                             reads=[("ps", 5 if qb < 7 else 6), "rec1"], writes=[("On1_all", qb)])
                    else:
                        P.op("act", (lambda src, qb, hd: lambda e: e.activation(out=on14[:, qb, hd, :], in_=src, func=AF.Identity, scale=rec1[:, qb:qb + 1]))(src, qb, hd),
                             reads=[("ps", 5 if qb < 7 else 6), "rec1"], writes=[("On1_all", qb)])
            for qb in range(8 if DBG >= 4 else 0):
                bt = bank()
                for cc in range(4):
                    P.op("pe", (lambda cc, bt, qb: lambda e: e.matmul(ps[bt][:, cc * 128:(cc + 1) * 128], On1_all[:, qb * 512 + cc * 128:qb * 512 + (cc + 1) * 128], ident[:],
                                                                     start=True, stop=True))(cc, bt, qb),
                         reads=[("On1_all", qb), "ident"], writes=[("ps", bt)])
                P.op("act", (lambda bt, qb: lambda e: e.activation(out=c13[:, 0:4, qb * 128:(qb + 1) * 128],
                                                                   in_=ps[bt][:, 0:512].rearrange("p (c n) -> p c n", c=4, n=128),
                                                                   func=AF.Copy))(bt, qb),
                     reads=[("ps", bt)], writes=[("cat1", c, qb) for c in range(4)])

            def rhs_cat1(kc, t0, t1):
                return c13[:, kc, t0:t1], tk("cat1", kc, t0, t1)

            def ev_o1(j, t0, t1, b):
                P.op("dve", lambda e: e.scalar_tensor_tensor(out=x3[:, j, CO1 + t0:CO1 + t1], in0=ps[b][:, 0:t1 - t0], scalar=mod_ap(1, 16 + j, t),
                                                             in1=x3[:, j, CO1 + t0:CO1 + t1], op0=ALU.mult, op1=ALU.add),
                     reads=[("ps", b), ("modT", 1, 16 + j), OVL] + tk("x", j, CO1 + t0, CO1 + t1), writes=tk("x", j, CO1 + t0, CO1 + t1))
            dense_A([wo1_d[j] for j in range(8)], rhs_cat1, tblocks(0, NC), ev_o1)
        if "ffn1" in stages:
            phase_switch()
            gs_c, sh_c = make_gs(1, 1, t)
            norm_mod(x3, CO1, NC, gs_c, sh_c, sq2, 0, [OVL], "x")
            ffn(1, t, NC, CO1)
        phase_switch()
        sq3 = v3(sq2, KC, 512)
        for bi_, (t0, t1) in enumerate(tblocks(0, NC)):
            w = t1 - t0
            b = bank()
            rstd = rstdp[bi_ % 2]
            rk_ = [("rstd", bi_ % 2)]
            for c in range(KC):
                P.op("act", (lambda c, t0, t1: lambda e: e.activation(out=sq3[:, c, 0:t1 - t0], in_=x3[:, c, CO1 + t0:CO1 + t1], func=AF.Square))(c, t0, t1),
                     reads=tk("x", c, CO1 + t0, CO1 + t1) + [OVL], writes=[("sq", c)])
                P.op("pe", (lambda c, b, w: lambda e: e.matmul(ps[b][:, 0:w], onesb[:], sq3[:, c, 0:w], start=(c == 0), stop=(c == KC - 1)))(c, b, w),
                     reads=[("sq", c), "onesb"], writes=[("ps", b)])
            P.op("act", (lambda b, w, rstd: lambda e: e.activation(out=rstd[:, 0:w], in_=ps[b][:, 0:w], func=AF.Sqrt, bias=epsb[:, 0:1], scale=1.0 / D))(b, w, rstd),
                 reads=[("ps", b), "epsb"], writes=rk_)
            P.op("dve", (lambda w, rstd: lambda e: e.reciprocal(out=rstd[:, 0:w], in_=rstd[:, 0:w]))(w, rstd),
                 reads=rk_, writes=rk_)
            for c in range(KC):
                yi = bi_ * KC + c
                ys = yst[:, yi * 512:(yi + 1) * 512]
                P.op("dve", (lambda ys, c, t0, t1, rstd: lambda e: e.scalar_tensor_tensor(out=ys[:, 0:t1 - t0], in0=x3[:, c, CO1 + t0:CO1 + t1], scalar=ng[:, 32 + c:33 + c],
                                                                                    in1=rstd[:, 0:t1 - t0], op0=ALU.mult, op1=ALU.mult))(ys, c, t0, t1, rstd),
                     reads=tk("x", c, CO1 + t0, CO1 + t1) + rk_ + ["ng", OVL], writes=[("yst", yi)])
                P.op("sp", (lambda ys, c, t0, t1: lambda e: e.dma_start(out=yT[t, :, c, t0:t1], in_=ys[:, 0:t1 - t0]))(ys, c, t0, t1),
                     reads=[("yst", yi)], dma=True, grp=("yst", yi))
    for t_ in range(NT):
        tile_body(t_)
    if "dbgmod" in stages:
        phase_switch()
        P.op("sp", lambda e: e.dma_start(out=yT[0, :, 0, 0:768], in_=modT[:, 0:768]), dma=True, grp="dbg")
    if stop_after is not None:
        pass
    if _record:
        st.close()
        return {k: list(v) for k, v in req.items()}
    P.emit()
    return nc, st


def _A(W):
    K, N = W.shape
    return np.ascontiguousarray(W.reshape(K // 128, 128, N // 128, 128).transpose(2, 1, 0, 3)).reshape(N // 128, 128, (K // 128) * 128)


def _B(W):
    K, N = W.shape
    return np.ascontiguousarray(W.reshape(K // 128, 128, N).transpose(1, 0, 2)).reshape(128, (K // 128) * N)


def _vecT(v):
    return np.ascontiguousarray(v.reshape(-1, 128).T)


def prep_shared(ada_w, ada_b, norm_g, final_g, ffn_w1, ffn_w3, ffn_w2, even_w_in, na_rpb, pool_w, pool_scale, even_w_out,
                odd_w_in, swa_sink, conv_w, conv_b, odd_w_out):
    f32 = np.float32
    sh = {}
    sh["ada"] = np.stack([_A(ada_w[l]) for l in range(2)])
    sh["adab"] = np.concatenate([_vecT(ada_b[0]), _vecT(ada_b[1])], axis=1).astype(f32)
    ngs = [_vecT(norm_g[0, 0]), _vecT(norm_g[0, 1]), _vecT(norm_g[1, 0]), _vecT(norm_g[1, 1]), _vecT(final_g)]
    sh["ng"] = np.ascontiguousarray(np.concatenate(ngs, axis=1)).astype(f32)
    wi = even_w_in[0]
    sh["wa0"] = _A(np.concatenate([wi[:, 0:512], wi[:, 512:1024], wi[:, 1536:2048]], axis=1))
    sh["wv0"] = _B(wi[:, 1024:1536])
    sh["wo0"] = _A(even_w_out[0])
    wo = odd_w_in[0]
    perm = np.concatenate([np.arange(h * 64, h * 64 + 64).reshape(2, 32)[::-1].reshape(-1) for h in range(8)])
    wq = wo[:, 0:512]
    wk = wo[:, 512:640]
    kperm = np.concatenate([np.arange(h * 64, h * 64 + 64).reshape(2, 32)[::-1].reshape(-1) for h in range(2)])
    wks = wk[:, kperm]
    kdup = np.concatenate([wk[:, 0:64], wk[:, 0:64], wk[:, 64:128], wk[:, 64:128]], axis=1)
    ksdup = np.concatenate([wks[:, 0:64], wks[:, 0:64], wks[:, 64:128], wks[:, 64:128]], axis=1)
    sh["wa1"] = _A(np.concatenate([wq, wq[:, perm], kdup, ksdup, wo[:, 768:1280], wo[:, 1280:1792], wo[:, 1792:2304]], axis=1))
    sh["wv1"] = _B(wo[:, 640:768])
    sh["wo1"] = _A(odd_w_out[0])
    sh["w1"] = np.stack([_A(ffn_w1[l]) for l in range(2)])
    sh["w3"] = np.stack([_A(ffn_w3[l]) for l in range(2)])
    sh["w2"] = np.stack([_A(ffn_w2[l]) for l in range(2)])
    sh["poolw"] = np.ascontiguousarray(pool_w[0].transpose(1, 0, 2)).reshape(128, 512)
    small = np.zeros((128, 32), f32)
    small[:, 0:4] = _vecT(pool_scale[0])
    for j in range(3):
        small[:, 4 + 4 * j:8 + 4 * j] = _vecT(conv_w[0, j])
    small[:, 16:20] = _vecT(conv_b[0])
    small[:, 20:28] = swa_sink[0][None, :]
    sh["small"] = small
    rpb_ext = np.concatenate([na_rpb[0].reshape(8, -1), np.full((8, 1), NEG, f32)], axis=1)
    p = np.arange(128)
    a = (p // 64)[:, None, None]
    kc = (p % 64)[:, None, None]
    i = np.arange(16)[None, :, None]
    qc = np.arange(64)[None, None, :]
    dr = a - i + 8
    cs = np.clip(qc - 8, 0, 48)
    ok = (np.abs(dr) <= 7) & (kc >= cs) & (kc < cs + 16)
    dc = np.clip(kc - qc, -15, 15) + 15
    idx = np.where(ok, (np.clip(dr, -7, 7) + 7) * 31 + dc, 15 * 31)
    sh["ttab"] = np.ascontiguousarray(rpb_ext[:, idx].transpose(1, 0, 2, 3)).reshape(128, 8 * 16 * 64).astype(f32)
    cm = np.zeros((128, 768), f32)
    cm[:, 0:128] = np.eye(128)
    cm[0, 128:192] = 1.0
    cm[1, 192:256] = 1.0
    cm[:, 256:384] = 1.0
    k_ = np.arange(128)[:, None]
    q_ = np.arange(128)[None, :]
    cm[:, 384:512] = np.where(k_ <= q_, 0.0, NEG)
    cm[:, 640:768] = np.where(k_ >= q_, 0.0, NEG)
    sh["cmat"] = cm
    eb = np.zeros((128, 14 * 128), f32)
    for j in range(14):
        for k in range(128):
            eb[2 * j + k // 64, j * 128 + k] = 1.0
    sh["ebig"] = eb
    return {k: np.ascontiguousarray(v, dtype=f32) for k, v in sh.items()}


def prep_tile(xseq, R):
    f32 = np.float32
    L = xseq.shape[0]
    rows = L // GW
    e0 = (R - 6) * GW
    xt = np.zeros((NE, D), f32)
    lo, hi = max(e0, 0), min(e0 + NE, L)
    xt[lo - e0:hi - e0] = xseq[lo:hi]
    xT = np.ascontiguousarray(xt.T.reshape(KC, 128, NE).transpose(1, 0, 2))
    pos_e = e0 + np.arange(NE)
    valid = ((pos_e >= 0) & (pos_e < L)).astype(f32)[None, :]
    pos_m = (R - 2) * GW + np.arange(NM)
    invc = np.ones((4, NM), f32)
    for g, w in enumerate((2, 4, 8, 16)):
        lo_ = np.clip(pos_m - w // 2, 0, L)
        hi_ = np.clip(pos_m - w // 2 + w, 0, L)
        cnt = np.maximum(hi_ - lo_, 1).astype(f32)
        invc[g] = (1.0 / cnt).astype(f32)
    nam = np.zeros((28, 20, 64), f32)
    for r in range(28):
        kr = R - 6 + r
        for qi in range(20):
            qr = R - 2 + qi
            if 0 <= qr < rows:
                w0 = min(max(qr - 4, 0), rows - 8)
            else:
                w0 = qr - 4
            nam[r, qi, :] = 0.0 if (w0 <= kr < w0 + 8) else NEG
    swm = np.where((pos_m >= 0) & (pos_m < L), 0.0, NEG).astype(f32)[None, :]
    inv = (1.0 / (np.float32(10000.0) ** (np.arange(0, 64, 2, dtype=f32) / np.float32(64)))).astype(f32)
    pp = np.arange(128)
    jj = pp % 32
    sgn = np.where((pp % 64) < 32, -1.0, 1.0).astype(f32)

    def tabs(pos, scale):
        ang = (pos.astype(f32)[None, :] * inv[jj][:, None]).astype(f32)
        return (np.cos(ang) * f32(scale)).astype(f32), (np.sin(ang) * sgn[:, None] * f32(scale)).astype(f32)
    ck, sk = tabs(pos_m, 1.0)
    rope = np.concatenate([ck, sk], axis=1)
    return dict(xT=xT, valid=valid, invc=invc.reshape(1, 4 * NM), nam=nam.reshape(28, NM), swm=swm, rope=rope)


def run_tiles(seqs, cs, shared, n_cores, NT, stop_after=None, stages=("mix0", "ffn0", "mix1", "ffn1")):
    tiles = []
    for si, xs in enumerate(seqs):
        for R in range(0, xs.shape[0] // GW, TR):
            tiles.append((si, R))
    assert len(tiles) == n_cores * NT, (len(tiles), n_cores, NT)
    nc, st = build_program(NT, stop_after, stages)
    in_maps = []
    for core in range(n_cores):
        mine = tiles[core * NT:(core + 1) * NT]
        per = [prep_tile(seqs[si], R) for (si, R) in mine]
        m = {k: np.ascontiguousarray(np.stack([p[k] for p in per])) for k in per[0]}
        cTa = np.zeros((128, KC, 8), np.float32)
        for ti_, (si, R) in enumerate(mine):
            cTa[:, :, ti_] = _vecT(cs[si])
        m["cT"] = cTa
        m.update(shared)
        in_maps.append(m)
    with st:
        res = run_bass_kernel_spmd(nc, in_maps, core_ids=list(range(n_cores)))
    outs = [np.zeros_like(xs) for xs in seqs]
    for core in range(n_cores):
        y = res.results[core]["yT"]
        for ti, (si, R) in enumerate(tiles[core * NT:(core + 1) * NT]):
            outs[si][R * GW:(R + TR) * GW] = y[ti].transpose(2, 1, 0).reshape(NC, D)
    return outs


def kernel(x_prompt, x_sample, c_prompt, c_sample, ada_w, ada_b, norm_g, final_g,
           ffn_w1, ffn_w3, ffn_w2, even_w_in, na_rpb, pool_w, pool_scale, even_w_out,
           odd_w_in, swa_sink, conv_w, conv_b, odd_w_out):
    a = lambda v: np.asarray(v, dtype=np.float32)
    shared = prep_shared(a(ada_w), a(ada_b), a(norm_g), a(final_g), a(ffn_w1), a(ffn_w3), a(ffn_w2), a(even_w_in), a(na_rpb),
                         a(pool_w), a(pool_scale), a(even_w_out), a(odd_w_in), a(swa_sink), a(conv_w), a(conv_b), a(odd_w_out))
    xp, xs = a(x_prompt), a(x_sample)
    cp, cs_ = a(c_prompt), a(c_sample)
    seqs = [xs[i] for i in range(xs.shape[0])] + [xp[i] for i in range(xp.shape[0])]
    cs = [cs_[i] for i in range(xs.shape[0])] + [cp[i] for i in range(xp.shape[0])]
    outs = run_tiles(seqs, cs, shared, 8, 6)
    ns = xs.shape[0]
    y_sample = np.stack(outs[:ns]).astype(np.float32)
    y_prompt = np.stack(outs[ns:]).astype(np.float32)
    return (y_prompt, y_sample)
```
